# Optimizing a Trainium2 kernel written in Bass

```python
import jax, jax.numpy as jnp
from jax import lax
import numpy as np

D_MODEL = 1024
BATCH = 2
SEQ = 16384
DEPTH = 1
DEC_BATCH = 1
DEC_SEQ = 16384
PAST_LEN = 128

GRID_W = 64
EPS = 1e-6
GLA_HEADS = 4
GLA_DK = 64
GLA_DV = 128
GLA_RANK = 16
GLA_GATE_NORM = 16.0
GLA_CHUNK = 64
GLA_QK_WIDTH = GLA_HEADS * GLA_DK
GLA_WIDTH = GLA_HEADS * GLA_DV
ATT_HEADS = 8
ATT_KV_HEADS = 2
ATT_GROUP = ATT_HEADS // ATT_KV_HEADS
ATT_HD = 64
ATT_BLOCK = 128
ROPE_THETA = 10000.0
ATT_WIDTH = ATT_HEADS * ATT_HD
ATT_KV_WIDTH = ATT_KV_HEADS * ATT_HD
MIX_WIDTH = GLA_WIDTH + ATT_WIDTH
IN_SIZES = (GLA_QK_WIDTH, GLA_QK_WIDTH, GLA_WIDTH, GLA_WIDTH, GLA_RANK, GLA_RANK,
            ATT_WIDTH, ATT_KV_WIDTH, ATT_KV_WIDTH)
IN_WIDTH = sum(IN_SIZES)
D_FF = 2816
CONV_W = 3

kernel_name = "hymba_gla_axialgqa_convffn_encoder"


def rmsnorm(x, w):
    xf = x.astype(jnp.float32)
    y = xf * lax.rsqrt(jnp.mean(xf * xf, axis=-1, keepdims=True) + EPS)
    return (y * w.astype(jnp.float32)).astype(x.dtype)


def axial_rope_tables(L):
    rows = L // GRID_W
    row = jnp.repeat(jnp.arange(rows), GRID_W).astype(jnp.float32)
    col = jnp.tile(jnp.arange(GRID_W), rows).astype(jnp.float32)
    half = ATT_HD // 2
    inv = 1.0 / (ROPE_THETA ** (jnp.arange(0, half, 2, dtype=jnp.float32) / half))
    ang_r = row[:, None] * inv
    ang_c = col[:, None] * inv
    return (jnp.cos(ang_r), jnp.sin(ang_r), jnp.cos(ang_c), jnp.sin(ang_c))


def _rotate(x, cos, sin):
    h = x.shape[-1] // 2
    x1, x2 = x[..., :h], x[..., h:]
    return jnp.concatenate([x1 * cos - x2 * sin, x1 * sin + x2 * cos], axis=-1)


def apply_axial_rope(x, tabs):
    cr, sr, cc, sc = tabs
    h = ATT_HD // 2
    return jnp.concatenate([_rotate(x[..., :h], cr, sr), _rotate(x[..., h:], cc, sc)], axis=-1)


def gla_chunked(q, k, v, gk):
    B, H, L, dk = q.shape
    dv = v.shape[-1]
    C = GLA_CHUNK
    N = L // C
    q = q.reshape(B, H, N, C, dk)
    k = k.reshape(B, H, N, C, dk)
    v = v.reshape(B, H, N, C, dv)
    b = jnp.cumsum(gk.reshape(B, H, N, C, dk), axis=3)
    b_last = b[..., -1:, :]
    qt = q * jnp.exp(b)
    kt = k * jnp.exp(-b)
    kend = k * jnp.exp(b_last - b)
    mask = jnp.tril(jnp.ones((C, C), dtype=bool))
    A = jnp.where(mask, jnp.einsum('bhncd,bhnsd->bhncs', qt, kt), 0.0)
    o_intra = jnp.einsum('bhncs,bhnse->bhnce', A, v)
    s_chunk = jnp.einsum('bhncd,bhnce->bhnde', kend, v)
    decay = jnp.exp(b_last[..., 0, :])

    def step(S, inp):
        s_c, d_c = inp
        return S * d_c[..., None] + s_c, S

    _, s_prev = lax.scan(step, jnp.zeros((B, H, dk, dv), jnp.float32),
                         (jnp.moveaxis(s_chunk, 2, 0), jnp.moveaxis(decay, 2, 0)))
    s_prev = jnp.moveaxis(s_prev, 0, 2)
    o_inter = jnp.einsum('bhncd,bhnde->bhnce', qt, s_prev)
    return (o_intra + o_inter).reshape(B, H, L, dv)


def gla_group(q, k, v, g, r_f, r_b, w_gk_f, b_gk_f, w_gk_b, b_gk_b, o_norm):
    B, L, _ = q.shape

    def heads(t, d):
        return t.reshape(B, L, GLA_HEADS, d).transpose(0, 2, 1, 3).astype(jnp.float32)

    gk_f = jax.nn.log_sigmoid((r_f @ w_gk_f + b_gk_f).astype(jnp.float32)) / GLA_GATE_NORM
    gk_b = jax.nn.log_sigmoid((r_b @ w_gk_b + b_gk_b).astype(jnp.float32)) / GLA_GATE_NORM
    qh = heads(q, GLA_DK) * (GLA_DK ** -0.5)
    kh = heads(k, GLA_DK)
    vh = heads(v, GLA_DV)
    gf = heads(gk_f, GLA_DK)
    gb = heads(gk_b, GLA_DK)
    o_fwd = gla_chunked(qh, kh, vh, gf)
    flip = lambda t: jnp.flip(t, axis=2)
    o_bwd = flip(gla_chunked(flip(qh), flip(kh), flip(vh), flip(gb)))
    o = (o_fwd + o_bwd).transpose(0, 2, 1, 3)
    o = rmsnorm(o, o_norm).reshape(B, L, GLA_WIDTH)
    return (o * jax.nn.silu(g.astype(jnp.float32))).astype(q.dtype)


def gqa_group(q, k, v, q_norm, k_norm, rope):
    B, L, _ = q.shape
    q = rmsnorm(q.reshape(B, L, ATT_KV_HEADS, ATT_GROUP, ATT_HD), q_norm).astype(jnp.float32)
    k = rmsnorm(k.reshape(B, L, ATT_KV_HEADS, ATT_HD), k_norm).astype(jnp.float32)
    v = v.reshape(B, L, ATT_KV_HEADS, ATT_HD).transpose(0, 2, 1, 3)
    q = apply_axial_rope(q.transpose(0, 2, 3, 1, 4), rope).astype(v.dtype)
    k = apply_axial_rope(k.transpose(0, 2, 1, 3), rope).astype(v.dtype)
    nb = L // ATT_BLOCK
    qb = q.reshape(B, ATT_KV_HEADS, ATT_GROUP, nb, ATT_BLOCK, ATT_HD).transpose(3, 0, 1, 2, 4, 5)
    scale = ATT_HD ** -0.5

    def one_block(qblk):
        s = jnp.einsum('bhgqd,bhkd->bhgqk', qblk, k).astype(jnp.float32) * scale
        p = jax.nn.softmax(s, axis=-1)
        return jnp.einsum('bhgqk,bhkd->bhgqd', p.astype(v.dtype), v)

    o = lax.map(one_block, qb)
    return o.transpose(1, 0, 4, 2, 3, 5).reshape(B, L, ATT_WIDTH)


def mixer_block(h, w_in, w_gk_f, b_gk_f, w_gk_b, b_gk_b, o_norm, q_norm, k_norm, w_out, rope):
    proj = h @ w_in
    points = [int(p) for p in np.cumsum(IN_SIZES)[:-1]]
    gq, gk, gv, gg, rf, rb, aq, ak, av = jnp.split(proj, points, axis=-1)
    o_gla = gla_group(gq, gk, gv, gg, rf, rb, w_gk_f, b_gk_f, w_gk_b, b_gk_b, o_norm)
    o_att = gqa_group(aq, ak, av, q_norm, k_norm, rope)
    return jnp.concatenate([o_gla, o_att.astype(o_gla.dtype)], axis=-1) @ w_out


def conv_ffn(h, w_up, conv_w, conv_b, w_down):
    u = h @ w_up
    up = jnp.pad(u, ((0, 0), (1, 1), (0, 0)))
    u = up[:, :-2] * conv_w[0] + up[:, 1:-1] * conv_w[1] + up[:, 2:] * conv_w[2] + conv_b
    a, gate = jnp.split(u, 2, axis=-1)
    return (jax.nn.silu(gate) * a) @ w_down


def encoder_trunk(x, ln_mix, w_in, w_gk_fwd, b_gk_fwd, w_gk_bwd, b_gk_bwd, gla_out_norm,
                  q_norm, k_norm, w_out, ln_ffn, w_up, conv_w, conv_b, w_down, ln_final):
    rope = axial_rope_tables(x.shape[1])
    for l in range(DEPTH):
        x = x + mixer_block(rmsnorm(x, ln_mix[l]), w_in[l], w_gk_fwd[l], b_gk_fwd[l],
                            w_gk_bwd[l], b_gk_bwd[l], gla_out_norm[l], q_norm[l], k_norm[l],
                            w_out[l], rope)
        x = x + conv_ffn(rmsnorm(x, ln_ffn[l]), w_up[l], conv_w[l], conv_b[l], w_down[l])
    return rmsnorm(x, ln_final)


def setup_inputs(seed: int = 0) -> dict:
    key = jax.random.key(seed)
    ks = jax.random.split(key, 20)
    f32 = jnp.float32
    nrm = lambda k, shape, s: jax.random.normal(k, shape, f32) * s
    gain = lambda k, shape: 1.0 + 0.02 * jax.random.normal(k, shape, f32)
    return {
        "x_prompt": jax.random.normal(ks[0], (BATCH, SEQ, D_MODEL), f32),
        "x_sample": jax.random.normal(ks[1], (DEC_BATCH, DEC_SEQ, D_MODEL), f32),
        "ln_mix": gain(ks[2], (DEPTH, D_MODEL)),
        "w_in": nrm(ks[3], (DEPTH, D_MODEL, IN_WIDTH), D_MODEL ** -0.5),
        "w_gk_fwd": nrm(ks[4], (DEPTH, GLA_RANK, GLA_QK_WIDTH), GLA_RANK ** -0.5),
        "b_gk_fwd": nrm(ks[5], (DEPTH, GLA_QK_WIDTH), 0.1),
        "w_gk_bwd": nrm(ks[6], (DEPTH, GLA_RANK, GLA_QK_WIDTH), GLA_RANK ** -0.5),
        "b_gk_bwd": nrm(ks[7], (DEPTH, GLA_QK_WIDTH), 0.1),
        "gla_out_norm": gain(ks[8], (DEPTH, GLA_DV)),
        "q_norm": gain(ks[9], (DEPTH, ATT_HD)),
        "k_norm": gain(ks[10], (DEPTH, ATT_HD)),
        "w_out": nrm(ks[11], (DEPTH, MIX_WIDTH, D_MODEL), MIX_WIDTH ** -0.5),
        "ln_ffn": gain(ks[12], (DEPTH, D_MODEL)),
        "w_up": nrm(ks[13], (DEPTH, D_MODEL, 2 * D_FF), D_MODEL ** -0.5),
        "conv_w": nrm(ks[14], (DEPTH, CONV_W, 2 * D_FF), CONV_W ** -0.5),
        "conv_b": nrm(ks[15], (DEPTH, 2 * D_FF), 0.02),
        "w_down": nrm(ks[16], (DEPTH, D_FF, D_MODEL), D_FF ** -0.5),
        "ln_final": gain(ks[17], (D_MODEL,)),
    }


def reference(x_prompt, x_sample, ln_mix, w_in, w_gk_fwd, b_gk_fwd, w_gk_bwd, b_gk_bwd,
              gla_out_norm, q_norm, k_norm, w_out, ln_ffn, w_up, conv_w, conv_b, w_down,
              ln_final):
    y_prompt = encoder_trunk(x_prompt, ln_mix, w_in, w_gk_fwd, b_gk_fwd, w_gk_bwd, b_gk_bwd,
                             gla_out_norm, q_norm, k_norm, w_out, ln_ffn, w_up, conv_w,
                             conv_b, w_down, ln_final)
    y_sample = encoder_trunk(x_sample, ln_mix, w_in, w_gk_fwd, b_gk_fwd, w_gk_bwd, b_gk_bwd,
                             gla_out_norm, q_norm, k_norm, w_out, ln_ffn, w_up, conv_w,
                             conv_b, w_down, ln_final)
    return (y_prompt, y_sample)
```

```python
import numpy as np
from contextlib import ExitStack
import concourse.bass as bass
import concourse.mybir as mybir
from concourse.bass_utils import run_bass_kernel_spmd

F32 = mybir.dt.float32
BF16 = mybir.dt.bfloat16
AF = mybir.ActivationFunctionType
ALU = mybir.AluOpType
AX = mybir.AxisListType

ENGS = ("pe", "act", "dve", "pool", "sp")
D = 1024
INW = 2336
DFF = 2816
EPS = 1e-6


class Buf:
    def __init__(self, name):
        self.name = name
        self.w = {}
        self.r = {}
        self.dsem = None
        self.dcnt = 0


class Prog:
    def __init__(self, nc):
        self.nc = nc
        self.es = ExitStack()
        self.sems = {}
        self.cnt = {e: 0 for e in ENGS}
        self.waited = {e: {} for e in ENGS}
        self.E = {"pe": nc.tensor, "act": nc.scalar, "dve": nc.vector, "pool": nc.gpsimd, "sp": nc.sync}
        for e in ENGS:
            self.sems[e] = self.es.enter_context(nc.semaphore("s_" + e))
        self.nins = 0
        self.allbufs = []
        self.defer = False
        self.pending = []
        self.EST = {"pe": 1.0, "act": 0.7, "dve": 0.5, "pool": 1.0, "sp": 0.1}

    def dram(self, name, shape, dt, kind="Internal"):
        return self.nc.dram_tensor(name, list(shape), dt, kind=kind).ap()

    def buf_sem(self, b, eng):
        if b.dsem is None:
            b.dsem = {}
            b.dcnt = {}
            self.allbufs.append(b)
        if eng not in b.dsem:
            sm = self.es.enter_context(self.nc.semaphore("d_%s_%s" % (b.name, eng)))
            b.dsem[eng] = sm
            b.dcnt[eng] = 0
            self.sems[sm] = sm
        return b.dsem[eng]

    def _collect(self, eng, reads, writes):
        need = {}
        for b in reads:
            for k, v in b.w.items():
                if need.get(k, 0) < v:
                    need[k] = v
        for b in writes:
            for k, v in b.w.items():
                if need.get(k, 0) < v:
                    need[k] = v
            for k, v in b.r.items():
                if need.get(k, 0) < v:
                    need[k] = v
        out = []
        wd = self.waited[eng]
        for k, v in need.items():
            if k == eng and eng == "pe":
                continue
            if wd.get(k, 0) >= v:
                continue
            wd[k] = v
            out.append((k, v))
        return out

    def _emit(self, eng, waits, fn, inc):
        e = self.E[eng]
        for k, v in waits:
            e.wait_ge(self.sems[k], v)
        if fn is None:
            return
        ins = fn(e)
        self.nins += 1
        if inc is not None:
            ins.then_inc(self.sems[inc[0]], inc[1])

    def op(self, eng, fn, reads=(), writes=(), est=None):
        if self.defer:
            self.pending.append(("op", eng, fn, tuple(reads), tuple(writes), est if est is not None else self.EST[eng], None, None))
            return
        waits = self._collect(eng, reads, writes)
        self.cnt[eng] += 1
        tok = (eng, self.cnt[eng])
        self._emit(eng, waits, fn, (eng, 1))
        for b in reads:
            b.r[tok[0]] = tok[1]
        for b in writes:
            b.w[tok[0]] = tok[1]

    def dma(self, eng, out, in_, own, reads=(), writes=(), **kw):
        if self.defer:
            self.pending.append(("dma", eng, (out, in_, kw), tuple(reads), tuple(writes), 3.0, own, None))
            return
        waits = self._collect(eng, reads, writes)
        sem = self.buf_sem(own, eng)
        own.dcnt[eng] += 16
        tok = (sem, own.dcnt[eng])
        self._emit(eng, waits, lambda e: e.dma_start(out=out, in_=in_, **kw), (sem, 16))
        for b in reads:
            b.r[tok[0]] = tok[1]
        for b in writes:
            b.w[tok[0]] = tok[1]

    def flush(self):
        ops = self.pending
        self.pending = []
        if not ops:
            return
        import heapq
        n = len(ops)
        preds = [set() for _ in range(n)]
        lastw = {}
        readers = {}
        for i, o in enumerate(ops):
            for b in o[3]:
                if b in lastw:
                    preds[i].add(lastw[b])
            for b in o[4]:
                if b in lastw:
                    preds[i].add(lastw[b])
                for r in readers.get(b, ()):
                    preds[i].add(r)
            for b in o[3]:
                readers.setdefault(b, []).append(i)
            for b in o[4]:
                lastw[b] = i
                readers[b] = []
            preds[i].discard(i)
        succs = [[] for _ in range(n)]
        npred = [0] * n
        for i in range(n):
            npred[i] = len(preds[i])
            for p in preds[i]:
                succs[p].append(i)
        ready_t = [0.0] * n
        finish = [0.0] * n
        heaps = {e: [] for e in ENGS}
        for i in range(n):
            if npred[i] == 0:
                heapq.heappush(heaps[ops[i][1]], (0.0, i))
        efree = {e: 0.0 for e in ENGS}
        order = []
        done = 0
        while done < n:
            best = None
            for e in ENGS:
                h = heaps[e]
                if not h:
                    continue
                rt, i = h[0]
                st = max(rt, efree[e])
                if best is None or (st, i) < (best[0], best[2]):
                    best = (st, e, i)
            st, e, i = best
            heapq.heappop(heaps[e])
            o = ops[i]
            if o[0] == "dma":
                efree[e] = st + 0.1
                finish[i] = st + o[5]
            else:
                efree[e] = st + o[5]
                finish[i] = st + o[5] + 0.15
            order.append(i)
            done += 1
            for sidx in succs[i]:
                npred[sidx] -= 1
                if finish[i] > ready_t[sidx]:
                    ready_t[sidx] = finish[i]
                if npred[sidx] == 0:
                    heapq.heappush(heaps[ops[sidx][1]], (ready_t[sidx], sidx))
        prev = self.defer
        self.defer = False
        for i in order:
            o = ops[i]
            if o[0] == "dma":
                out, in_, kw = o[2]
                self.dma(o[1], out, in_, o[6], reads=o[3], writes=o[4], **kw)
            else:
                self.op(o[1], o[2], reads=o[3], writes=o[4])
        self.defer = prev

    def barrier(self):
        self.flush()
        b = Buf("barrier")
        for e in ENGS:
            if self.cnt[e] > 0:
                b.w[e] = self.cnt[e]
        for k, v in self.dsem_counts().items():
            b.w[k] = v
        for e in ENGS:
            self.wait_all(e, [b])

    def dsem_counts(self):
        out = {}
        for bb in self.allbufs:
            for eng, sm in bb.dsem.items():
                out[sm] = bb.dcnt[eng]
        return out

    def wait_all(self, eng, bufs):
        self.flush()
        waits = self._collect(eng, bufs, ())
        self._emit(eng, waits, None, None)


class Phase:
    def __init__(self, P):
        self.P = P
        self.es = ExitStack()

    def sb(self, name, shape, dt):
        t = self.es.enter_context(self.P.nc.sbuf_tensor(name, list(shape), dt))
        return t, Buf(name)

    def ps(self, name, shape, dt=F32):
        t = self.es.enter_context(self.P.nc.psum_tensor(name, list(shape), dt))
        return t, Buf(name)

    def close(self):
        self.P.barrier()
        self.es.close()


def rr(lst, i):
    return lst[i % len(lst)]


def build_program(L, dbg=False, stop=None, H=None):
    assert L % 512 == 0
    NT = L // 128
    NB = L // 512
    if H is None:
        HO = L
        QB = NB
        qblocks = [(qb * 512, 512) for qb in range(NB)]
        C1T = NT
    else:
        assert H % 512 == 0 and H < L
        HO = H
        QB = H // 512 + 1
        qblocks = [(qb * 512, 512) for qb in range(H // 512)] + [(H, 128)]
        C1T = H // 128 + 1
    nc = bass.Bass("TRN2", target_bir_lowering=False)
    P = Prog(nc)
    op, dma = P.op, P.dma

    def ext(name, shape, dt=F32):
        return P.dram(name, shape, dt, kind="ExternalInput")

    x = ext("x", [L, D])
    w_in = ext("w_in", [D, INW])
    w_rot = ext("w_rot", [D, 640])
    ln_mix = ext("ln_mix", [128, 8])
    ln_ffn = ext("ln_ffn", [128, 8])
    lnf_bc = ext("lnf_bc", [128, D])
    wg_aug = ext("wg_aug", [33, 512])
    nrm = ext("nrm", [128, 8])
    cosT = ext("cosT", [128, L])
    sinT = ext("sinT", [128, L])
    consts = ext("consts", [128, 6 * 128])
    sel65 = ext("sel65", [65, 64])
    w_out = ext("w_out", [D, D])
    w_up = ext("w_up", [D, 2 * DFF])
    w_down = ext("w_down", [DFF, D])
    convp = ext("convp", [128, 44, 4])
    y = P.dram("y", [HO, D], F32, kind="ExternalOutput")
    by = Buf("y")

    kind = "ExternalOutput" if dbg else "Internal"
    GQT = P.dram("GQT", [256, L], F32, kind=kind); bGQT = Buf("GQT")
    GKT = P.dram("GKT", [256, L], F32, kind=kind); bGKT = Buf("GKT")
    GGT = P.dram("GGT", [512, L], F32, kind=kind); bGGT = Buf("GGT")
    GK = P.dram("GK", [L, 256], F32, kind=kind); bGK = Buf("GK")
    GV = P.dram("GV", [L, 512], BF16, kind=kind); bGV = Buf("GV")
    G = P.dram("G", [L, 512], F32, kind=kind); bG = Buf("G")
    AQT = P.dram("AQT", [512, L], BF16, kind=kind); bAQT = Buf("AQT")
    AKT = P.dram("AKT", [128, L], BF16, kind=kind); bAKT = Buf("AKT")
    AV = P.dram("AV", [L, 130], BF16, kind=kind); bAV = Buf("AV")
    OF = P.dram("OF", [512, L], F32, kind=kind); bOF = Buf("OF")
    MIXT = P.dram("MIXT", [1024, L], BF16, kind=kind); bMIXT = Buf("MIXT")
    X1 = P.dram("X1", [L, D], F32, kind=kind); bX1 = Buf("X1")
    H2T = P.dram("H2T", [1024, L + 2], BF16, kind=kind); bH2T = Buf("H2T")

    G0 = Phase(P)
    cst, bcst = G0.sb("cst", [128, 6 * 128], F32)
    cstb, bcstb = G0.sb("cstb", [128, 6 * 128], BF16)
    nrm_sb, bnrm = G0.sb("nrm_sb", [128, 8], F32)
    dma("sp", cst[:], consts[:, :], bcst, writes=[bcst])
    dma("sp", nrm_sb[:], nrm[:, :], bnrm, writes=[bnrm])
    op("dve", lambda e: e.tensor_copy(out=cstb[:], in_=cst[:]), reads=[bcst], writes=[bcstb])
    ident_b = cstb[:, 0:128]
    UIf = [cst[:, 128:256], cst[:, 256:384]]
    LSf = [cst[:, 384:512], cst[:, 512:640]]
    UIb = [cstb[:, 128:256], cstb[:, 256:384]]
    BDb = cstb[:, 640:768]

    def fin():
        P.wait_all("pool", [by, bGQT, bGKT, bGGT, bGK, bGV, bG, bAQT, bAKT, bAV, bOF, bMIXT, bX1, bH2T])
        G0.close()
        P.es.close()
        return nc, P

    def rstd_from_ssq(ph_bufs, ssq, bssq, n, shape_ap=None):
        op("act", lambda e: e.activation(out=ssq, in_=ssq, func=AF.Ln, scale=1.0 / n, bias=EPS), reads=[bssq], writes=[bssq])
        op("act", lambda e: e.activation(out=ssq, in_=ssq, func=AF.Exp, scale=-0.5), reads=[bssq], writes=[bssq])

    import os
    SCHED = os.environ.get("SCHED", "A,G,C1,C2").split(",")
    A = Phase(P)
    P.defer = "A" in SCHED
    WC = INW + 640
    wb, bwb = A.sb("wb", [128, 8, WC], BF16)
    lnm, blnm = A.sb("lnm", [128, 8], F32)
    dma("sp", lnm[:], ln_mix[:, :], blnm, writes=[blnm])
    stg = [A.sb("stg%d" % i, [128, WC], F32) for i in range(2)]
    for c in range(8):
        st, bst = rr(stg, c)
        dma("sp", st[:, 0:INW], w_in[c * 128:(c + 1) * 128, :], bst, writes=[bst])
        dma("sp", st[:, INW:WC], w_rot[c * 128:(c + 1) * 128, :], bst, writes=[bst])
        op("dve", lambda e, st=st, c=c: e.tensor_scalar(out=wb[:, c, :], in0=st[:], scalar1=lnm[:, c:c + 1], scalar2=None, op0=ALU.mult),
           reads=[bst, blnm], writes=[bwb])
    wgf, bwgf = A.sb("wgf", [33, 512], F32)
    wgb, bwgb = A.sb("wgb", [33, 512], BF16)
    dma("sp", wgf[:], wg_aug[:, :], bwgf, writes=[bwgf])
    op("dve", lambda e: e.tensor_copy(out=wgb[:], in_=wgf[:]), reads=[bwgf], writes=[bwgb])

    xts = [A.sb("xt%d" % i, [128, D], F32) for i in range(3)]
    junk, bjunk = A.sb("junk", [128, D], BF16)
    ssqs = [A.sb("ssq%d" % i, [128, 1], F32) for i in range(3)]
    xbs = [A.sb("xb%d" % i, [128, D], BF16) for i in range(2)]
    xnTs = [A.sb("xnT%d" % i, [128, 8, 512], BF16) for i in range(2)]
    raug, braug = A.sb("raug", [33, 512], BF16)
    op("dve", lambda e: e.memset(raug[:], 1.0), writes=[braug])
    ptr = [A.ps("ptr%d" % i, [128, 8, 128], BF16) for i in range(1)]
    pfm = [A.ps("pfm%d" % i, [128, 512], F32) for i in range(3)]
    pss, bpss = A.ps("pss", [128, 512], F32)
    ptm1, bptm1 = A.ps("ptm1", [128, 512], F32)
    ptm2, bptm2 = A.ps("ptm2", [128, 512], F32)
    pz, bpz = A.ps("pz", [128, 512], F32)
    fmo = [A.sb("fmo%d" % i, [128, 512], F32) for i in range(3)]
    cos_sb = [A.sb("cos%d" % i, [128, 512], F32) for i in range(2)]
    sin_sb = [A.sb("sin%d" % i, [128, 512], F32) for i in range(2)]
    sqb, bsqb = A.sb("sqb", [128, 512], BF16)
    rsd, brsd = A.sb("rsd", [128, 512], F32)
    t1s, bt1 = A.sb("t1s", [128, 512], F32)
    t2s, bt2 = A.sb("t2s", [128, 512], F32)
    qfo = [A.sb("qfo%d" % i, [128, 512], BF16) for i in range(2)]
    gko = [A.sb("gko%d" % i, [128, 256], F32) for i in range(2)]
    gvo = [A.sb("gvo%d" % i, [128, 512], BF16) for i in range(2)]
    avo = [A.sb("avo%d" % i, [128, 2, 65], BF16) for i in range(2)]
    for t, b in avo:
        op("dve", lambda e, t=t: e.memset(t[:], 1.0), writes=[b])
    ge, bge = A.sb("ge", [128, 512], F32)
    go = [A.sb("go%d" % i, [128, 512], F32) for i in range(2)]

    fmcount = [0]

    def fm_proj(col0, ncols, xnT, bxnT):
        pt, bpt = rr(pfm, fmcount[0])
        fmcount[0] += 1

        def f(e):
            ins = None
            for c in range(8):
                ins = e.matmul(pt[0:ncols, :], lhsT=wb[:, c, col0:col0 + ncols], rhs=xnT[:, c, :], start=(c == 0), stop=(c == 7))
            return ins
        op("pe", f, reads=[bwb, bxnT], writes=[bpt])
        return pt, bpt

    stc = [0]
    import os
    LVL = int(os.environ.get("PALVL", "9"))
    def stage_a1(blk):
        xnT, bxnT = rr(xnTs, blk)
        for ti in range(4):
            tile = blk * 4 + ti
            xt, bxt = rr(xts, tile)
            ssq, bssq = rr(ssqs, tile)
            xb, bxb = rr(xbs, tile)
            pt, bpt = rr(ptr, tile)
            dma("sp", xt[:], x[tile * 128:(tile + 1) * 128, :], bxt, writes=[bxt])
            op("act", lambda e, xt=xt, ssq=ssq: e.activation(out=junk[:], in_=xt[:], func=AF.Square, accum_out=ssq[:]),
               reads=[bxt], writes=[bjunk, bssq])
            rstd_from_ssq(None, ssq[:], bssq, D)
            op("dve", lambda e, xb=xb, xt=xt, ssq=ssq: e.tensor_scalar(out=xb[:], in0=xt[:], scalar1=ssq[:, 0:1], scalar2=None, op0=ALU.mult),
               reads=[bxt, bssq], writes=[bxb])

            def ftr(e, xb=xb, pt=pt):
                ins = None
                for c in range(8):
                    ins = e.transpose(out=pt[:, c, :], in_=xb[:, c * 128:(c + 1) * 128], identity=ident_b)
                return ins
            op("pe", ftr, reads=[bxb, bcstb], writes=[bpt])
            op("act", lambda e, xnT=xnT, pt=pt, ti=ti: e.copy(out=xnT[:, :, ti * 128:(ti + 1) * 128], in_=pt[:]),
               reads=[bpt], writes=[bxnT])
    def stage_proj(blk):
        xnT, bxnT = rr(xnTs, blk)
        t0 = blk * 512
        qside = blk < QB
        for (col0, dst, bdst, row0, scale) in [] if not qside else (
            [(0 + 128 * j, GQT, bGQT, 128 * j, 0.125) for j in range(2)]
            + [(256 + 128 * j, GKT, bGKT, 128 * j, 1.0) for j in range(2)]
            + [(1024 + 128 * j, GGT, bGGT, 128 * j, 1.0) for j in range(4)]
        ):
            pt, bpt = fm_proj(col0, 128, xnT, bxnT)
            so, bso = rr(fmo, stc[0]); stc[0] += 1
            op("act", lambda e, so=so, pt=pt, scale=scale: e.activation(out=so[:], in_=pt[:], func=AF.Copy, scale=scale),
               reads=[bpt], writes=[bso])
            dma("pool", dst[row0:row0 + 128, t0:t0 + 512], so[:], bso, reads=[bso], writes=[bdst])
        pt, bpt = fm_proj(1536, 32, xnT, bxnT)
        op("act", lambda e, pt=pt: e.copy(out=raug[0:32, :], in_=pt[0:32, :]), reads=[bpt], writes=[braug])
        cs, bcs = rr(cos_sb, blk)
        sn, bsn = rr(sin_sb, blk)
        dma("sp", cs[:], cosT[:, t0:t0 + 512], bcs, writes=[bcs])
        dma("sp", sn[:], sinT[:, t0:t0 + 512], bsn, writes=[bsn])
        for j in (range(5) if qside else [4]):
            col0 = 1568 + 128 * j
            colr = INW + 128 * j
            wcol = 0 if j < 4 else 2
            pa, bpa = fm_proj(col0, 128, xnT, bxnT)
            pr, bpr = fm_proj(colr, 128, xnT, bxnT)
            op("act", lambda e, pa=pa: e.activation(out=sqb[:], in_=pa[:], func=AF.Square), reads=[bpa], writes=[bsqb])
            op("pe", lambda e: e.matmul(pss[:], lhsT=BDb, rhs=sqb[:], start=True, stop=True), reads=[bsqb, bcstb], writes=[bpss])
            op("act", lambda e: e.activation(out=rsd[:], in_=pss[:], func=AF.Ln, scale=1.0 / 64, bias=EPS), reads=[bpss], writes=[brsd])
            op("act", lambda e: e.activation(out=rsd[:], in_=rsd[:], func=AF.Exp, scale=-0.5), reads=[brsd], writes=[brsd])
            op("act", lambda e, pa=pa, wcol=wcol: e.activation(out=t1s[:], in_=pa[:], func=AF.Copy, scale=nrm_sb[:, wcol:wcol + 1]), reads=[bpa, bnrm], writes=[bt1])
            op("act", lambda e, pr=pr, wcol=wcol: e.activation(out=t2s[:], in_=pr[:], func=AF.Copy, scale=nrm_sb[:, wcol + 1:wcol + 2]), reads=[bpr, bnrm], writes=[bt2])
            op("dve", lambda e, cs=cs: e.tensor_tensor(out=t1s[:], in0=t1s[:], in1=cs[:], op=ALU.mult), reads=[bt1, bcs], writes=[bt1])
            op("dve", lambda e, sn=sn: e.tensor_tensor(out=t2s[:], in0=t2s[:], in1=sn[:], op=ALU.mult), reads=[bt2, bsn], writes=[bt2])
            op("dve", lambda e: e.tensor_tensor(out=t1s[:], in0=t1s[:], in1=t2s[:], op=ALU.add), reads=[bt1, bt2], writes=[bt1])
            qo, bqo = rr(qfo, j)
            op("dve", lambda e, qo=qo: e.tensor_tensor(out=qo[:], in0=t1s[:], in1=rsd[:], op=ALU.mult), reads=[bt1, brsd], writes=[bqo])
            if j < 4:
                dma("pool", AQT[128 * j:128 * j + 128, t0:t0 + 512], qo[:], bqo, reads=[bqo], writes=[bAQT])
            else:
                dma("pool", AKT[:, t0:t0 + 512], qo[:], bqo, reads=[bqo], writes=[bAKT])
        for ti in range(4):
            tile = blk * 4 + ti
            r0 = tile * 128
            lhs = lambda c, ti=ti: xnT[:, c, ti * 128:(ti + 1) * 128]

            def ftm(e, lhs=lhs):
                ins = None
                for c in range(8):
                    ins = e.matmul(ptm1[:, 0:256], lhsT=lhs(c), rhs=wb[:, c, 256:512], start=(c == 0), stop=(c == 7))
                for c in range(8):
                    ins = e.matmul(ptm1[:, 256:384], lhsT=lhs(c), rhs=wb[:, c, 2208:2336], start=(c == 0), stop=(c == 7))
                return ins
            op("pe", ftm, reads=[bwb, bxnT], writes=[bptm1])

            def ftm2(e, lhs=lhs):
                ins = None
                for c in range(8):
                    ins = e.matmul(ptm2[:], lhsT=lhs(c), rhs=wb[:, c, 512:1024], start=(c == 0), stop=(c == 7))
                return ins
            op("pe", ftm2, reads=[bwb, bxnT], writes=[bptm2])
            op("pe", lambda e, ti=ti: e.matmul(pz[:], lhsT=raug[:, ti * 128:(ti + 1) * 128], rhs=wgb[:], start=True, stop=True),
               reads=[braug, bwgb], writes=[bpz])
            gk_t, bgk_t = rr(gko, tile)
            gv_t, bgv_t = rr(gvo, tile)
            av_t, bav_t = rr(avo, tile)
            g_t, bg_t = rr(go, tile)
            op("dve", lambda e, gk_t=gk_t: e.tensor_copy(out=gk_t[:], in_=ptm1[:, 0:256]), reads=[bptm1], writes=[bgk_t])
            op("dve", lambda e, av_t=av_t: e.tensor_copy(out=av_t[:, :, 0:64], in_=ptm1[:, 256:384].rearrange("p (g d) -> p g d", g=2)),
               reads=[bptm1], writes=[bav_t])
            op("act", lambda e, gv_t=gv_t: e.copy(out=gv_t[:], in_=ptm2[:]), reads=[bptm2], writes=[bgv_t])
            op("act", lambda e: e.activation(out=ge[:], in_=pz[:], func=AF.Exp, scale=-1.0), reads=[bpz], writes=[bge])
            op("act", lambda e: e.activation(out=ge[:], in_=ge[:], func=AF.Ln, bias=1.0), reads=[bge], writes=[bge])
            op("dve", lambda e, g_t=g_t: e.tensor_scalar(out=g_t[:], in0=ge[:], scalar1=-1.0 / 16.0, scalar2=None, op0=ALU.mult),
               reads=[bge], writes=[bg_t])
            dma("pool", GK[r0:r0 + 128, :], gk_t[:], bgk_t, reads=[bgk_t], writes=[bGK])
            dma("pool", GV[r0:r0 + 128, :], gv_t[:], bgv_t, reads=[bgv_t], writes=[bGV])
            dma("pool", AV[r0:r0 + 128, :], av_t[:].rearrange("p g d -> p (g d)"), bav_t, reads=[bav_t], writes=[bAV])
            dma("pool", G[r0:r0 + 128, :], g_t[:], bg_t, reads=[bg_t], writes=[bG])
    for blk in range(NB + 1):
        if blk < NB:
            stage_a1(blk)
        if blk >= 1:
            stage_proj(blk - 1)
    A.close()
    if stop == "A":
        return fin()

    Gp = Phase(P)
    P.defer = "G" in SCHED
    S32 = [Gp.sb("S32_%d" % i, [128, 128], F32) for i in range(2)]
    Sbf = [Gp.sb("Sbf_%d" % i, [128, 128], BF16) for i in range(2)]
    onesb, bonesb = Gp.sb("onesb", [128, 128], BF16)
    op("dve", lambda e: e.memset(onesb[:], 1.0), writes=[bonesb])
    NGB = 2
    g_g = [Gp.sb("g_g%d" % i, [128, 4, 256], F32) for i in range(NGB)]
    k_g = [Gp.sb("k_g%d" % i, [128, 4, 256], F32) for i in range(NGB)]
    v_g = [Gp.sb("v_g%d" % i, [128, 4, 512], BF16) for i in range(NGB)]
    qT_g = [Gp.sb("qT_g%d" % i, [128, 2, 512], F32) for i in range(NGB)]
    kT_g = [Gp.sb("kT_g%d" % i, [128, 2, 512], F32) for i in range(NGB)]
    of_g = [Gp.sb("of_g%d" % i, [128, 4, 512], F32) for i in range(NGB)]
    gg_g = [Gp.sb("gg_g%d" % i, [128, 4, 512], F32) for i in range(NGB)]
    mx_g = [Gp.sb("mx_g%d" % i, [128, 4, 512], BF16) for i in range(NGB)]
    E1, bE1 = Gp.sb("E1", [128, 256], F32)
    kend, bkend = Gp.sb("kend", [128, 256], BF16)
    Eb, bEb = Gp.sb("Eb", [128, 2, 128], F32)
    Enb, bEnb = Gp.sb("Enb", [128, 2, 128], F32)
    qtT = [Gp.sb("qtT%d" % i, [128, 2, 2, 128], BF16) for i in range(2)]
    for t_, b_ in qtT:
        op("dve", lambda e, t_=t_: e.memset(t_[:], 0.0), writes=[b_])
    ktT, bktT = Gp.sb("ktT", [128, 2, 128], BF16)
    Am = [Gp.sb("Am%d" % i, [128, 4, 128], BF16) for i in range(2)]
    osum, bosum = Gp.sb("osum", [128, 4, 128], F32)
    atf, batf = Gp.sb("atf", [128, 4, 128], F32)
    scs, bscs = Gp.sb("scs", [128, 4, 128], F32)
    osq, bosq = Gp.sb("osq", [128, 4, 128], BF16)
    orst, borst = Gp.sb("orst", [128, 4, 128], F32)
    gsig, bgsig = Gp.sb("gsig", [128, 4, 128], F32)
    p_rb, bp_rb = Gp.ps("p_rb", [128, 512], F32)
    p_at = [Gp.ps("p_at%d" % i, [128, 4, 128], F32) for i in range(2)]
    p_o = [Gp.ps("p_o%d" % i, [128, 4, 128], F32) for i in range(2)]
    p_sc, bp_sc = Gp.ps("p_sc", [128, 4, 128], F32)
    p_ss, bp_ss = Gp.ps("p_ss", [128, 4, 128], F32)

    NG = L // 512
    kends = [Gp.sb("kend_%d" % i, [128, 256], BF16) for i in range(2)]
    Ebs = [Gp.sb("Eb_%d" % i, [128, 2, 128], F32) for i in range(2)]
    for dr in range(2):
        for i in range(2):
            op("dve", lambda e, i=i: e.memset(S32[i][0][:], 0.0), writes=[S32[i][1]])
            op("dve", lambda e, i=i: e.memset(Sbf[i][0][:], 0.0), writes=[Sbf[i][1]])
        chunks = []
        for gi in range(QB if dr == 0 else NG):
            grp = gi if dr == 0 else NG - 1 - gi
            for cj in range(4):
                chunks.append(dict(gi=gi, grp=grp, full=(grp < QB), cj=cj, ci=(cj if dr == 0 else 3 - cj), idx=len(chunks)))
        gctx = {}
        dcol = 127 if dr == 0 else 0

        def load_group(ch):
            gi, grp, full = ch["gi"], ch["grp"], ch["full"]
            t0 = grp * 512
            gsel = dr * NG + gi
            c = dict(t0=t0)
            c["gt"], c["bgt"] = rr(g_g, gsel)
            c["kt"], c["bkt"] = rr(k_g, gsel)
            c["vt"], c["bvt"] = rr(v_g, gsel)
            dma("sp", c["gt"][:], G[t0:t0 + 512, dr * 256:(dr + 1) * 256].rearrange("(n p) c -> p n c", p=128), c["bgt"], reads=[bG], writes=[c["bgt"]])
            dma("sp", c["kt"][:], GK[t0:t0 + 512, :].rearrange("(n p) c -> p n c", p=128), c["bkt"], reads=[bGK], writes=[c["bkt"]])
            dma("sp", c["vt"][:], GV[t0:t0 + 512, :].rearrange("(n p) c -> p n c", p=128), c["bvt"], reads=[bGV], writes=[c["bvt"]])
            if full:
                c["qTt"], c["bqTt"] = rr(qT_g, gsel)
                c["kTt"], c["bkTt"] = rr(kT_g, gsel)
                dma("sp", c["qTt"][:], GQT[:, t0:t0 + 512].rearrange("(h p) t -> p h t", p=128), c["bqTt"], reads=[bGQT], writes=[c["bqTt"]])
                dma("sp", c["kTt"][:], GKT[:, t0:t0 + 512].rearrange("(h p) t -> p h t", p=128), c["bkTt"], reads=[bGKT], writes=[c["bkTt"]])
                c["oft"], c["boft"] = rr(of_g, gsel)
                if dr == 1:
                    c["ggt"], c["bggt"] = rr(gg_g, gsel)
                    c["mxt"], c["bmxt"] = rr(mx_g, gsel)
                    dma("sp", c["oft"][:], OF[:, t0:t0 + 512].rearrange("(h p) t -> p h t", p=128), c["boft"], reads=[bOF], writes=[c["boft"]])
                    dma("sp", c["ggt"][:], GGT[:, t0:t0 + 512].rearrange("(h p) t -> p h t", p=128), c["bggt"], reads=[bGGT], writes=[c["bggt"]])
            gctx[gi] = c

        def stage1(ch):
            if ch["cj"] == 0:
                load_group(ch)
            c = gctx[ch["gi"]]
            full, ci, idx = ch["full"], ch["ci"], ch["idx"]
            gt, bgt, kt, bkt = c["gt"], c["bgt"], c["kt"], c["bkt"]
            kend, bkend = rr(kends, idx)
            Eb, bEb = rr(Ebs, idx)
            tc0 = ci * 128

            def fm1(e):
                e.matmul(p_rb[:, 0:256], lhsT=LSf[dr], rhs=gt[:, ci, :], start=True, stop=True)
                ins = None
                for hp in range(2):
                    if full:
                        ins = e.matmul(p_rb[:, 256 + hp * 128:256 + (hp + 1) * 128], lhsT=gt[:, ci, hp * 128:(hp + 1) * 128], rhs=UIf[dr], start=True, stop=True)
                    else:
                        ins = e.matmul(p_rb[:, 256 + hp * 128 + dcol:256 + hp * 128 + dcol + 1], lhsT=gt[:, ci, hp * 128:(hp + 1) * 128], rhs=UIf[dr][:, dcol:dcol + 1], start=True, stop=True)
                return ins
            op("pe", fm1, reads=[bgt, bcst], writes=[bp_rb])
            op("act", lambda e: e.activation(out=E1[:], in_=p_rb[:, 0:256], func=AF.Exp), reads=[bp_rb], writes=[bE1])
            pbv = p_rb[:, 256:512].rearrange("p (h t) -> p h t", h=2)
            if full:
                op("act", lambda e: e.activation(out=Eb[:], in_=pbv, func=AF.Exp), reads=[bp_rb], writes=[bEb])
                op("act", lambda e: e.activation(out=Enb[:], in_=pbv, func=AF.Exp, scale=-1.0), reads=[bp_rb], writes=[bEnb])
            else:
                op("act", lambda e: e.activation(out=Eb[:, :, dcol:dcol + 1], in_=pbv[:, :, dcol:dcol + 1], func=AF.Exp), reads=[bp_rb], writes=[bEb])
            op("dve", lambda e: e.tensor_tensor(out=kend[:], in0=kt[:, ci, :], in1=E1[:], op=ALU.mult), reads=[bkt, bE1], writes=[bkend])
            if not full:
                return
            qTt, bqTt, kTt, bkTt = c["qTt"], c["bqTt"], c["kTt"], c["bkTt"]
            qq, bqq = rr(qtT, idx)
            op("dve", lambda e: e.tensor_tensor(out=qq[0:64, 0, :, :], in0=qTt[0:64, :, tc0:tc0 + 128], in1=Eb[0:64, :, :], op=ALU.mult), reads=[bqTt, bEb], writes=[bqq])
            op("dve", lambda e: e.tensor_tensor(out=qq[64:128, 1, :, :], in0=qTt[64:128, :, tc0:tc0 + 128], in1=Eb[64:128, :, :], op=ALU.mult), reads=[bqTt, bEb], writes=[bqq])
            op("dve", lambda e: e.tensor_tensor(out=ktT[:], in0=kTt[:, :, tc0:tc0 + 128], in1=Enb[:], op=ALU.mult), reads=[bkTt, bEnb], writes=[bktT])
            pat, bpat = rr(p_at, idx)

            def fm4(e):
                ins = None
                for h in range(4):
                    hp, h2 = h // 2, h % 2
                    ins = e.matmul(pat[:, h, :], lhsT=ktT[:, hp, :], rhs=qq[:, h2, hp, :], start=True, stop=True)
                return ins
            op("pe", fm4, reads=[bktT, bqq], writes=[bpat])
            am, bam = rr(Am, idx)
            op("act", lambda e: e.copy(out=atf[:], in_=pat[:]), reads=[bpat], writes=[batf])
            for h in range(4):
                op("pool", lambda e, h=h: e.tensor_tensor(out=am[:, h, :], in0=atf[:, h, :], in1=UIf[dr], op=ALU.mult),
                   reads=[batf, bcst], writes=[bam])

        def stage2(ch):
            c = gctx[ch["gi"]]
            full, ci, idx = ch["full"], ch["ci"], ch["idx"]
            vt, bvt = c["vt"], c["bvt"]
            kend, bkend = rr(kends, idx)
            Eb, bEb = rr(Ebs, idx)
            tc0 = ci * 128
            if full:
                qq, bqq = rr(qtT, idx)
                am, bam = rr(Am, idx)
                po, bpo = rr(p_o, idx)

                def fm5(e):
                    ins = None
                    for h in range(4):
                        hp, h2 = h // 2, h % 2
                        e.matmul(po[:, h, :], lhsT=vt[:, ci, h * 128:(h + 1) * 128], rhs=am[:, h, :], start=True, stop=False)
                        ins = e.matmul(po[:, h, :], lhsT=Sbf[hp][0][:, :], rhs=qq[:, h2, hp, :], start=False, stop=True)
                    return ins
                op("pe", fm5, reads=[bam, bvt, bqq, Sbf[0][1], Sbf[1][1]], writes=[bpo])

            def fm2(e):
                ins = None
                for h in range(4):
                    hp = h // 2
                    ins = e.matmul(p_sc[:, h, :], lhsT=kend[:, hp * 128:(hp + 1) * 128], rhs=vt[:, ci, h * 128:(h + 1) * 128], start=True, stop=True)
                return ins
            op("pe", fm2, reads=[bkend, bvt], writes=[bp_sc])
            op("act", lambda e: e.copy(out=scs[:], in_=p_sc[:]), reads=[bp_sc], writes=[bscs])
            for h in range(4):
                hp, h2 = h // 2, h % 2
                sl = slice(64 * h2, 64 * h2 + 64)
                op("dve", lambda e, hp=hp, sl=sl, h=h: e.scalar_tensor_tensor(out=S32[hp][0][sl, :], in0=S32[hp][0][sl, :], scalar=Eb[sl, hp, dcol:dcol + 1], in1=scs[sl, h, :], op0=ALU.mult, op1=ALU.add),
                   reads=[bEb, bscs], writes=[S32[hp][1]])
            for hp in range(2):
                op("act", lambda e, hp=hp: e.copy(out=Sbf[hp][0][:], in_=S32[hp][0][:]), reads=[S32[hp][1]], writes=[Sbf[hp][1]], est=0.4)
            if not full:
                return
            oft, boft = c["oft"], c["boft"]
            t0 = c["t0"]
            if dr == 0:
                op("act", lambda e: e.copy(out=oft[:, :, tc0:tc0 + 128], in_=po[:]), reads=[bpo], writes=[boft])
            else:
                ggt, bggt, mxt, bmxt = c["ggt"], c["bggt"], c["mxt"], c["bmxt"]
                op("act", lambda e: e.copy(out=osum[:], in_=po[:]), reads=[bpo], writes=[bosum])
                op("dve", lambda e: e.tensor_tensor(out=osum[:], in0=osum[:], in1=oft[:, :, tc0:tc0 + 128], op=ALU.add), reads=[bosum, boft], writes=[bosum])
                op("act", lambda e: e.activation(out=osq[:], in_=osum[:], func=AF.Square), reads=[bosum], writes=[bosq])
                op("pe", lambda e: e.matmul(p_ss[:], lhsT=onesb[:], rhs=osq[:], start=True, stop=True), reads=[bosq, bonesb], writes=[bp_ss])
                op("act", lambda e: e.activation(out=orst[:], in_=p_ss[:], func=AF.Ln, scale=1.0 / 128, bias=EPS), reads=[bp_ss], writes=[borst])
                op("act", lambda e: e.activation(out=orst[:], in_=orst[:], func=AF.Exp, scale=-0.5), reads=[borst], writes=[borst])
                op("act", lambda e: e.activation(out=gsig[:], in_=ggt[:, :, tc0:tc0 + 128], func=AF.Exp, scale=-1.0), reads=[bggt], writes=[bgsig])
                op("dve", lambda e: e.tensor_scalar(out=gsig[:], in0=gsig[:], scalar1=1.0, scalar2=None, op0=ALU.add), reads=[bgsig], writes=[bgsig])
                op("dve", lambda e: e.reciprocal(out=gsig[:], in_=gsig[:]), reads=[bgsig], writes=[bgsig])
                op("dve", lambda e: e.tensor_tensor(out=gsig[:], in0=gsig[:], in1=ggt[:, :, tc0:tc0 + 128], op=ALU.mult), reads=[bgsig, bggt], writes=[bgsig])
                op("dve", lambda e: e.scalar_tensor_tensor(out=osum[:], in0=osum[:], scalar=nrm_sb[:, 4:5], in1=orst[:], op0=ALU.mult, op1=ALU.mult), reads=[bosum, borst, bnrm], writes=[bosum])
                op("dve", lambda e: e.tensor_tensor(out=mxt[:, :, tc0:tc0 + 128], in0=osum[:], in1=gsig[:], op=ALU.mult), reads=[bosum, bgsig], writes=[bmxt])
            if ch["cj"] == 3:
                if dr == 0:
                    dma("pool", OF[:, t0:t0 + 512].rearrange("(h p) t -> p h t", p=128), oft[:], boft, reads=[boft], writes=[bOF])
                else:
                    dma("pool", MIXT[0:512, t0:t0 + 512].rearrange("(h p) t -> p h t", p=128), mxt[:], bmxt, reads=[bmxt], writes=[bMIXT])

        for i in range(len(chunks) + 1):
            if i < len(chunks):
                stage1(chunks[i])
            if i >= 1:
                stage2(chunks[i - 1])
        P.flush()
    Gp.close()
    if stop == "G":
        return fin()

    P.flush()
    P.defer = False
    W2a = Phase(P)
    wu_b, bwu_b = W2a.sb("wu_b", [128, 8, 2 * DFF], BF16)
    lnf, blnf = W2a.sb("lnf", [128, 8], F32)
    dma("sp", lnf[:], ln_ffn[:, :], blnf, writes=[blnf])
    W2b = Phase(P)
    stg2 = [W2b.sb("stg2_%d" % i, [128, 1408], F32) for i in range(2)]

    def load_wu_chunk(k):
        c, q4 = k // 4, k % 4
        st, bst = rr(stg2, k)
        dma("pool", st[:], w_up[c * 128:(c + 1) * 128, q4 * 1408:(q4 + 1) * 1408], bst, writes=[bst])
        op("dve", lambda e, st=st, c=c, q4=q4: e.tensor_scalar(out=wu_b[:, c, q4 * 1408:(q4 + 1) * 1408], in0=st[:], scalar1=lnf[:, c:c + 1], scalar2=None, op0=ALU.mult),
           reads=[bst, blnf], writes=[bwu_b])
    B = Phase(P)
    P.defer = "B" in SCHED
    KT, bKT = B.sb("KT", [128, L], BF16)
    VA, bVA = B.sb("VA", [128, NT, 130], BF16)
    s65, bs65 = B.sb("s65", [65, 64], F32)
    dma("sp", s65[:], sel65[:, :], bs65, writes=[bs65])
    dma("sp", KT[:], AKT[:, :], bKT, reads=[bAKT], writes=[bKT])
    for c0 in range(0, NT, 8):
        c1 = min(NT, c0 + 8)
        dma("sp", VA[:, c0:c1, :], AV[c0 * 128:c1 * 128, :].rearrange("(n p) c -> p n c", p=128), bVA, reads=[bAV], writes=[bVA])
    QTg = [[B.sb("QT%d_%d" % (g_, i), [128, 512], BF16) for i in range(4)] for g_ in range(2)]
    for g_ in range(2):
        for t_, b_ in QTg[g_]:
            op("dve", lambda e, t_=t_: e.memset(t_[:], 0.0), writes=[b_])
    pst = [B.ps("pst%d" % i, [128, 3, 512], F32) for i in range(2)]
    pTs = [B.sb("pT%d" % i, [128, 3, 512], BF16) for i in range(3)]
    oaccs = [B.ps("oacc%d" % i, [128, 512], F32) for i in range(2)]
    osb = [B.sb("osb%d" % i, [65, 512], F32) for i in range(2)]
    rcp, brcp = B.sb("rcp", [64, 512], F32)
    oat = [B.sb("oat%d" % i, [64, 512], BF16) for i in range(2)]
    groups = []
    k0 = 0
    while k0 < NT:
        n = min(3, NT - k0)
        groups.append((k0, n))
        k0 += n
    items = []
    it = 0
    for (t0, nq) in qblocks:
        for h in range(8):
            for gi_, (k0, n) in enumerate(groups):
                items.append((it, h, t0, nq, gi_, k0, n))
            it += 1
    state = {}

    first_idx = {}
    for idx_, itm in enumerate(items):
        first_idx.setdefault(itm[0], idx_)
    n_items = it

    def load_q(it_):
        if it_ >= n_items or it_ in state:
            return
        _, h, t0, nq, _, _, _ = items[first_idx[it_]]
        g = h // 4
        qt, bqt = rr(QTg[g], it_)
        dma("sp", qt[64 * g:64 * g + 64, 0:nq], AQT[64 * h:64 * h + 64, t0:t0 + nq], bqt, reads=[bAQT], writes=[bqt])
        state[it_] = (qt, bqt)

    def emit_s(idx):
        it_, h, t0, nq, gi_, k0, n = items[idx]
        g = h // 4
        if gi_ == 0:
            load_q(it_)
            load_q(it_ + 1)
            if it_ < 32:
                load_wu_chunk(it_)
        qt, bqt = state[it_]
        st, bst = rr(pst, idx)
        pT, bpT = rr(pTs, idx)

        def fs(e):
            ins = None
            for j in range(n):
                ins = e.matmul(st[:, j, 0:nq], lhsT=KT[:, (k0 + j) * 128:(k0 + j + 1) * 128], rhs=qt[:, 0:nq], start=True, stop=True)
            return ins
        op("pe", fs, reads=[bKT, bqt], writes=[bst])
        op("act", lambda e: e.activation(out=pT[:, 0:n, 0:nq], in_=st[:, 0:n, 0:nq], func=AF.Exp, scale=0.125), reads=[bst], writes=[bpT])

    def emit_pv(idx):
        it_, h, t0, nq, gi_, k0, n = items[idx]
        g = h // 4
        pT, bpT = rr(pTs, idx)
        oacc, boacc = rr(oaccs, it_)
        first = (gi_ == 0)
        last = (gi_ == len(groups) - 1)

        def fpv(e):
            ins = None
            for j in range(n):
                ins = e.matmul(oacc[0:65, 0:nq], lhsT=VA[:, k0 + j, 65 * g:65 * g + 65], rhs=pT[:, j, 0:nq], start=(first and j == 0), stop=(last and j == n - 1))
            return ins
        op("pe", fpv, reads=[bVA, bpT], writes=[boacc])
        if last:
            ob, bob = rr(osb, it_)
            oa, boa = rr(oat, it_)
            op("dve", lambda e: e.tensor_copy(out=ob[:, 0:nq], in_=oacc[0:65, 0:nq]), reads=[boacc], writes=[bob])
            op("pe", lambda e: e.matmul(oacc[0:64, 0:nq], lhsT=s65[:], rhs=ob[:, 0:nq], start=True, stop=True), reads=[bob, bs65], writes=[boacc])
            op("dve", lambda e: e.reciprocal(out=rcp[:, 0:nq], in_=oacc[0:64, 0:nq]), reads=[boacc], writes=[brcp])
            op("dve", lambda e: e.tensor_tensor(out=oa[:, 0:nq], in0=ob[0:64, 0:nq], in1=rcp[:, 0:nq], op=ALU.mult), reads=[bob, brcp], writes=[boa])
            dma("pool", MIXT[512 + 64 * h:512 + 64 * h + 64, t0:t0 + nq], oa[:, 0:nq], boa, reads=[boa], writes=[bMIXT])

    for idx in range(len(items) + 1):
        if idx < len(items):
            emit_s(idx)
        if idx >= 1:
            emit_pv(idx - 1)
    for k_ in range(min(32, n_items), 32):
        load_wu_chunk(k_)
    B.close()
    W2b.close()
    wd_b, bwd_b = W2a.sb("wd_b", [128, 22, D], BF16)
    if stop == "B":
        W2a.close()
        return fin()

    C1 = Phase(P)
    P.defer = "C1" in SCHED
    wo_b, bwo_b = C1.sb("wo_b", [128, 8, D], BF16)
    stg1 = [C1.sb("stg1_%d" % i, [128, D], F32) for i in range(2)]
    for c in range(8):
        st, bst = rr(stg1, c)
        dma("sp", st[:], w_out[c * 128:(c + 1) * 128, :], bst, writes=[bst])
        op("dve", lambda e, st=st, c=c: e.tensor_copy(out=wo_b[:, c, :], in_=st[:]), reads=[bst], writes=[bwo_b])
    for f in range(22):
        st, bst = rr(stg1, f)
        dma("sp", st[:], w_down[f * 128:(f + 1) * 128, :], bst, writes=[bst])
        op("dve", lambda e, st=st, f=f: e.tensor_copy(out=wd_b[:, f, :], in_=st[:]), reads=[bst], writes=[bwd_b])
    zc, bzc = C1.sb("zc", [128, 8, 1], BF16)
    op("dve", lambda e: e.memset(zc[:], 0.0), writes=[bzc])
    dma("pool", H2T[:, 0:1].rearrange("(c p) t -> p c t", p=128), zc[:], bzc, reads=[bzc], writes=[bH2T], allow_slow_non_contiguous=True)
    dma("pool", H2T[:, L + 1:L + 2].rearrange("(c p) t -> p c t", p=128), zc[:], bzc, reads=[bzc], writes=[bH2T], allow_slow_non_contiguous=True)
    mixs = [C1.sb("mix%d" % i, [128, 8, 128], BF16) for i in range(3)]
    xs = [C1.sb("xs%d" % i, [128, D], F32) for i in range(3)]
    x1s = [C1.sb("x1s%d" % i, [128, D], F32) for i in range(2)]
    junk1, bjunk1 = C1.sb("junk1", [128, D], BF16)
    ssq1 = [C1.sb("ssq1_%d" % i, [128, 1], F32) for i in range(2)]
    h2s = [C1.sb("h2s%d" % i, [128, D], BF16) for i in range(2)]
    h2Ts = [C1.sb("h2T%d" % i, [128, 8, 128], BF16) for i in range(2)]
    pc1 = [C1.ps("pc1_%d" % i, [128, 2, 512], F32) for i in range(2)]
    ptr1 = [C1.ps("ptr1_%d" % i, [128, 8, 128], BF16) for i in range(2)]
    for tile in range(C1T):
        r0 = tile * 128
        mx, bmx = rr(mixs, tile)
        xt, bxt = rr(xs, tile)
        x1, bx1 = rr(x1s, tile)
        sq, bsq = rr(ssq1, tile)
        h2, bh2 = rr(h2s, tile)
        h2T, bh2T = rr(h2Ts, tile)
        pc, bpc = rr(pc1, tile)
        pt, bpt = rr(ptr1, tile)
        dma("sp", mx[:], MIXT[:, r0:r0 + 128].rearrange("(c p) t -> p c t", p=128), bmx, reads=[bMIXT], writes=[bmx])
        dma("sp", xt[:], x[r0:r0 + 128, :], bxt, writes=[bxt])

        def fo(e, mx=mx, pc=pc):
            ins = None
            for hf in range(2):
                for c in range(8):
                    ins = e.matmul(pc[:, hf, :], lhsT=mx[:, c, :], rhs=wo_b[:, c, hf * 512:(hf + 1) * 512], start=(c == 0), stop=(c == 7))
            return ins
        op("pe", fo, reads=[bmx, bwo_b], writes=[bpc])
        op("act", lambda e, x1=x1, pc=pc: e.copy(out=x1[:], in_=pc[:].rearrange("p a b -> p (a b)")), reads=[bpc], writes=[bx1])
        op("dve", lambda e, x1=x1, xt=xt: e.tensor_tensor(out=x1[:], in0=x1[:], in1=xt[:], op=ALU.add), reads=[bx1, bxt], writes=[bx1])
        dma("pool", X1[r0:r0 + 128, :], x1[:], bx1, reads=[bx1], writes=[bX1])
        op("act", lambda e, x1=x1, sq=sq: e.activation(out=junk1[:], in_=x1[:], func=AF.Square, accum_out=sq[:]), reads=[bx1], writes=[bjunk1, bsq])
        rstd_from_ssq(None, sq[:], bsq, D)
        op("dve", lambda e, h2=h2, x1=x1, sq=sq: e.tensor_scalar(out=h2[:], in0=x1[:], scalar1=sq[:, 0:1], scalar2=None, op0=ALU.mult), reads=[bx1, bsq], writes=[bh2])

        def ftr1(e, h2=h2, pt=pt):
            ins = None
            for c in range(8):
                ins = e.transpose(out=pt[:, c, :], in_=h2[:, c * 128:(c + 1) * 128], identity=ident_b)
            return ins
        op("pe", ftr1, reads=[bh2, bcstb], writes=[bpt])
        op("act", lambda e, h2T=h2T, pt=pt: e.copy(out=h2T[:], in_=pt[:]), reads=[bpt], writes=[bh2T])
        dma("pool", H2T[:, 1 + r0:1 + r0 + 128].rearrange("(c p) t -> p c t", p=128), h2T[:], bh2T, reads=[bh2T], writes=[bH2T])
    C1.close()
    if stop == "C1":
        W2a.close()
        return fin()

    C2 = Phase(P)
    P.defer = "C2" in SCHED
    lfb, blfb = C2.sb("lfb", [128, D], F32)
    dma("sp", lfb[:], lnf_bc[:, :], blfb, writes=[blfb])
    cvp, bcvp = C2.sb("cvp", [128, 44, 4], F32)
    dma("sp", cvp[:], convp[:, :, :], bcvp, writes=[bcvp])
    TBW = 254
    hws = [C2.sb("hw%d" % i, [128, 8, 256], BF16) for i in range(2)]
    acs = [C2.sb("acs%d" % i, [128, 256], BF16) for i in range(6)]
    pu = [C2.ps("pu%d" % i, [128, 512], F32) for i in range(4)]
    pyt, bpyt = C2.ps("py", [128, 2, 2, 512], F32)
    tas = [C2.sb("ta%d" % i, [128, 256], F32) for i in range(2)]
    tgs = [C2.sb("tg%d" % i, [128, 256], F32) for i in range(2)]
    sgs = [C2.sb("sg%d" % i, [128, 256], F32) for i in range(2)]
    usb = [C2.sb("usb%d" % i, [128, 256], F32) for i in range(4)]
    x1l = [C2.sb("x1l%d" % i, [128, D], F32) for i in range(2)]
    yo = [C2.sb("yo%d" % i, [128, D], F32) for i in range(2)]
    junk2, bjunk2 = C2.sb("junk2", [128, D], BF16)
    ssq2 = [C2.sb("ssq2_%d" % i, [128, 1], F32) for i in range(2)]
    blocks = []
    t = 0
    while t < HO:
        n = min(TBW, HO - t)
        blocks.append((t, n))
        t += n
    nonlocal_puc = [0]
    slc = 0
    slc_ = [0]

    def do_block(bi, t0, n):
            hw, bhw = rr(hws, bi)
            dma("sp", hw[:, :, 0:n + 2], H2T[:, t0:t0 + n + 2].rearrange("(c p) t -> p c t", p=128), bhw, reads=[bH2T], writes=[bhw])
            stash = {}
            slices = []
            s0_ = 0
            while s0_ < n:
                m_ = min(128, n - s0_)
                slices.append((s0_, m_))
                s0_ += m_

            def stage1(f):
                res = []
                for part in range(2):
                    nonlocal_puc[0] += 1
                    pc_ = nonlocal_puc[0]
                    col0 = part * DFF + f * 128
                    pp, bpp = rr(pu, pc_)

                    def fu(e, pp=pp, col0=col0):
                        ins = None
                        for c in range(8):
                            ins = e.matmul(pp[:, 0:n + 2], lhsT=wu_b[:, c, col0:col0 + 128], rhs=hw[:, c, 0:n + 2], start=(c == 0), stop=(c == 7))
                        return ins
                    op("pe", fu, reads=[bwu_b, bhw], writes=[bpp])
                    tt, btt = rr(tas if part == 0 else tgs, f)
                    fc = part * 22 + f
                    us, bus = rr(usb, pc_)
                    op("act", lambda e, us=us, pp=pp: e.copy(out=us[:, 0:n + 2], in_=pp[:, 0:n + 2]), reads=[bpp], writes=[bus])
                    op("dve", lambda e, tt=tt, us=us, fc=fc: e.tensor_scalar(out=tt[:, 0:n], in0=us[:, 1:n + 1], scalar1=cvp[:, fc, 1:2], scalar2=cvp[:, fc, 3:4], op0=ALU.mult, op1=ALU.add),
                       reads=[bus, bcvp], writes=[btt])
                    op("dve", lambda e, tt=tt, us=us, fc=fc: e.scalar_tensor_tensor(out=tt[:, 0:n], in0=us[:, 0:n], scalar=cvp[:, fc, 0:1], in1=tt[:, 0:n], op0=ALU.mult, op1=ALU.add),
                       reads=[bus, bcvp, btt], writes=[btt])
                    op("dve", lambda e, tt=tt, us=us, fc=fc: e.scalar_tensor_tensor(out=tt[:, 0:n], in0=us[:, 2:n + 2], scalar=cvp[:, fc, 2:3], in1=tt[:, 0:n], op0=ALU.mult, op1=ALU.add),
                       reads=[bus, bcvp, btt], writes=[btt])
                    res.append((tt, btt))
                stash[f] = res

            def stage2(f):
                (ta, bta), (tg, btg) = stash.pop(f)
                sg, bsg = rr(sgs, f)
                op("act", lambda e: e.activation(out=sg[:, 0:n], in_=tg[:, 0:n], func=AF.Exp, scale=-1.0), reads=[btg], writes=[bsg])
                op("act", lambda e: e.activation(out=sg[:, 0:n], in_=sg[:, 0:n], func=AF.Ln, bias=1.0), reads=[bsg], writes=[bsg])
                op("act", lambda e: e.activation(out=sg[:, 0:n], in_=sg[:, 0:n], func=AF.Exp, scale=-1.0), reads=[bsg], writes=[bsg])
                op("pool", lambda e: e.tensor_tensor(out=tg[:, 0:n], in0=tg[:, 0:n], in1=sg[:, 0:n], op=ALU.mult), reads=[bsg, btg], writes=[btg])
                ac, bac = rr(acs, f)
                op("pool", lambda e: e.tensor_tensor(out=ac[:, 0:n], in0=ta[:, 0:n], in1=tg[:, 0:n], op=ALU.mult), reads=[bta, btg], writes=[bac])

            def stage3(f):
                ac, bac = rr(acs, f)

                def fd(e):
                    ins = None
                    for si, (s0, m) in enumerate(slices):
                        for hf in range(2):
                            ins = e.matmul(pyt[0:m, si, hf, :], lhsT=ac[:, s0:s0 + m], rhs=wd_b[:, f, hf * 512:(hf + 1) * 512], start=(f == 0), stop=(f == 21))
                    return ins
                op("pe", fd, reads=[bac, bwd_b], writes=[bpyt])

            for f in range(22 + 3):
                if f < 22:
                    stage1(f)
                if 0 <= f - 1 < 22:
                    stage2(f - 1)
                if 0 <= f - 3 < 22:
                    stage3(f - 3)
            for si, (s0, m) in enumerate(slices):
                r0 = t0 + s0
                xl, bxl = rr(x1l, slc_[0])
                yt_, byt = rr(yo, slc_[0])
                sq, bsq = rr(ssq2, slc_[0])
                slc_[0] += 1
                dma("sp", xl[0:m, :], X1[r0:r0 + m, :], bxl, reads=[bX1], writes=[bxl])
                op("act", lambda e, yt_=yt_, m=m, si=si: e.copy(out=yt_[0:m, :], in_=pyt[0:m, si].rearrange("p a b -> p (a b)")), reads=[bpyt], writes=[byt])
                op("dve", lambda e, yt_=yt_, xl=xl, m=m: e.tensor_tensor(out=yt_[0:m, :], in0=yt_[0:m, :], in1=xl[0:m, :], op=ALU.add),
                   reads=[byt, bxl], writes=[byt])
                op("act", lambda e, yt_=yt_, sq=sq, m=m: e.activation(out=junk2[0:m, :], in_=yt_[0:m, :], func=AF.Square, accum_out=sq[0:m, :]), reads=[byt], writes=[bjunk2, bsq])
                rstd_from_ssq(None, sq[0:m, :], bsq, D)
                op("dve", lambda e, yt_=yt_, sq=sq, m=m: e.scalar_tensor_tensor(out=yt_[0:m, :], in0=yt_[0:m, :], scalar=sq[0:m, 0:1], in1=lfb[0:m, :], op0=ALU.mult, op1=ALU.mult),
                   reads=[byt, bsq, blfb], writes=[byt])
                dma("pool", y[r0:r0 + m, :], yt_[0:m, :], byt, reads=[byt], writes=[by])

    for bi, (t0, n) in enumerate(blocks):
        do_block(bi, t0, n)
    C2.close()
    W2a.close()
    return fin()


def host_consts(L):
    i = np.arange(128)
    s = i[:, None]
    c = i[None, :]
    ident = (s == c)
    UI_f = (s <= c)
    UI_b = (s >= c)
    LS_f = (s > c)
    LS_b = (s < c)
    BD = ((s // 64) == (c // 64))
    consts = np.concatenate([m.astype(np.float32) for m in (ident, UI_f, UI_b, LS_f, LS_b, BD)], axis=1)
    sel65 = np.zeros((65, 64), np.float32)
    sel65[64, :] = 1.0
    half = 32
    inv = (1.0 / (10000.0 ** (np.arange(0, half, 2, dtype=np.float32) / half))).astype(np.float32)
    t = np.arange(L)
    row = (t // 64).astype(np.float32)
    col = (t % 64).astype(np.float32)
    ang_r = row[:, None] * inv[None, :]
    ang_c = col[:, None] * inv[None, :]
    cosT = np.zeros((64, L), np.float32)
    sinT = np.zeros((64, L), np.float32)
    for d in range(64):
        hf = d // 32
        j = d % 32
        ang = (ang_r if hf == 0 else ang_c)[:, j % 16]
        cosT[d] = np.cos(ang)
        sinT[d] = np.sin(ang) * (-1.0 if j < 16 else 1.0)
    cosT = np.ascontiguousarray(np.tile(cosT, (2, 1)))
    sinT = np.ascontiguousarray(np.tile(sinT, (2, 1)))
    return consts, sel65, cosT, sinT


def perm64():
    p = np.zeros(64, np.int64)
    for d in range(64):
        j = d % 32
        p[d] = d + 16 if j < 16 else d - 16
    return p


def host_inputs(xs, ln_mix, w_in, w_gk_fwd, b_gk_fwd, w_gk_bwd, b_gk_bwd, gla_out_norm, q_norm, k_norm,
                w_out, ln_ffn, w_up, conv_w, conv_b, w_down, ln_final, L, revs=None):
    f = np.float32
    if revs is None:
        revs = [False] * len(xs)
    consts, sel65, cosT, sinT = host_consts(L)
    w_in0 = np.ascontiguousarray(np.asarray(w_in[0], f))
    pm = perm64()
    cols = []
    for h in range(8):
        cols.append(1568 + h * 64 + pm)
    for h in range(2):
        cols.append(2080 + h * 64 + pm)
    cols = np.concatenate(cols)
    w_rot = np.ascontiguousarray(w_in0[:, cols])
    nrm = np.zeros((128, 8), f)
    qn = np.asarray(q_norm[0], f)
    kn = np.asarray(k_norm[0], f)
    nrm[:, 0] = np.tile(qn, 2)
    nrm[:, 1] = np.tile(qn[pm], 2)
    nrm[:, 2] = np.tile(kn, 2)
    nrm[:, 3] = np.tile(kn[pm], 2)
    nrm[:, 4] = np.asarray(gla_out_norm[0], f)
    cw = np.asarray(conv_w[0], f)
    cb = np.asarray(conv_b[0], f)
    variants = {}
    for rv in (False, True):
        wg = np.zeros((33, 512), f)
        fs, bs = (slice(0, 256), slice(256, 512)) if not rv else (slice(256, 512), slice(0, 256))
        wg[0:16, fs] = w_gk_fwd[0]
        wg[16:32, bs] = w_gk_bwd[0]
        wg[32, fs] = b_gk_fwd[0]
        wg[32, bs] = b_gk_bwd[0]
        convp = np.zeros((128, 44, 4), f)
        for k in range(3):
            kk = k if not rv else 2 - k
            convp[:, :, kk] = cw[k].reshape(44, 128).T
        convp[:, :, 3] = cb.reshape(44, 128).T
        ct = cosT if not rv else np.ascontiguousarray(cosT[:, ::-1])
        sn = sinT if not rv else np.ascontiguousarray(sinT[:, ::-1])
        variants[rv] = {"wg_aug": wg, "convp": convp, "cosT": ct, "sinT": sn}
    common = {
        "w_in": w_in0, "w_rot": w_rot,
        "ln_mix": np.ascontiguousarray(np.asarray(ln_mix[0], f).reshape(8, 128).T),
        "ln_ffn": np.ascontiguousarray(np.asarray(ln_ffn[0], f).reshape(8, 128).T),
        "lnf_bc": np.ascontiguousarray(np.broadcast_to(np.asarray(ln_final, f)[None, :], (128, D))),
        "nrm": nrm, "consts": consts, "sel65": sel65,
        "w_out": np.ascontiguousarray(np.asarray(w_out[0], f)),
        "w_up": np.ascontiguousarray(np.asarray(w_up[0], f)),
        "w_down": np.ascontiguousarray(np.asarray(w_down[0], f)),
    }
    maps = []
    for xx, rv in zip(xs, revs):
        m = dict(common)
        m.update(variants[bool(rv)])
        xx = np.asarray(xx, f)
        m["x"] = np.ascontiguousarray(xx[::-1] if rv else xx)
        maps.append(m)
    return maps


def run_sequences(seqs, W, L, n_cores=8):
    H = L // 2
    zero = np.zeros((L, D), np.float32)
    xs = [zero] * n_cores
    revs = [False] * n_cores
    for k, sq in enumerate(seqs):
        xs[2 * k] = sq
        xs[2 * k + 1] = sq
        revs[2 * k + 1] = True
    maps = host_inputs(xs, W["ln_mix"], W["w_in"], W["w_gk_fwd"], W["b_gk_fwd"], W["w_gk_bwd"], W["b_gk_bwd"],
                       W["gla_out_norm"], W["q_norm"], W["k_norm"], W["w_out"], W["ln_ffn"], W["w_up"],
                       W["conv_w"], W["conv_b"], W["w_down"], W["ln_final"], L, revs)
    nc, _ = build_program(L, H=H)
    res = run_bass_kernel_spmd(nc, maps, core_ids=list(range(n_cores)))
    outs = []
    for k in range(len(seqs)):
        a = np.asarray(res.results[2 * k]["y"], np.float32)
        b = np.asarray(res.results[2 * k + 1]["y"], np.float32)
        outs.append(np.concatenate([a, b[::-1]], axis=0))
    return outs


def kernel(x_prompt, x_sample, ln_mix, w_in, w_gk_fwd, b_gk_fwd, w_gk_bwd, b_gk_bwd, gla_out_norm,
           q_norm, k_norm, w_out, ln_ffn, w_up, conv_w, conv_b, w_down, ln_final):
    x_prompt = np.asarray(x_prompt)
    x_sample = np.asarray(x_sample)
    L = x_prompt.shape[1]
    seqs = [x_prompt[0], x_prompt[1], x_sample[0]]
    W = dict(ln_mix=np.asarray(ln_mix), w_in=np.asarray(w_in), w_gk_fwd=np.asarray(w_gk_fwd), b_gk_fwd=np.asarray(b_gk_fwd),
             w_gk_bwd=np.asarray(w_gk_bwd), b_gk_bwd=np.asarray(b_gk_bwd), gla_out_norm=np.asarray(gla_out_norm),
             q_norm=np.asarray(q_norm), k_norm=np.asarray(k_norm), w_out=np.asarray(w_out), ln_ffn=np.asarray(ln_ffn),
             w_up=np.asarray(w_up), conv_w=np.asarray(conv_w), conv_b=np.asarray(conv_b), w_down=np.asarray(w_down),
             ln_final=np.asarray(ln_final))
    outs = run_sequences(seqs, W, L, 8)
    y_prompt = np.stack([outs[0], outs[1]], axis=0)
    y_sample = outs[2][None]
    return (y_prompt, y_sample)
```

```python
import numpy as np
from contextlib import ExitStack
import concourse.bass as bass
import concourse.mybir as mybir
from concourse.bass_utils import run_bass_kernel_spmd

F32 = mybir.dt.float32
BF16 = mybir.dt.bfloat16
AF = mybir.ActivationFunctionType
ALU = mybir.AluOpType
AX = mybir.AxisListType

ENGS = ("pe", "act", "dve", "pool", "sp")
D = 1024
INW = 2336
DFF = 2816
EPS = 1e-6


class Buf:
    def __init__(self, name):
        self.name = name
        self.w = {}
        self.r = {}
        self.dsem = None
        self.dcnt = 0


class Prog:
    def __init__(self, nc):
        self.nc = nc
        self.es = ExitStack()
        self.sems = {}
        self.cnt = {e: 0 for e in ENGS}
        self.waited = {e: {} for e in ENGS}
        self.E = {"pe": nc.tensor, "act": nc.scalar, "dve": nc.vector, "pool": nc.gpsimd, "sp": nc.sync}
        for e in ENGS:
            self.sems[e] = self.es.enter_context(nc.semaphore("s_" + e))
        self.nins = 0
        self.allbufs = []
        self.defer = False
        self.pending = []
        self.EST = {"pe": 1.0, "act": 0.7, "dve": 0.5, "pool": 1.0, "sp": 0.1}

    def dram(self, name, shape, dt, kind="Internal"):
        return self.nc.dram_tensor(name, list(shape), dt, kind=kind).ap()

    def buf_sem(self, b, eng):
        if b.dsem is None:
            b.dsem = {}
            b.dcnt = {}
            self.allbufs.append(b)
        if eng not in b.dsem:
            sm = self.es.enter_context(self.nc.semaphore("d_%s_%s" % (b.name, eng)))
            b.dsem[eng] = sm
            b.dcnt[eng] = 0
            self.sems[sm] = sm
        return b.dsem[eng]

    def _collect(self, eng, reads, writes):
        need = {}
        for b in reads:
            for k, v in b.w.items():
                if need.get(k, 0) < v:
                    need[k] = v
        for b in writes:
            for k, v in b.w.items():
                if need.get(k, 0) < v:
                    need[k] = v
            for k, v in b.r.items():
                if need.get(k, 0) < v:
                    need[k] = v
        out = []
        wd = self.waited[eng]
        for k, v in need.items():
            if k == eng and eng == "pe":
                continue
            if wd.get(k, 0) >= v:
                continue
            wd[k] = v
            out.append((k, v))
        return out

    def _emit(self, eng, waits, fn, inc):
        e = self.E[eng]
        for k, v in waits:
            e.wait_ge(self.sems[k], v)
        if fn is None:
            return
        ins = fn(e)
        self.nins += 1
        if inc is not None:
            ins.then_inc(self.sems[inc[0]], inc[1])

    def op(self, eng, fn, reads=(), writes=(), est=None):
        if self.defer:
            self.pending.append(("op", eng, fn, tuple(reads), tuple(writes), est if est is not None else self.EST[eng], None, None))
            return
        waits = self._collect(eng, reads, writes)
        self.cnt[eng] += 1
        tok = (eng, self.cnt[eng])
        self._emit(eng, waits, fn, (eng, 1))
        for b in reads:
            b.r[tok[0]] = tok[1]
        for b in writes:
            b.w[tok[0]] = tok[1]

    def dma(self, eng, out, in_, own, reads=(), writes=(), **kw):
        if self.defer:
            self.pending.append(("dma", eng, (out, in_, kw), tuple(reads), tuple(writes), 3.0, own, None))
            return
        waits = self._collect(eng, reads, writes)
        sem = self.buf_sem(own, eng)
        own.dcnt[eng] += 16
        tok = (sem, own.dcnt[eng])
        self._emit(eng, waits, lambda e: e.dma_start(out=out, in_=in_, **kw), (sem, 16))
        for b in reads:
            b.r[tok[0]] = tok[1]
        for b in writes:
            b.w[tok[0]] = tok[1]

    def flush(self):
        ops = self.pending
        self.pending = []
        if not ops:
            return
        import heapq
        n = len(ops)
        preds = [set() for _ in range(n)]
        lastw = {}
        readers = {}
        for i, o in enumerate(ops):
            for b in o[3]:
                if b in lastw:
                    preds[i].add(lastw[b])
            for b in o[4]:
                if b in lastw:
                    preds[i].add(lastw[b])
                for r in readers.get(b, ()):
                    preds[i].add(r)
            for b in o[3]:
                readers.setdefault(b, []).append(i)
            for b in o[4]:
                lastw[b] = i
                readers[b] = []
            preds[i].discard(i)
        succs = [[] for _ in range(n)]
        npred = [0] * n
        for i in range(n):
            npred[i] = len(preds[i])
            for p in preds[i]:
                succs[p].append(i)
        ready_t = [0.0] * n
        finish = [0.0] * n
        heaps = {e: [] for e in ENGS}
        for i in range(n):
            if npred[i] == 0:
                heapq.heappush(heaps[ops[i][1]], (0.0, i))
        efree = {e: 0.0 for e in ENGS}
        order = []
        done = 0
        while done < n:
            best = None
            for e in ENGS:
                h = heaps[e]
                if not h:
                    continue
                rt, i = h[0]
                st = max(rt, efree[e])
                if best is None or (st, i) < (best[0], best[2]):
                    best = (st, e, i)
            st, e, i = best
            heapq.heappop(heaps[e])
            o = ops[i]
            if o[0] == "dma":
                efree[e] = st + 0.1
                finish[i] = st + o[5]
            else:
                efree[e] = st + o[5]
                finish[i] = st + o[5] + 0.15
            order.append(i)
            done += 1
            for sidx in succs[i]:
                npred[sidx] -= 1
                if finish[i] > ready_t[sidx]:
                    ready_t[sidx] = finish[i]
                if npred[sidx] == 0:
                    heapq.heappush(heaps[ops[sidx][1]], (ready_t[sidx], sidx))
        prev = self.defer
        self.defer = False
        for i in order:
            o = ops[i]
            if o[0] == "dma":
                out, in_, kw = o[2]
                self.dma(o[1], out, in_, o[6], reads=o[3], writes=o[4], **kw)
            else:
                self.op(o[1], o[2], reads=o[3], writes=o[4])
        self.defer = prev

    def barrier(self):
        self.flush()
        b = Buf("barrier")
        for e in ENGS:
            if self.cnt[e] > 0:
                b.w[e] = self.cnt[e]
        for k, v in self.dsem_counts().items():
            b.w[k] = v
        for e in ENGS:
            self.wait_all(e, [b])

    def dsem_counts(self):
        out = {}
        for bb in self.allbufs:
            for eng, sm in bb.dsem.items():
                out[sm] = bb.dcnt[eng]
        return out

    def wait_all(self, eng, bufs):
        self.flush()
        waits = self._collect(eng, bufs, ())
        self._emit(eng, waits, None, None)


class Phase:
    def __init__(self, P):
        self.P = P
        self.es = ExitStack()

    def sb(self, name, shape, dt):
        t = self.es.enter_context(self.P.nc.sbuf_tensor(name, list(shape), dt))
        return t, Buf(name)

    def ps(self, name, shape, dt=F32):
        t = self.es.enter_context(self.P.nc.psum_tensor(name, list(shape), dt))
        return t, Buf(name)

    def close(self):
        self.P.barrier()
        self.es.close()


def rr(lst, i):
    return lst[i % len(lst)]


def build_program(L, dbg=False, stop=None, H=None):
    assert L % 512 == 0
    EXTQ = 16
    NT = L // 128
    NB = L // 512
    if H is None:
        HO = L
        QB = NB
        qblocks = [(qb * 512, 512) for qb in range(NB)]
        C1T = NT
    else:
        assert H % 512 == 0 and H < L
        HO = H
        QB = H // 512 + 1
        qblocks = [(qb * 512, 512) for qb in range(H // 512)] + [(H, EXTQ)]
        C1T = H // 128 + 1
    nc = bass.Bass("TRN2", target_bir_lowering=False)
    P = Prog(nc)
    op, dma = P.op, P.dma

    def ext(name, shape, dt=F32):
        return P.dram(name, shape, dt, kind="ExternalInput")

    x = ext("x", [L, D])
    w_in = ext("w_in", [D, INW])
    w_rot = ext("w_rot", [D, 640])
    ln_mix = ext("ln_mix", [128, 8])
    ln_ffn = ext("ln_ffn", [128, 8])
    lnf_bc = ext("lnf_bc", [128, D])
    wg_aug = ext("wg_aug", [33, 512])
    nrm = ext("nrm", [128, 8])
    cosT = ext("cosT", [128, L])
    sinT = ext("sinT", [128, L])
    consts = ext("consts", [128, 6 * 128])
    sel65 = ext("sel65", [65, 64])
    w_out = ext("w_out", [D, D])
    w_up = ext("w_up", [D, 2 * DFF])
    w_down = ext("w_down", [DFF, D])
    convp = ext("convp", [128, 44, 4])
    y = P.dram("y", [HO, D], F32, kind="ExternalOutput")
    by = Buf("y")

    kind = "ExternalOutput" if dbg else "Internal"
    GQT = P.dram("GQT", [256, L], F32, kind=kind); bGQT = Buf("GQT")
    GKT = P.dram("GKT", [256, L], F32, kind=kind); bGKT = Buf("GKT")
    GGT = P.dram("GGT", [512, L], F32, kind=kind); bGGT = Buf("GGT")
    GK = P.dram("GK", [L, 256], F32, kind=kind); bGK = Buf("GK")
    GV = P.dram("GV", [L, 512], BF16, kind=kind); bGV = Buf("GV")
    G = P.dram("G", [L, 512], F32, kind=kind); bG = Buf("G")
    AQT = P.dram("AQT", [512, L], BF16, kind=kind); bAQT = Buf("AQT")
    AKT = P.dram("AKT", [128, L], BF16, kind=kind); bAKT = Buf("AKT")
    AV = P.dram("AV", [L, 130], BF16, kind=kind); bAV = Buf("AV")
    OF = P.dram("OF", [512, L], F32, kind=kind); bOF = Buf("OF")
    MIXT = P.dram("MIXT", [1024, L], BF16, kind=kind); bMIXT = Buf("MIXT")
    X1 = P.dram("X1", [L, D], F32, kind=kind); bX1 = Buf("X1")
    H2T = P.dram("H2T", [1024, L + 2], BF16, kind=kind); bH2T = Buf("H2T")

    G0 = Phase(P)
    cst, bcst = G0.sb("cst", [128, 6 * 128], F32)
    cstb, bcstb = G0.sb("cstb", [128, 6 * 128], BF16)
    nrm_sb, bnrm = G0.sb("nrm_sb", [128, 8], F32)
    dma("sp", cst[:], consts[:, :], bcst, writes=[bcst])
    dma("sp", nrm_sb[:], nrm[:, :], bnrm, writes=[bnrm])
    op("dve", lambda e: e.tensor_copy(out=cstb[:], in_=cst[:]), reads=[bcst], writes=[bcstb])
    ident_b = cstb[:, 0:128]
    UIf = [cst[:, 128:256], cst[:, 256:384]]
    LSf = [cst[:, 384:512], cst[:, 512:640]]
    UIb = [cstb[:, 128:256], cstb[:, 256:384]]
    BDb = cstb[:, 640:768]

    def fin():
        P.wait_all("pool", [by, bGQT, bGKT, bGGT, bGK, bGV, bG, bAQT, bAKT, bAV, bOF, bMIXT, bX1, bH2T])
        G0.close()
        P.es.close()
        return nc, P

    def rstd_from_ssq(ph_bufs, ssq, bssq, n, shape_ap=None):
        op("act", lambda e: e.activation(out=ssq, in_=ssq, func=AF.Ln, scale=1.0 / n, bias=EPS), reads=[bssq], writes=[bssq])
        op("act", lambda e: e.activation(out=ssq, in_=ssq, func=AF.Exp, scale=-0.5), reads=[bssq], writes=[bssq])

    import os
    SCHED = os.environ.get("SCHED", "A,G,C1,C2").split(",")
    A = Phase(P)
    P.defer = "A" in SCHED
    WC = INW + 640
    wb, bwb = A.sb("wb", [128, 8, WC], BF16)
    lnm, blnm = A.sb("lnm", [128, 8], F32)
    dma("sp", lnm[:], ln_mix[:, :], blnm, writes=[blnm])
    stg = [A.sb("stg%d" % i, [128, WC], F32) for i in range(2)]
    for c in range(8):
        st, bst = rr(stg, c)
        dma("sp", st[:, 0:INW], w_in[c * 128:(c + 1) * 128, :], bst, writes=[bst])
        dma("sp", st[:, INW:WC], w_rot[c * 128:(c + 1) * 128, :], bst, writes=[bst])
        op("dve", lambda e, st=st, c=c: e.tensor_scalar(out=wb[:, c, :], in0=st[:], scalar1=lnm[:, c:c + 1], scalar2=None, op0=ALU.mult),
           reads=[bst, blnm], writes=[bwb])
    wgf, bwgf = A.sb("wgf", [33, 512], F32)
    wgb, bwgb = A.sb("wgb", [33, 512], BF16)
    dma("sp", wgf[:], wg_aug[:, :], bwgf, writes=[bwgf])
    op("dve", lambda e: e.tensor_copy(out=wgb[:], in_=wgf[:]), reads=[bwgf], writes=[bwgb])

    xts = [A.sb("xt%d" % i, [128, D], F32) for i in range(3)]
    junk, bjunk = A.sb("junk", [128, D], BF16)
    ssqs = [A.sb("ssq%d" % i, [128, 1], F32) for i in range(3)]
    xbs = [A.sb("xb%d" % i, [128, D], BF16) for i in range(2)]
    xnTs = [A.sb("xnT%d" % i, [128, 8, 512], BF16) for i in range(2)]
    raug, braug = A.sb("raug", [33, 512], BF16)
    op("dve", lambda e: e.memset(raug[:], 1.0), writes=[braug])
    ptr = [A.ps("ptr%d" % i, [128, 8, 128], BF16) for i in range(1)]
    pfm = [A.ps("pfm%d" % i, [128, 512], F32) for i in range(3)]
    pss, bpss = A.ps("pss", [128, 512], F32)
    ptm1, bptm1 = A.ps("ptm1", [128, 512], F32)
    ptm2, bptm2 = A.ps("ptm2", [128, 512], F32)
    pz, bpz = A.ps("pz", [128, 512], F32)
    fmo = [A.sb("fmo%d" % i, [128, 512], F32) for i in range(3)]
    cos_sb = [A.sb("cos%d" % i, [128, 512], F32) for i in range(2)]
    sin_sb = [A.sb("sin%d" % i, [128, 512], F32) for i in range(2)]
    sqb, bsqb = A.sb("sqb", [128, 512], BF16)
    rsd, brsd = A.sb("rsd", [128, 512], F32)
    t1s, bt1 = A.sb("t1s", [128, 512], F32)
    t2s, bt2 = A.sb("t2s", [128, 512], F32)
    qfo = [A.sb("qfo%d" % i, [128, 512], BF16) for i in range(2)]
    gko = [A.sb("gko%d" % i, [128, 256], F32) for i in range(2)]
    gvo = [A.sb("gvo%d" % i, [128, 512], BF16) for i in range(2)]
    avo = [A.sb("avo%d" % i, [128, 2, 65], BF16) for i in range(2)]
    for t, b in avo:
        op("dve", lambda e, t=t: e.memset(t[:], 1.0), writes=[b])
    ge, bge = A.sb("ge", [128, 512], F32)
    go = [A.sb("go%d" % i, [128, 512], F32) for i in range(2)]

    fmcount = [0]

    def fm_proj(col0, ncols, xnT, bxnT):
        pt, bpt = rr(pfm, fmcount[0])
        fmcount[0] += 1

        def f(e):
            ins = None
            for c in range(8):
                ins = e.matmul(pt[0:ncols, :], lhsT=wb[:, c, col0:col0 + ncols], rhs=xnT[:, c, :], start=(c == 0), stop=(c == 7))
            return ins
        op("pe", f, reads=[bwb, bxnT], writes=[bpt])
        return pt, bpt

    stc = [0]
    import os
    LVL = int(os.environ.get("PALVL", "9"))
    def stage_a1(blk):
        xnT, bxnT = rr(xnTs, blk)
        for ti in range(4):
            tile = blk * 4 + ti
            xt, bxt = rr(xts, tile)
            ssq, bssq = rr(ssqs, tile)
            xb, bxb = rr(xbs, tile)
            pt, bpt = rr(ptr, tile)
            dma("sp", xt[:], x[tile * 128:(tile + 1) * 128, :], bxt, writes=[bxt])
            op("act", lambda e, xt=xt, ssq=ssq: e.activation(out=junk[:], in_=xt[:], func=AF.Square, accum_out=ssq[:]),
               reads=[bxt], writes=[bjunk, bssq])
            rstd_from_ssq(None, ssq[:], bssq, D)
            op("dve", lambda e, xb=xb, xt=xt, ssq=ssq: e.tensor_scalar(out=xb[:], in0=xt[:], scalar1=ssq[:, 0:1], scalar2=None, op0=ALU.mult),
               reads=[bxt, bssq], writes=[bxb])

            def ftr(e, xb=xb, pt=pt):
                ins = None
                for c in range(8):
                    ins = e.transpose(out=pt[:, c, :], in_=xb[:, c * 128:(c + 1) * 128], identity=ident_b)
                return ins
            op("pe", ftr, reads=[bxb, bcstb], writes=[bpt])
            op("act", lambda e, xnT=xnT, pt=pt, ti=ti: e.copy(out=xnT[:, :, ti * 128:(ti + 1) * 128], in_=pt[:]),
               reads=[bpt], writes=[bxnT])
    def stage_proj(blk):
        xnT, bxnT = rr(xnTs, blk)
        t0 = blk * 512
        qside = blk < QB
        for (col0, dst, bdst, row0, scale) in [] if not qside else (
            [(0 + 128 * j, GQT, bGQT, 128 * j, 0.125) for j in range(2)]
            + [(256 + 128 * j, GKT, bGKT, 128 * j, 1.0) for j in range(2)]
            + [(1024 + 128 * j, GGT, bGGT, 128 * j, 1.0) for j in range(4)]
        ):
            pt, bpt = fm_proj(col0, 128, xnT, bxnT)
            so, bso = rr(fmo, stc[0]); stc[0] += 1
            op("act", lambda e, so=so, pt=pt, scale=scale: e.activation(out=so[:], in_=pt[:], func=AF.Copy, scale=scale),
               reads=[bpt], writes=[bso])
            dma("pool", dst[row0:row0 + 128, t0:t0 + 512], so[:], bso, reads=[bso], writes=[bdst])
        pt, bpt = fm_proj(1536, 32, xnT, bxnT)
        op("act", lambda e, pt=pt: e.copy(out=raug[0:32, :], in_=pt[0:32, :]), reads=[bpt], writes=[braug])
        cs, bcs = rr(cos_sb, blk)
        sn, bsn = rr(sin_sb, blk)
        dma("sp", cs[:], cosT[:, t0:t0 + 512], bcs, writes=[bcs])
        dma("sp", sn[:], sinT[:, t0:t0 + 512], bsn, writes=[bsn])
        for j in (range(5) if qside else [4]):
            col0 = 1568 + 128 * j
            colr = INW + 128 * j
            wcol = 0 if j < 4 else 2
            pa, bpa = fm_proj(col0, 128, xnT, bxnT)
            pr, bpr = fm_proj(colr, 128, xnT, bxnT)
            op("act", lambda e, pa=pa: e.activation(out=sqb[:], in_=pa[:], func=AF.Square), reads=[bpa], writes=[bsqb])
            op("pe", lambda e: e.matmul(pss[:], lhsT=BDb, rhs=sqb[:], start=True, stop=True), reads=[bsqb, bcstb], writes=[bpss])
            op("act", lambda e: e.activation(out=rsd[:], in_=pss[:], func=AF.Ln, scale=1.0 / 64, bias=EPS), reads=[bpss], writes=[brsd])
            op("act", lambda e: e.activation(out=rsd[:], in_=rsd[:], func=AF.Exp, scale=-0.5), reads=[brsd], writes=[brsd])
            op("act", lambda e, pa=pa, wcol=wcol: e.activation(out=t1s[:], in_=pa[:], func=AF.Copy, scale=nrm_sb[:, wcol:wcol + 1]), reads=[bpa, bnrm], writes=[bt1])
            op("act", lambda e, pr=pr, wcol=wcol: e.activation(out=t2s[:], in_=pr[:], func=AF.Copy, scale=nrm_sb[:, wcol + 1:wcol + 2]), reads=[bpr, bnrm], writes=[bt2])
            op("dve", lambda e, cs=cs: e.tensor_tensor(out=t1s[:], in0=t1s[:], in1=cs[:], op=ALU.mult), reads=[bt1, bcs], writes=[bt1])
            op("dve", lambda e, sn=sn: e.tensor_tensor(out=t2s[:], in0=t2s[:], in1=sn[:], op=ALU.mult), reads=[bt2, bsn], writes=[bt2])
            op("dve", lambda e: e.tensor_tensor(out=t1s[:], in0=t1s[:], in1=t2s[:], op=ALU.add), reads=[bt1, bt2], writes=[bt1])
            qo, bqo = rr(qfo, j)
            op("dve", lambda e, qo=qo: e.tensor_tensor(out=qo[:], in0=t1s[:], in1=rsd[:], op=ALU.mult), reads=[bt1, brsd], writes=[bqo])
            if j < 4:
                dma("pool", AQT[128 * j:128 * j + 128, t0:t0 + 512], qo[:], bqo, reads=[bqo], writes=[bAQT])
            else:
                dma("pool", AKT[:, t0:t0 + 512], qo[:], bqo, reads=[bqo], writes=[bAKT])
        for ti in range(4):
            tile = blk * 4 + ti
            r0 = tile * 128
            lhs = lambda c, ti=ti: xnT[:, c, ti * 128:(ti + 1) * 128]

            def ftm(e, lhs=lhs):
                ins = None
                for c in range(8):
                    ins = e.matmul(ptm1[:, 0:256], lhsT=lhs(c), rhs=wb[:, c, 256:512], start=(c == 0), stop=(c == 7))
                for c in range(8):
                    ins = e.matmul(ptm1[:, 256:384], lhsT=lhs(c), rhs=wb[:, c, 2208:2336], start=(c == 0), stop=(c == 7))
                return ins
            op("pe", ftm, reads=[bwb, bxnT], writes=[bptm1])

            def ftm2(e, lhs=lhs):
                ins = None
                for c in range(8):
                    ins = e.matmul(ptm2[:], lhsT=lhs(c), rhs=wb[:, c, 512:1024], start=(c == 0), stop=(c == 7))
                return ins
            op("pe", ftm2, reads=[bwb, bxnT], writes=[bptm2])
            op("pe", lambda e, ti=ti: e.matmul(pz[:], lhsT=raug[:, ti * 128:(ti + 1) * 128], rhs=wgb[:], start=True, stop=True),
               reads=[braug, bwgb], writes=[bpz])
            gk_t, bgk_t = rr(gko, tile)
            gv_t, bgv_t = rr(gvo, tile)
            av_t, bav_t = rr(avo, tile)
            g_t, bg_t = rr(go, tile)
            op("dve", lambda e, gk_t=gk_t: e.tensor_copy(out=gk_t[:], in_=ptm1[:, 0:256]), reads=[bptm1], writes=[bgk_t])
            op("dve", lambda e, av_t=av_t: e.tensor_copy(out=av_t[:, :, 0:64], in_=ptm1[:, 256:384].rearrange("p (g d) -> p g d", g=2)),
               reads=[bptm1], writes=[bav_t])
            op("act", lambda e, gv_t=gv_t: e.copy(out=gv_t[:], in_=ptm2[:]), reads=[bptm2], writes=[bgv_t])
            op("act", lambda e: e.activation(out=ge[:], in_=pz[:], func=AF.Exp, scale=-1.0), reads=[bpz], writes=[bge])
            op("act", lambda e: e.activation(out=ge[:], in_=ge[:], func=AF.Ln, bias=1.0), reads=[bge], writes=[bge])
            op("dve", lambda e, g_t=g_t: e.tensor_scalar(out=g_t[:], in0=ge[:], scalar1=-1.0 / 16.0, scalar2=None, op0=ALU.mult),
               reads=[bge], writes=[bg_t])
            dma("pool", GK[r0:r0 + 128, :], gk_t[:], bgk_t, reads=[bgk_t], writes=[bGK])
            dma("pool", GV[r0:r0 + 128, :], gv_t[:], bgv_t, reads=[bgv_t], writes=[bGV])
            dma("pool", AV[r0:r0 + 128, :], av_t[:].rearrange("p g d -> p (g d)"), bav_t, reads=[bav_t], writes=[bAV])
            dma("pool", G[r0:r0 + 128, :], g_t[:], bg_t, reads=[bg_t], writes=[bG])
    for blk in range(NB + 1):
        if blk < NB:
            stage_a1(blk)
        if blk >= 1:
            stage_proj(blk - 1)
    A.close()
    if stop == "A":
        return fin()

    Gp = Phase(P)
    P.defer = "G" in SCHED
    S32 = [Gp.sb("S32_%d" % i, [128, 128], F32) for i in range(2)]
    Sbf = [Gp.sb("Sbf_%d" % i, [128, 128], BF16) for i in range(2)]
    onesb, bonesb = Gp.sb("onesb", [128, 128], BF16)
    op("dve", lambda e: e.memset(onesb[:], 1.0), writes=[bonesb])
    NGB = 2
    g_g = [Gp.sb("g_g%d" % i, [128, 4, 256], F32) for i in range(NGB)]
    k_g = [Gp.sb("k_g%d" % i, [128, 4, 256], F32) for i in range(NGB)]
    v_g = [Gp.sb("v_g%d" % i, [128, 4, 512], BF16) for i in range(NGB)]
    qT_g = [Gp.sb("qT_g%d" % i, [128, 2, 512], F32) for i in range(NGB)]
    kT_g = [Gp.sb("kT_g%d" % i, [128, 2, 512], F32) for i in range(NGB)]
    of_g = [Gp.sb("of_g%d" % i, [128, 4, 512], F32) for i in range(NGB)]
    gg_g = [Gp.sb("gg_g%d" % i, [128, 4, 512], F32) for i in range(NGB)]
    mx_g = [Gp.sb("mx_g%d" % i, [128, 4, 512], BF16) for i in range(NGB)]
    E1, bE1 = Gp.sb("E1", [128, 256], F32)
    kend, bkend = Gp.sb("kend", [128, 256], BF16)
    Eb, bEb = Gp.sb("Eb", [128, 2, 128], F32)
    Enb, bEnb = Gp.sb("Enb", [128, 2, 128], F32)
    qtT = [Gp.sb("qtT%d" % i, [128, 2, 2, 128], BF16) for i in range(2)]
    for t_, b_ in qtT:
        op("dve", lambda e, t_=t_: e.memset(t_[:], 0.0), writes=[b_])
    ktT, bktT = Gp.sb("ktT", [128, 2, 128], BF16)
    Am = [Gp.sb("Am%d" % i, [128, 4, 128], BF16) for i in range(2)]
    osum, bosum = Gp.sb("osum", [128, 4, 128], F32)
    atf, batf = Gp.sb("atf", [128, 4, 128], F32)
    scs, bscs = Gp.sb("scs", [128, 4, 128], F32)
    osq, bosq = Gp.sb("osq", [128, 4, 128], BF16)
    orst, borst = Gp.sb("orst", [128, 4, 128], F32)
    gsig, bgsig = Gp.sb("gsig", [128, 4, 128], F32)
    p_rb, bp_rb = Gp.ps("p_rb", [128, 512], F32)
    p_at = [Gp.ps("p_at%d" % i, [128, 4, 128], F32) for i in range(2)]
    p_o = [Gp.ps("p_o%d" % i, [128, 4, 128], F32) for i in range(2)]
    p_sc, bp_sc = Gp.ps("p_sc", [128, 4, 128], F32)
    p_ss, bp_ss = Gp.ps("p_ss", [128, 4, 128], F32)

    NG = L // 512
    kends = [Gp.sb("kend_%d" % i, [128, 256], BF16) for i in range(2)]
    Ebs = [Gp.sb("Eb_%d" % i, [128, 2, 128], F32) for i in range(2)]
    for dr in range(2):
        for i in range(2):
            op("dve", lambda e, i=i: e.memset(S32[i][0][:], 0.0), writes=[S32[i][1]])
            op("dve", lambda e, i=i: e.memset(Sbf[i][0][:], 0.0), writes=[Sbf[i][1]])
        chunks = []
        for gi in range(QB if dr == 0 else NG):
            grp = gi if dr == 0 else NG - 1 - gi
            for cj in range(4):
                chunks.append(dict(gi=gi, grp=grp, full=(grp < QB), cj=cj, ci=(cj if dr == 0 else 3 - cj), idx=len(chunks)))
        gctx = {}
        dcol = 127 if dr == 0 else 0

        def load_group(ch):
            gi, grp, full = ch["gi"], ch["grp"], ch["full"]
            t0 = grp * 512
            gsel = dr * NG + gi
            c = dict(t0=t0)
            c["gt"], c["bgt"] = rr(g_g, gsel)
            c["kt"], c["bkt"] = rr(k_g, gsel)
            c["vt"], c["bvt"] = rr(v_g, gsel)
            dma("sp", c["gt"][:], G[t0:t0 + 512, dr * 256:(dr + 1) * 256].rearrange("(n p) c -> p n c", p=128), c["bgt"], reads=[bG], writes=[c["bgt"]])
            dma("sp", c["kt"][:], GK[t0:t0 + 512, :].rearrange("(n p) c -> p n c", p=128), c["bkt"], reads=[bGK], writes=[c["bkt"]])
            dma("sp", c["vt"][:], GV[t0:t0 + 512, :].rearrange("(n p) c -> p n c", p=128), c["bvt"], reads=[bGV], writes=[c["bvt"]])
            if full:
                c["qTt"], c["bqTt"] = rr(qT_g, gsel)
                c["kTt"], c["bkTt"] = rr(kT_g, gsel)
                dma("sp", c["qTt"][:], GQT[:, t0:t0 + 512].rearrange("(h p) t -> p h t", p=128), c["bqTt"], reads=[bGQT], writes=[c["bqTt"]])
                dma("sp", c["kTt"][:], GKT[:, t0:t0 + 512].rearrange("(h p) t -> p h t", p=128), c["bkTt"], reads=[bGKT], writes=[c["bkTt"]])
                c["oft"], c["boft"] = rr(of_g, gsel)
                if dr == 1:
                    c["ggt"], c["bggt"] = rr(gg_g, gsel)
                    c["mxt"], c["bmxt"] = rr(mx_g, gsel)
                    dma("sp", c["oft"][:], OF[:, t0:t0 + 512].rearrange("(h p) t -> p h t", p=128), c["boft"], reads=[bOF], writes=[c["boft"]])
                    dma("sp", c["ggt"][:], GGT[:, t0:t0 + 512].rearrange("(h p) t -> p h t", p=128), c["bggt"], reads=[bGGT], writes=[c["bggt"]])
            gctx[gi] = c

        def stage1(ch):
            if ch["cj"] == 0:
                load_group(ch)
            c = gctx[ch["gi"]]
            full, ci, idx = ch["full"], ch["ci"], ch["idx"]
            gt, bgt, kt, bkt = c["gt"], c["bgt"], c["kt"], c["bkt"]
            kend, bkend = rr(kends, idx)
            Eb, bEb = rr(Ebs, idx)
            tc0 = ci * 128

            def fm1(e):
                e.matmul(p_rb[:, 0:256], lhsT=LSf[dr], rhs=gt[:, ci, :], start=True, stop=True)
                ins = None
                for hp in range(2):
                    if full:
                        ins = e.matmul(p_rb[:, 256 + hp * 128:256 + (hp + 1) * 128], lhsT=gt[:, ci, hp * 128:(hp + 1) * 128], rhs=UIf[dr], start=True, stop=True)
                    else:
                        ins = e.matmul(p_rb[:, 256 + hp * 128 + dcol:256 + hp * 128 + dcol + 1], lhsT=gt[:, ci, hp * 128:(hp + 1) * 128], rhs=UIf[dr][:, dcol:dcol + 1], start=True, stop=True)
                return ins
            op("pe", fm1, reads=[bgt, bcst], writes=[bp_rb])
            op("act", lambda e: e.activation(out=E1[:], in_=p_rb[:, 0:256], func=AF.Exp), reads=[bp_rb], writes=[bE1])
            pbv = p_rb[:, 256:512].rearrange("p (h t) -> p h t", h=2)
            if full:
                op("act", lambda e: e.activation(out=Eb[:], in_=pbv, func=AF.Exp), reads=[bp_rb], writes=[bEb])
                op("act", lambda e: e.activation(out=Enb[:], in_=pbv, func=AF.Exp, scale=-1.0), reads=[bp_rb], writes=[bEnb])
            else:
                op("act", lambda e: e.activation(out=Eb[:, :, dcol:dcol + 1], in_=pbv[:, :, dcol:dcol + 1], func=AF.Exp), reads=[bp_rb], writes=[bEb])
            op("dve", lambda e: e.tensor_tensor(out=kend[:], in0=kt[:, ci, :], in1=E1[:], op=ALU.mult), reads=[bkt, bE1], writes=[bkend])
            if not full:
                return
            qTt, bqTt, kTt, bkTt = c["qTt"], c["bqTt"], c["kTt"], c["bkTt"]
            qq, bqq = rr(qtT, idx)
            op("dve", lambda e: e.tensor_tensor(out=qq[0:64, 0, :, :], in0=qTt[0:64, :, tc0:tc0 + 128], in1=Eb[0:64, :, :], op=ALU.mult), reads=[bqTt, bEb], writes=[bqq])
            op("dve", lambda e: e.tensor_tensor(out=qq[64:128, 1, :, :], in0=qTt[64:128, :, tc0:tc0 + 128], in1=Eb[64:128, :, :], op=ALU.mult), reads=[bqTt, bEb], writes=[bqq])
            op("dve", lambda e: e.tensor_tensor(out=ktT[:], in0=kTt[:, :, tc0:tc0 + 128], in1=Enb[:], op=ALU.mult), reads=[bkTt, bEnb], writes=[bktT])
            pat, bpat = rr(p_at, idx)

            def fm4(e):
                ins = None
                for h in range(4):
                    hp, h2 = h // 2, h % 2
                    ins = e.matmul(pat[:, h, :], lhsT=ktT[:, hp, :], rhs=qq[:, h2, hp, :], start=True, stop=True)
                return ins
            op("pe", fm4, reads=[bktT, bqq], writes=[bpat])
            am, bam = rr(Am, idx)
            op("act", lambda e: e.copy(out=atf[:], in_=pat[:]), reads=[bpat], writes=[batf])
            for h in range(4):
                op("pool", lambda e, h=h: e.tensor_tensor(out=am[:, h, :], in0=atf[:, h, :], in1=UIf[dr], op=ALU.mult),
                   reads=[batf, bcst], writes=[bam])

        def stage2(ch):
            c = gctx[ch["gi"]]
            full, ci, idx = ch["full"], ch["ci"], ch["idx"]
            vt, bvt = c["vt"], c["bvt"]
            kend, bkend = rr(kends, idx)
            Eb, bEb = rr(Ebs, idx)
            tc0 = ci * 128
            if full:
                qq, bqq = rr(qtT, idx)
                am, bam = rr(Am, idx)
                po, bpo = rr(p_o, idx)

                def fm5(e):
                    ins = None
                    for h in range(4):
                        hp, h2 = h // 2, h % 2
                        e.matmul(po[:, h, :], lhsT=vt[:, ci, h * 128:(h + 1) * 128], rhs=am[:, h, :], start=True, stop=False)
                        ins = e.matmul(po[:, h, :], lhsT=Sbf[hp][0][:, :], rhs=qq[:, h2, hp, :], start=False, stop=True)
                    return ins
                op("pe", fm5, reads=[bam, bvt, bqq, Sbf[0][1], Sbf[1][1]], writes=[bpo])

            def fm2(e):
                ins = None
                for h in range(4):
                    hp = h // 2
                    ins = e.matmul(p_sc[:, h, :], lhsT=kend[:, hp * 128:(hp + 1) * 128], rhs=vt[:, ci, h * 128:(h + 1) * 128], start=True, stop=True)
                return ins
            op("pe", fm2, reads=[bkend, bvt], writes=[bp_sc])
            op("act", lambda e: e.copy(out=scs[:], in_=p_sc[:]), reads=[bp_sc], writes=[bscs])
            for h in range(4):
                hp, h2 = h // 2, h % 2
                sl = slice(64 * h2, 64 * h2 + 64)
                op("dve", lambda e, hp=hp, sl=sl, h=h: e.scalar_tensor_tensor(out=S32[hp][0][sl, :], in0=S32[hp][0][sl, :], scalar=Eb[sl, hp, dcol:dcol + 1], in1=scs[sl, h, :], op0=ALU.mult, op1=ALU.add),
                   reads=[bEb, bscs], writes=[S32[hp][1]])
            for hp in range(2):
                op("act", lambda e, hp=hp: e.copy(out=Sbf[hp][0][:], in_=S32[hp][0][:]), reads=[S32[hp][1]], writes=[Sbf[hp][1]], est=0.4)
            if not full:
                return
            oft, boft = c["oft"], c["boft"]
            t0 = c["t0"]
            if dr == 0:
                op("act", lambda e: e.copy(out=oft[:, :, tc0:tc0 + 128], in_=po[:]), reads=[bpo], writes=[boft])
            else:
                ggt, bggt, mxt, bmxt = c["ggt"], c["bggt"], c["mxt"], c["bmxt"]
                op("act", lambda e: e.copy(out=osum[:], in_=po[:]), reads=[bpo], writes=[bosum])
                op("dve", lambda e: e.tensor_tensor(out=osum[:], in0=osum[:], in1=oft[:, :, tc0:tc0 + 128], op=ALU.add), reads=[bosum, boft], writes=[bosum])
                op("act", lambda e: e.activation(out=osq[:], in_=osum[:], func=AF.Square), reads=[bosum], writes=[bosq])
                op("pe", lambda e: e.matmul(p_ss[:], lhsT=onesb[:], rhs=osq[:], start=True, stop=True), reads=[bosq, bonesb], writes=[bp_ss])
                op("act", lambda e: e.activation(out=orst[:], in_=p_ss[:], func=AF.Ln, scale=1.0 / 128, bias=EPS), reads=[bp_ss], writes=[borst])
                op("act", lambda e: e.activation(out=orst[:], in_=orst[:], func=AF.Exp, scale=-0.5), reads=[borst], writes=[borst])
                op("act", lambda e: e.activation(out=gsig[:], in_=ggt[:, :, tc0:tc0 + 128], func=AF.Exp, scale=-1.0), reads=[bggt], writes=[bgsig])
                op("dve", lambda e: e.tensor_scalar(out=gsig[:], in0=gsig[:], scalar1=1.0, scalar2=None, op0=ALU.add), reads=[bgsig], writes=[bgsig])
                op("dve", lambda e: e.reciprocal(out=gsig[:], in_=gsig[:]), reads=[bgsig], writes=[bgsig])
                op("dve", lambda e: e.tensor_tensor(out=gsig[:], in0=gsig[:], in1=ggt[:, :, tc0:tc0 + 128], op=ALU.mult), reads=[bgsig, bggt], writes=[bgsig])
                op("dve", lambda e: e.scalar_tensor_tensor(out=osum[:], in0=osum[:], scalar=nrm_sb[:, 4:5], in1=orst[:], op0=ALU.mult, op1=ALU.mult), reads=[bosum, borst, bnrm], writes=[bosum])
                op("dve", lambda e: e.tensor_tensor(out=mxt[:, :, tc0:tc0 + 128], in0=osum[:], in1=gsig[:], op=ALU.mult), reads=[bosum, bgsig], writes=[bmxt])
            if ch["cj"] == 3:
                if dr == 0:
                    dma("pool", OF[:, t0:t0 + 512].rearrange("(h p) t -> p h t", p=128), oft[:], boft, reads=[boft], writes=[bOF])
                else:
                    dma("pool", MIXT[0:512, t0:t0 + 512].rearrange("(h p) t -> p h t", p=128), mxt[:], bmxt, reads=[bmxt], writes=[bMIXT])

        for i in range(len(chunks) + 1):
            if i < len(chunks):
                stage1(chunks[i])
            if i >= 1:
                stage2(chunks[i - 1])
        P.flush()
    Gp.close()
    if stop == "G":
        return fin()

    P.flush()
    P.defer = False
    W2a = Phase(P)
    wu_b, bwu_b = W2a.sb("wu_b", [128, 8, 2 * DFF], BF16)
    lnf, blnf = W2a.sb("lnf", [128, 8], F32)
    dma("sp", lnf[:], ln_ffn[:, :], blnf, writes=[blnf])
    W2b = Phase(P)
    stg2 = [W2b.sb("stg2_%d" % i, [128, 1408], F32) for i in range(2)]

    def load_wu_chunk(k):
        c, q4 = k // 4, k % 4
        st, bst = rr(stg2, k)
        dma("pool", st[:], w_up[c * 128:(c + 1) * 128, q4 * 1408:(q4 + 1) * 1408], bst, writes=[bst])
        op("dve", lambda e, st=st, c=c, q4=q4: e.tensor_scalar(out=wu_b[:, c, q4 * 1408:(q4 + 1) * 1408], in0=st[:], scalar1=lnf[:, c:c + 1], scalar2=None, op0=ALU.mult),
           reads=[bst, blnf], writes=[bwu_b])
    B = Phase(P)
    P.defer = "B" in SCHED
    KT, bKT = B.sb("KT", [128, L], BF16)
    VA, bVA = B.sb("VA", [128, NT, 130], BF16)
    s65, bs65 = B.sb("s65", [65, 64], F32)
    dma("sp", s65[:], sel65[:, :], bs65, writes=[bs65])
    dma("sp", KT[:], AKT[:, :], bKT, reads=[bAKT], writes=[bKT])
    for c0 in range(0, NT, 8):
        c1 = min(NT, c0 + 8)
        dma("sp", VA[:, c0:c1, :], AV[c0 * 128:c1 * 128, :].rearrange("(n p) c -> p n c", p=128), bVA, reads=[bAV], writes=[bVA])
    QTg = [[B.sb("QT%d_%d" % (g_, i), [128, 512], BF16) for i in range(4)] for g_ in range(2)]
    for g_ in range(2):
        for t_, b_ in QTg[g_]:
            op("dve", lambda e, t_=t_: e.memset(t_[:], 0.0), writes=[b_])
    pst = [B.ps("pst%d" % i, [128, 3, 512], F32) for i in range(2)]
    pTs = [B.sb("pT%d" % i, [128, 3, 512], BF16) for i in range(3)]
    oaccs = [B.ps("oacc%d" % i, [128, 512], F32) for i in range(2)]
    osb = [B.sb("osb%d" % i, [65, 512], F32) for i in range(2)]
    rcp, brcp = B.sb("rcp", [64, 512], F32)
    oat = [B.sb("oat%d" % i, [64, 512], BF16) for i in range(2)]
    groups = []
    k0 = 0
    while k0 < NT:
        n = min(3, NT - k0)
        groups.append((k0, n))
        k0 += n
    items = []
    it = 0
    for (t0, nq) in qblocks:
        for h in range(8):
            for gi_, (k0, n) in enumerate(groups):
                items.append((it, h, t0, nq, gi_, k0, n))
            it += 1
    state = {}

    first_idx = {}
    for idx_, itm in enumerate(items):
        first_idx.setdefault(itm[0], idx_)
    n_items = it

    def load_q(it_):
        if it_ >= n_items or it_ in state:
            return
        _, h, t0, nq, _, _, _ = items[first_idx[it_]]
        g = h // 4
        qt, bqt = rr(QTg[g], it_)
        dma("sp", qt[64 * g:64 * g + 64, 0:nq], AQT[64 * h:64 * h + 64, t0:t0 + nq], bqt, reads=[bAQT], writes=[bqt])
        state[it_] = (qt, bqt)

    def emit_s(idx):
        it_, h, t0, nq, gi_, k0, n = items[idx]
        g = h // 4
        if gi_ == 0:
            load_q(it_)
            load_q(it_ + 1)
            if it_ < 32:
                load_wu_chunk(it_)
        qt, bqt = state[it_]
        st, bst = rr(pst, idx)
        pT, bpT = rr(pTs, idx)

        def fs(e):
            ins = None
            for j in range(n):
                ins = e.matmul(st[:, j, 0:nq], lhsT=KT[:, (k0 + j) * 128:(k0 + j + 1) * 128], rhs=qt[:, 0:nq], start=True, stop=True)
            return ins
        op("pe", fs, reads=[bKT, bqt], writes=[bst])
        op("act", lambda e: e.activation(out=pT[:, 0:n, 0:nq], in_=st[:, 0:n, 0:nq], func=AF.Exp, scale=0.125), reads=[bst], writes=[bpT])

    def emit_pv(idx):
        it_, h, t0, nq, gi_, k0, n = items[idx]
        g = h // 4
        pT, bpT = rr(pTs, idx)
        oacc, boacc = rr(oaccs, it_)
        first = (gi_ == 0)
        last = (gi_ == len(groups) - 1)

        def fpv(e):
            ins = None
            for j in range(n):
                ins = e.matmul(oacc[0:65, 0:nq], lhsT=VA[:, k0 + j, 65 * g:65 * g + 65], rhs=pT[:, j, 0:nq], start=(first and j == 0), stop=(last and j == n - 1))
            return ins
        op("pe", fpv, reads=[bVA, bpT], writes=[boacc])
        if last:
            ob, bob = rr(osb, it_)
            oa, boa = rr(oat, it_)
            op("dve", lambda e: e.tensor_copy(out=ob[:, 0:nq], in_=oacc[0:65, 0:nq]), reads=[boacc], writes=[bob])
            op("pe", lambda e: e.matmul(oacc[0:64, 0:nq], lhsT=s65[:], rhs=ob[:, 0:nq], start=True, stop=True), reads=[bob, bs65], writes=[boacc])
            op("dve", lambda e: e.reciprocal(out=rcp[:, 0:nq], in_=oacc[0:64, 0:nq]), reads=[boacc], writes=[brcp])
            op("dve", lambda e: e.tensor_tensor(out=oa[:, 0:nq], in0=ob[0:64, 0:nq], in1=rcp[:, 0:nq], op=ALU.mult), reads=[bob, brcp], writes=[boa])
            dma("pool", MIXT[512 + 64 * h:512 + 64 * h + 64, t0:t0 + nq], oa[:, 0:nq], boa, reads=[boa], writes=[bMIXT])

    for idx in range(len(items) + 1):
        if idx < len(items):
            emit_s(idx)
        if idx >= 1:
            emit_pv(idx - 1)
    for k_ in range(min(32, n_items), 32):
        load_wu_chunk(k_)
    B.close()
    W2b.close()
    wd_b, bwd_b = W2a.sb("wd_b", [128, 22, D], BF16)
    if stop == "B":
        W2a.close()
        return fin()

    C1 = Phase(P)
    P.defer = "C1" in SCHED
    wo_b, bwo_b = C1.sb("wo_b", [128, 8, D], BF16)
    stg1 = [C1.sb("stg1_%d" % i, [128, D], F32) for i in range(2)]
    for c in range(8):
        st, bst = rr(stg1, c)
        dma("sp", st[:], w_out[c * 128:(c + 1) * 128, :], bst, writes=[bst])
        op("dve", lambda e, st=st, c=c: e.tensor_copy(out=wo_b[:, c, :], in_=st[:]), reads=[bst], writes=[bwo_b])
    for f in range(22):
        st, bst = rr(stg1, f)
        dma("sp", st[:], w_down[f * 128:(f + 1) * 128, :], bst, writes=[bst])
        op("dve", lambda e, st=st, f=f: e.tensor_copy(out=wd_b[:, f, :], in_=st[:]), reads=[bst], writes=[bwd_b])
    zc, bzc = C1.sb("zc", [128, 8, 1], BF16)
    op("dve", lambda e: e.memset(zc[:], 0.0), writes=[bzc])
    dma("pool", H2T[:, 0:1].rearrange("(c p) t -> p c t", p=128), zc[:], bzc, reads=[bzc], writes=[bH2T], allow_slow_non_contiguous=True)
    dma("pool", H2T[:, L + 1:L + 2].rearrange("(c p) t -> p c t", p=128), zc[:], bzc, reads=[bzc], writes=[bH2T], allow_slow_non_contiguous=True)
    mixs = [C1.sb("mix%d" % i, [128, 8, 128], BF16) for i in range(3)]
    xs = [C1.sb("xs%d" % i, [128, D], F32) for i in range(3)]
    x1s = [C1.sb("x1s%d" % i, [128, D], F32) for i in range(2)]
    junk1, bjunk1 = C1.sb("junk1", [128, D], BF16)
    ssq1 = [C1.sb("ssq1_%d" % i, [128, 1], F32) for i in range(2)]
    h2s = [C1.sb("h2s%d" % i, [128, D], BF16) for i in range(2)]
    h2Ts = [C1.sb("h2T%d" % i, [128, 8, 128], BF16) for i in range(2)]
    pc1 = [C1.ps("pc1_%d" % i, [128, 2, 512], F32) for i in range(2)]
    ptr1 = [C1.ps("ptr1_%d" % i, [128, 8, 128], BF16) for i in range(2)]
    def c1_tile(tile, m):
        r0 = tile * 128
        mx, bmx = rr(mixs, tile)
        xt, bxt = rr(xs, tile)
        x1, bx1 = rr(x1s, tile)
        sq, bsq = rr(ssq1, tile)
        h2, bh2 = rr(h2s, tile)
        h2T, bh2T = rr(h2Ts, tile)
        pc, bpc = rr(pc1, tile)
        pt, bpt = rr(ptr1, tile)
        dma("sp", mx[:, :, 0:m], MIXT[:, r0:r0 + m].rearrange("(c p) t -> p c t", p=128), bmx, reads=[bMIXT], writes=[bmx])
        dma("sp", xt[0:m, :], x[r0:r0 + m, :], bxt, writes=[bxt])

        def fo(e):
            ins = None
            for hf in range(2):
                for c in range(8):
                    ins = e.matmul(pc[0:m, hf, :], lhsT=mx[:, c, 0:m], rhs=wo_b[:, c, hf * 512:(hf + 1) * 512], start=(c == 0), stop=(c == 7))
            return ins
        op("pe", fo, reads=[bmx, bwo_b], writes=[bpc])
        op("act", lambda e: e.copy(out=x1[0:m, :], in_=pc[0:m].rearrange("p a b -> p (a b)")), reads=[bpc], writes=[bx1])
        op("dve", lambda e: e.tensor_tensor(out=x1[0:m, :], in0=x1[0:m, :], in1=xt[0:m, :], op=ALU.add), reads=[bx1, bxt], writes=[bx1])
        dma("pool", X1[r0:r0 + m, :], x1[0:m, :], bx1, reads=[bx1], writes=[bX1])
        op("act", lambda e: e.activation(out=junk1[0:m, :], in_=x1[0:m, :], func=AF.Square, accum_out=sq[0:m, :]), reads=[bx1], writes=[bjunk1, bsq])
        rstd_from_ssq(None, sq[0:m, :], bsq, D)
        op("dve", lambda e: e.tensor_scalar(out=h2[0:m, :], in0=x1[0:m, :], scalar1=sq[0:m, 0:1], scalar2=None, op0=ALU.mult), reads=[bx1, bsq], writes=[bh2])

        def ftr1(e):
            ins = None
            for c in range(8):
                ins = e.transpose(out=pt[:, c, 0:m], in_=h2[0:m, c * 128:(c + 1) * 128], identity=cstb[0:m, 0:m])
            return ins
        op("pe", ftr1, reads=[bh2, bcstb], writes=[bpt])
        op("act", lambda e: e.copy(out=h2T[:, :, 0:m], in_=pt[:, :, 0:m]), reads=[bpt], writes=[bh2T])
        dma("pool", H2T[:, 1 + r0:1 + r0 + m].rearrange("(c p) t -> p c t", p=128), h2T[:, :, 0:m], bh2T, reads=[bh2T], writes=[bH2T])

    for tile in range(C1T):
        c1_tile(tile, EXTQ if (H is not None and tile == C1T - 1) else 128)
    C1.close()
    if stop == "C1":
        W2a.close()
        return fin()

    C2 = Phase(P)
    P.defer = "C2" in SCHED
    lfb, blfb = C2.sb("lfb", [128, D], F32)
    dma("sp", lfb[:], lnf_bc[:, :], blfb, writes=[blfb])
    cvp, bcvp = C2.sb("cvp", [128, 44, 4], F32)
    dma("sp", cvp[:], convp[:, :, :], bcvp, writes=[bcvp])
    TBW = 254
    hws = [C2.sb("hw%d" % i, [128, 8, 256], BF16) for i in range(2)]
    acs = [C2.sb("acs%d" % i, [128, 256], BF16) for i in range(6)]
    pu = [C2.ps("pu%d" % i, [128, 512], F32) for i in range(4)]
    pyt, bpyt = C2.ps("py", [128, 2, 2, 512], F32)
    tas = [C2.sb("ta%d" % i, [128, 256], F32) for i in range(2)]
    tgs = [C2.sb("tg%d" % i, [128, 256], F32) for i in range(2)]
    sgs = [C2.sb("sg%d" % i, [128, 256], F32) for i in range(2)]
    usb = [C2.sb("usb%d" % i, [128, 256], F32) for i in range(4)]
    x1l = [C2.sb("x1l%d" % i, [128, D], F32) for i in range(2)]
    yo = [C2.sb("yo%d" % i, [128, D], F32) for i in range(2)]
    junk2, bjunk2 = C2.sb("junk2", [128, D], BF16)
    ssq2 = [C2.sb("ssq2_%d" % i, [128, 1], F32) for i in range(2)]
    blocks = []
    t = 0
    while t < HO:
        n = min(TBW, HO - t)
        blocks.append((t, n))
        t += n
    nonlocal_puc = [0]
    slc = 0
    slc_ = [0]

    def do_block(bi, t0, n):
            hw, bhw = rr(hws, bi)
            dma("sp", hw[:, :, 0:n + 2], H2T[:, t0:t0 + n + 2].rearrange("(c p) t -> p c t", p=128), bhw, reads=[bH2T], writes=[bhw])
            stash = {}
            slices = []
            s0_ = 0
            while s0_ < n:
                m_ = min(128, n - s0_)
                slices.append((s0_, m_))
                s0_ += m_

            def stage1(f):
                res = []
                for part in range(2):
                    nonlocal_puc[0] += 1
                    pc_ = nonlocal_puc[0]
                    col0 = part * DFF + f * 128
                    pp, bpp = rr(pu, pc_)

                    def fu(e, pp=pp, col0=col0):
                        ins = None
                        for c in range(8):
                            ins = e.matmul(pp[:, 0:n + 2], lhsT=wu_b[:, c, col0:col0 + 128], rhs=hw[:, c, 0:n + 2], start=(c == 0), stop=(c == 7))
                        return ins
                    op("pe", fu, reads=[bwu_b, bhw], writes=[bpp])
                    tt, btt = rr(tas if part == 0 else tgs, f)
                    fc = part * 22 + f
                    us, bus = rr(usb, pc_)
                    op("act", lambda e, us=us, pp=pp: e.copy(out=us[:, 0:n + 2], in_=pp[:, 0:n + 2]), reads=[bpp], writes=[bus])
                    op("dve", lambda e, tt=tt, us=us, fc=fc: e.tensor_scalar(out=tt[:, 0:n], in0=us[:, 1:n + 1], scalar1=cvp[:, fc, 1:2], scalar2=cvp[:, fc, 3:4], op0=ALU.mult, op1=ALU.add),
                       reads=[bus, bcvp], writes=[btt])
                    op("dve", lambda e, tt=tt, us=us, fc=fc: e.scalar_tensor_tensor(out=tt[:, 0:n], in0=us[:, 0:n], scalar=cvp[:, fc, 0:1], in1=tt[:, 0:n], op0=ALU.mult, op1=ALU.add),
                       reads=[bus, bcvp, btt], writes=[btt])
                    op("dve", lambda e, tt=tt, us=us, fc=fc: e.scalar_tensor_tensor(out=tt[:, 0:n], in0=us[:, 2:n + 2], scalar=cvp[:, fc, 2:3], in1=tt[:, 0:n], op0=ALU.mult, op1=ALU.add),
                       reads=[bus, bcvp, btt], writes=[btt])
                    res.append((tt, btt))
                stash[f] = res

            def stage2(f):
                (ta, bta), (tg, btg) = stash.pop(f)
                sg, bsg = rr(sgs, f)
                op("act", lambda e: e.activation(out=sg[:, 0:n], in_=tg[:, 0:n], func=AF.Exp, scale=-1.0), reads=[btg], writes=[bsg])
                op("act", lambda e: e.activation(out=sg[:, 0:n], in_=sg[:, 0:n], func=AF.Ln, bias=1.0), reads=[bsg], writes=[bsg])
                op("act", lambda e: e.activation(out=sg[:, 0:n], in_=sg[:, 0:n], func=AF.Exp, scale=-1.0), reads=[bsg], writes=[bsg])
                op("pool", lambda e: e.tensor_tensor(out=tg[:, 0:n], in0=tg[:, 0:n], in1=sg[:, 0:n], op=ALU.mult), reads=[bsg, btg], writes=[btg])
                ac, bac = rr(acs, f)
                op("pool", lambda e: e.tensor_tensor(out=ac[:, 0:n], in0=ta[:, 0:n], in1=tg[:, 0:n], op=ALU.mult), reads=[bta, btg], writes=[bac])

            def stage3(f):
                ac, bac = rr(acs, f)

                def fd(e):
                    ins = None
                    for si, (s0, m) in enumerate(slices):
                        for hf in range(2):
                            ins = e.matmul(pyt[0:m, si, hf, :], lhsT=ac[:, s0:s0 + m], rhs=wd_b[:, f, hf * 512:(hf + 1) * 512], start=(f == 0), stop=(f == 21))
                    return ins
                op("pe", fd, reads=[bac, bwd_b], writes=[bpyt])

            for f in range(22 + 3):
                if f < 22:
                    stage1(f)
                if 0 <= f - 1 < 22:
                    stage2(f - 1)
                if 0 <= f - 3 < 22:
                    stage3(f - 3)
            for si, (s0, m) in enumerate(slices):
                r0 = t0 + s0
                xl, bxl = rr(x1l, slc_[0])
                yt_, byt = rr(yo, slc_[0])
                sq, bsq = rr(ssq2, slc_[0])
                slc_[0] += 1
                dma("sp", xl[0:m, :], X1[r0:r0 + m, :], bxl, reads=[bX1], writes=[bxl])
                op("act", lambda e, yt_=yt_, m=m, si=si: e.copy(out=yt_[0:m, :], in_=pyt[0:m, si].rearrange("p a b -> p (a b)")), reads=[bpyt], writes=[byt])
                op("dve", lambda e, yt_=yt_, xl=xl, m=m: e.tensor_tensor(out=yt_[0:m, :], in0=yt_[0:m, :], in1=xl[0:m, :], op=ALU.add),
                   reads=[byt, bxl], writes=[byt])
                op("act", lambda e, yt_=yt_, sq=sq, m=m: e.activation(out=junk2[0:m, :], in_=yt_[0:m, :], func=AF.Square, accum_out=sq[0:m, :]), reads=[byt], writes=[bjunk2, bsq])
                rstd_from_ssq(None, sq[0:m, :], bsq, D)
                op("dve", lambda e, yt_=yt_, sq=sq, m=m: e.scalar_tensor_tensor(out=yt_[0:m, :], in0=yt_[0:m, :], scalar=sq[0:m, 0:1], in1=lfb[0:m, :], op0=ALU.mult, op1=ALU.mult),
                   reads=[byt, bsq, blfb], writes=[byt])
                dma("pool", y[r0:r0 + m, :], yt_[0:m, :], byt, reads=[byt], writes=[by])

    for bi, (t0, n) in enumerate(blocks):
        do_block(bi, t0, n)
    C2.close()
    W2a.close()
    return fin()


def host_consts(L):
    i = np.arange(128)
    s = i[:, None]
    c = i[None, :]
    ident = (s == c)
    UI_f = (s <= c)
    UI_b = (s >= c)
    LS_f = (s > c)
    LS_b = (s < c)
    BD = ((s // 64) == (c // 64))
    consts = np.concatenate([m.astype(np.float32) for m in (ident, UI_f, UI_b, LS_f, LS_b, BD)], axis=1)
    sel65 = np.zeros((65, 64), np.float32)
    sel65[64, :] = 1.0
    half = 32
    inv = (1.0 / (10000.0 ** (np.arange(0, half, 2, dtype=np.float32) / half))).astype(np.float32)
    t = np.arange(L)
    row = (t // 64).astype(np.float32)
    col = (t % 64).astype(np.float32)
    ang_r = row[:, None] * inv[None, :]
    ang_c = col[:, None] * inv[None, :]
    cosT = np.zeros((64, L), np.float32)
    sinT = np.zeros((64, L), np.float32)
    for d in range(64):
        hf = d // 32
        j = d % 32
        ang = (ang_r if hf == 0 else ang_c)[:, j % 16]
        cosT[d] = np.cos(ang)
        sinT[d] = np.sin(ang) * (-1.0 if j < 16 else 1.0)
    cosT = np.ascontiguousarray(np.tile(cosT, (2, 1)))
    sinT = np.ascontiguousarray(np.tile(sinT, (2, 1)))
    return consts, sel65, cosT, sinT


def perm64():
    p = np.zeros(64, np.int64)
    for d in range(64):
        j = d % 32
        p[d] = d + 16 if j < 16 else d - 16
    return p


def host_inputs(xs, ln_mix, w_in, w_gk_fwd, b_gk_fwd, w_gk_bwd, b_gk_bwd, gla_out_norm, q_norm, k_norm,
                w_out, ln_ffn, w_up, conv_w, conv_b, w_down, ln_final, L, revs=None):
    f = np.float32
    if revs is None:
        revs = [False] * len(xs)
    consts, sel65, cosT, sinT = host_consts(L)
    w_in0 = np.ascontiguousarray(np.asarray(w_in[0], f))
    pm = perm64()
    cols = []
    for h in range(8):
        cols.append(1568 + h * 64 + pm)
    for h in range(2):
        cols.append(2080 + h * 64 + pm)
    cols = np.concatenate(cols)
    w_rot = np.ascontiguousarray(w_in0[:, cols])
    nrm = np.zeros((128, 8), f)
    qn = np.asarray(q_norm[0], f)
    kn = np.asarray(k_norm[0], f)
    nrm[:, 0] = np.tile(qn, 2)
    nrm[:, 1] = np.tile(qn[pm], 2)
    nrm[:, 2] = np.tile(kn, 2)
    nrm[:, 3] = np.tile(kn[pm], 2)
    nrm[:, 4] = np.asarray(gla_out_norm[0], f)
    cw = np.asarray(conv_w[0], f)
    cb = np.asarray(conv_b[0], f)
    variants = {}
    for rv in (False, True):
        wg = np.zeros((33, 512), f)
        fs, bs = (slice(0, 256), slice(256, 512)) if not rv else (slice(256, 512), slice(0, 256))
        wg[0:16, fs] = w_gk_fwd[0]
        wg[16:32, bs] = w_gk_bwd[0]
        wg[32, fs] = b_gk_fwd[0]
        wg[32, bs] = b_gk_bwd[0]
        convp = np.zeros((128, 44, 4), f)
        for k in range(3):
            kk = k if not rv else 2 - k
            convp[:, :, kk] = cw[k].reshape(44, 128).T
        convp[:, :, 3] = cb.reshape(44, 128).T
        ct = cosT if not rv else np.ascontiguousarray(cosT[:, ::-1])
        sn = sinT if not rv else np.ascontiguousarray(sinT[:, ::-1])
        variants[rv] = {"wg_aug": wg, "convp": convp, "cosT": ct, "sinT": sn}
    common = {
        "w_in": w_in0, "w_rot": w_rot,
        "ln_mix": np.ascontiguousarray(np.asarray(ln_mix[0], f).reshape(8, 128).T),
        "ln_ffn": np.ascontiguousarray(np.asarray(ln_ffn[0], f).reshape(8, 128).T),
        "lnf_bc": np.ascontiguousarray(np.broadcast_to(np.asarray(ln_final, f)[None, :], (128, D))),
        "nrm": nrm, "consts": consts, "sel65": sel65,
        "w_out": np.ascontiguousarray(np.asarray(w_out[0], f)),
        "w_up": np.ascontiguousarray(np.asarray(w_up[0], f)),
        "w_down": np.ascontiguousarray(np.asarray(w_down[0], f)),
    }
    maps = []
    for xx, rv in zip(xs, revs):
        m = dict(common)
        m.update(variants[bool(rv)])
        xx = np.asarray(xx, f)
        m["x"] = np.ascontiguousarray(xx[::-1] if rv else xx)
        maps.append(m)
    return maps


def run_sequences(seqs, W, L, n_cores=8):
    H = L // 2
    zero = np.zeros((L, D), np.float32)
    xs = [zero] * n_cores
    revs = [False] * n_cores
    for k, sq in enumerate(seqs):
        xs[2 * k] = sq
        xs[2 * k + 1] = sq
        revs[2 * k + 1] = True
    maps = host_inputs(xs, W["ln_mix"], W["w_in"], W["w_gk_fwd"], W["b_gk_fwd"], W["w_gk_bwd"], W["b_gk_bwd"],
                       W["gla_out_norm"], W["q_norm"], W["k_norm"], W["w_out"], W["ln_ffn"], W["w_up"],
                       W["conv_w"], W["conv_b"], W["w_down"], W["ln_final"], L, revs)
    nc, _ = build_program(L, H=H)
    res = run_bass_kernel_spmd(nc, maps, core_ids=list(range(n_cores)))
    outs = []
    for k in range(len(seqs)):
        a = np.asarray(res.results[2 * k]["y"], np.float32)
        b = np.asarray(res.results[2 * k + 1]["y"], np.float32)
        outs.append(np.concatenate([a, b[::-1]], axis=0))
    return outs


def kernel(x_prompt, x_sample, ln_mix, w_in, w_gk_fwd, b_gk_fwd, w_gk_bwd, b_gk_bwd, gla_out_norm,
           q_norm, k_norm, w_out, ln_ffn, w_up, conv_w, conv_b, w_down, ln_final):
    x_prompt = np.asarray(x_prompt)
    x_sample = np.asarray(x_sample)
    L = x_prompt.shape[1]
    seqs = [x_prompt[0], x_prompt[1], x_sample[0]]
    W = dict(ln_mix=np.asarray(ln_mix), w_in=np.asarray(w_in), w_gk_fwd=np.asarray(w_gk_fwd), b_gk_fwd=np.asarray(b_gk_fwd),
             w_gk_bwd=np.asarray(w_gk_bwd), b_gk_bwd=np.asarray(b_gk_bwd), gla_out_norm=np.asarray(gla_out_norm),
             q_norm=np.asarray(q_norm), k_norm=np.asarray(k_norm), w_out=np.asarray(w_out), ln_ffn=np.asarray(ln_ffn),
             w_up=np.asarray(w_up), conv_w=np.asarray(conv_w), conv_b=np.asarray(conv_b), w_down=np.asarray(w_down),
             ln_final=np.asarray(ln_final))
    outs = run_sequences(seqs, W, L, 8)
    y_prompt = np.stack([outs[0], outs[1]], axis=0)
    y_sample = outs[2][None]
    return (y_prompt, y_sample)
```

```python
import numpy as np
from contextlib import ExitStack
import concourse.bass as bass
import concourse.mybir as mybir
from concourse.bass_utils import run_bass_kernel_spmd

F32 = mybir.dt.float32
BF16 = mybir.dt.bfloat16
AF = mybir.ActivationFunctionType
ALU = mybir.AluOpType
AX = mybir.AxisListType

ENGS = ("pe", "act", "dve", "pool", "sp")
D = 1024
INW = 2336
DFF = 2816
EPS = 1e-6


class Buf:
    def __init__(self, name):
        self.name = name
        self.w = {}
        self.r = {}
        self.dsem = None
        self.dcnt = 0


class Prog:
    def __init__(self, nc):
        self.nc = nc
        self.es = ExitStack()
        self.sems = {}
        self.cnt = {e: 0 for e in ENGS}
        self.waited = {e: {} for e in ENGS}
        self.E = {"pe": nc.tensor, "act": nc.scalar, "dve": nc.vector, "pool": nc.gpsimd, "sp": nc.sync}
        for e in ENGS:
            self.sems[e] = self.es.enter_context(nc.semaphore("s_" + e))
        self.nins = 0
        self.allbufs = []
        self.defer = False
        self.pending = []
        self.EST = {"pe": 1.0, "act": 0.7, "dve": 0.5, "pool": 1.0, "sp": 0.1}

    def dram(self, name, shape, dt, kind="Internal"):
        return self.nc.dram_tensor(name, list(shape), dt, kind=kind).ap()

    def buf_sem(self, b, eng):
        if b.dsem is None:
            b.dsem = {}
            b.dcnt = {}
            self.allbufs.append(b)
        if eng not in b.dsem:
            sm = self.es.enter_context(self.nc.semaphore("d_%s_%s" % (b.name, eng)))
            b.dsem[eng] = sm
            b.dcnt[eng] = 0
            self.sems[sm] = sm
        return b.dsem[eng]

    def _collect(self, eng, reads, writes):
        need = {}
        for b in reads:
            for k, v in b.w.items():
                if need.get(k, 0) < v:
                    need[k] = v
        for b in writes:
            for k, v in b.w.items():
                if need.get(k, 0) < v:
                    need[k] = v
            for k, v in b.r.items():
                if need.get(k, 0) < v:
                    need[k] = v
        out = []
        wd = self.waited[eng]
        for k, v in need.items():
            if k == eng and eng == "pe":
                continue
            if wd.get(k, 0) >= v:
                continue
            wd[k] = v
            out.append((k, v))
        return out

    def _emit(self, eng, waits, fn, inc):
        e = self.E[eng]
        for k, v in waits:
            e.wait_ge(self.sems[k], v)
        if fn is None:
            return
        ins = fn(e)
        self.nins += 1
        if inc is not None:
            ins.then_inc(self.sems[inc[0]], inc[1])

    def op(self, eng, fn, reads=(), writes=(), est=None):
        if self.defer:
            self.pending.append(("op", eng, fn, tuple(reads), tuple(writes), est if est is not None else self.EST[eng], None, None))
            return
        waits = self._collect(eng, reads, writes)
        self.cnt[eng] += 1
        tok = (eng, self.cnt[eng])
        self._emit(eng, waits, fn, (eng, 1))
        for b in reads:
            b.r[tok[0]] = tok[1]
        for b in writes:
            b.w[tok[0]] = tok[1]

    def dma(self, eng, out, in_, own, reads=(), writes=(), **kw):
        if self.defer:
            self.pending.append(("dma", eng, (out, in_, kw), tuple(reads), tuple(writes), 3.0, own, None))
            return
        waits = self._collect(eng, reads, writes)
        sem = self.buf_sem(own, eng)
        own.dcnt[eng] += 16
        tok = (sem, own.dcnt[eng])
        self._emit(eng, waits, lambda e: e.dma_start(out=out, in_=in_, **kw), (sem, 16))
        for b in reads:
            b.r[tok[0]] = tok[1]
        for b in writes:
            b.w[tok[0]] = tok[1]

    def flush(self):
        ops = self.pending
        self.pending = []
        if not ops:
            return
        import heapq
        n = len(ops)
        preds = [set() for _ in range(n)]
        lastw = {}
        readers = {}
        for i, o in enumerate(ops):
            for b in o[3]:
                if b in lastw:
                    preds[i].add(lastw[b])
            for b in o[4]:
                if b in lastw:
                    preds[i].add(lastw[b])
                for r in readers.get(b, ()):
                    preds[i].add(r)
            for b in o[3]:
                readers.setdefault(b, []).append(i)
            for b in o[4]:
                lastw[b] = i
                readers[b] = []
            preds[i].discard(i)
        succs = [[] for _ in range(n)]
        npred = [0] * n
        for i in range(n):
            npred[i] = len(preds[i])
            for p in preds[i]:
                succs[p].append(i)
        ready_t = [0.0] * n
        finish = [0.0] * n
        heaps = {e: [] for e in ENGS}
        for i in range(n):
            if npred[i] == 0:
                heapq.heappush(heaps[ops[i][1]], (0.0, i))
        efree = {e: 0.0 for e in ENGS}
        order = []
        done = 0
        while done < n:
            best = None
            for e in ENGS:
                h = heaps[e]
                if not h:
                    continue
                rt, i = h[0]
                st = max(rt, efree[e])
                if best is None or (st, i) < (best[0], best[2]):
                    best = (st, e, i)
            st, e, i = best
            heapq.heappop(heaps[e])
            o = ops[i]
            if o[0] == "dma":
                efree[e] = st + 0.1
                finish[i] = st + o[5]
            else:
                efree[e] = st + o[5]
                finish[i] = st + o[5] + 0.15
            order.append(i)
            done += 1
            for sidx in succs[i]:
                npred[sidx] -= 1
                if finish[i] > ready_t[sidx]:
                    ready_t[sidx] = finish[i]
                if npred[sidx] == 0:
                    heapq.heappush(heaps[ops[sidx][1]], (ready_t[sidx], sidx))
        prev = self.defer
        self.defer = False
        for i in order:
            o = ops[i]
            if o[0] == "dma":
                out, in_, kw = o[2]
                self.dma(o[1], out, in_, o[6], reads=o[3], writes=o[4], **kw)
            else:
                self.op(o[1], o[2], reads=o[3], writes=o[4])
        self.defer = prev

    def barrier(self):
        self.flush()
        b = Buf("barrier")
        for e in ENGS:
            if self.cnt[e] > 0:
                b.w[e] = self.cnt[e]
        for k, v in self.dsem_counts().items():
            b.w[k] = v
        for e in ENGS:
            self.wait_all(e, [b])

    def dsem_counts(self):
        out = {}
        for bb in self.allbufs:
            for eng, sm in bb.dsem.items():
                out[sm] = bb.dcnt[eng]
        return out

    def wait_all(self, eng, bufs):
        self.flush()
        waits = self._collect(eng, bufs, ())
        self._emit(eng, waits, None, None)


class Phase:
    def __init__(self, P):
        self.P = P
        self.es = ExitStack()

    def sb(self, name, shape, dt):
        t = self.es.enter_context(self.P.nc.sbuf_tensor(name, list(shape), dt))
        return t, Buf(name)

    def ps(self, name, shape, dt=F32):
        t = self.es.enter_context(self.P.nc.psum_tensor(name, list(shape), dt))
        return t, Buf(name)

    def close(self):
        self.P.barrier()
        self.es.close()


def rr(lst, i):
    return lst[i % len(lst)]


def build_program(L, dbg=False, stop=None, H=None):
    assert L % 512 == 0
    EXTQ = 16
    NT = L // 128
    NB = L // 512
    if H is None:
        HO = L
        QB = NB
        qblocks = [(qb * 512, 512) for qb in range(NB)]
        C1T = NT
    else:
        assert H % 512 == 0 and H < L
        HO = H
        QB = H // 512 + 1
        qblocks = [(qb * 512, 512) for qb in range(H // 512)] + [(H, EXTQ)]
        C1T = H // 128 + 1
    nc = bass.Bass("TRN2", target_bir_lowering=False)
    P = Prog(nc)
    op, dma = P.op, P.dma

    def ext(name, shape, dt=F32):
        return P.dram(name, shape, dt, kind="ExternalInput")

    x = ext("x", [L, D])
    w_in = ext("w_in", [D, INW])
    w_rot = ext("w_rot", [D, 640])
    ln_mix = ext("ln_mix", [128, 8])
    ln_ffn = ext("ln_ffn", [128, 8])
    lnf_bc = ext("lnf_bc", [128, D])
    wg_aug = ext("wg_aug", [33, 512])
    nrm = ext("nrm", [128, 8])
    cosT = ext("cosT", [128, L])
    sinT = ext("sinT", [128, L])
    consts = ext("consts", [128, 6 * 128])
    sel65 = ext("sel65", [65, 64])
    w_out = ext("w_out", [D, D])
    w_up = ext("w_up", [D, 2 * DFF])
    w_down = ext("w_down", [DFF, D])
    convp = ext("convp", [128, 44, 4])
    y = P.dram("y", [HO, D], F32, kind="ExternalOutput")
    by = Buf("y")

    kind = "ExternalOutput" if dbg else "Internal"
    GQT = P.dram("GQT", [256, L], F32, kind=kind); bGQT = Buf("GQT")
    GKT = P.dram("GKT", [256, L], F32, kind=kind); bGKT = Buf("GKT")
    GGT = P.dram("GGT", [512, L], F32, kind=kind); bGGT = Buf("GGT")
    GK = P.dram("GK", [L, 256], F32, kind=kind); bGK = Buf("GK")
    GV = P.dram("GV", [L, 512], BF16, kind=kind); bGV = Buf("GV")
    G = P.dram("G", [L, 512], F32, kind=kind); bG = Buf("G")
    AQT = P.dram("AQT", [512, L], BF16, kind=kind); bAQT = Buf("AQT")
    AKT = P.dram("AKT", [128, L], BF16, kind=kind); bAKT = Buf("AKT")
    AV = P.dram("AV", [L, 130], BF16, kind=kind); bAV = Buf("AV")
    OF = P.dram("OF", [512, L], F32, kind=kind); bOF = Buf("OF")
    MIXT = P.dram("MIXT", [1024, L], BF16, kind=kind); bMIXT = Buf("MIXT")
    X1 = P.dram("X1", [L, D], F32, kind=kind); bX1 = Buf("X1")
    H2T = P.dram("H2T", [1024, L + 2], BF16, kind=kind); bH2T = Buf("H2T")

    G0 = Phase(P)
    cst, bcst = G0.sb("cst", [128, 6 * 128], F32)
    cstb, bcstb = G0.sb("cstb", [128, 6 * 128], BF16)
    nrm_sb, bnrm = G0.sb("nrm_sb", [128, 8], F32)
    dma("sp", cst[:], consts[:, :], bcst, writes=[bcst])
    dma("sp", nrm_sb[:], nrm[:, :], bnrm, writes=[bnrm])
    op("dve", lambda e: e.tensor_copy(out=cstb[:], in_=cst[:]), reads=[bcst], writes=[bcstb])
    ident_b = cstb[:, 0:128]
    UIf = [cst[:, 128:256], cst[:, 256:384]]
    LSf = [cst[:, 384:512], cst[:, 512:640]]
    UIb = [cstb[:, 128:256], cstb[:, 256:384]]
    BDb = cstb[:, 640:768]

    def fin():
        P.wait_all("pool", [by, bGQT, bGKT, bGGT, bGK, bGV, bG, bAQT, bAKT, bAV, bOF, bMIXT, bX1, bH2T])
        G0.close()
        P.es.close()
        return nc, P

    def rstd_from_ssq(ph_bufs, ssq, bssq, n, shape_ap=None):
        op("act", lambda e: e.activation(out=ssq, in_=ssq, func=AF.Ln, scale=1.0 / n, bias=EPS), reads=[bssq], writes=[bssq])
        op("act", lambda e: e.activation(out=ssq, in_=ssq, func=AF.Exp, scale=-0.5), reads=[bssq], writes=[bssq])

    import os
    SCHED = os.environ.get("SCHED", "A,G,C1,C2").split(",")
    A = Phase(P)
    P.defer = "A" in SCHED
    WC = INW + 640
    wb, bwb = A.sb("wb", [128, 8, WC], BF16)
    lnm, blnm = A.sb("lnm", [128, 8], F32)
    dma("sp", lnm[:], ln_mix[:, :], blnm, writes=[blnm])
    stg = [A.sb("stg%d" % i, [128, WC], F32) for i in range(2)]
    for c in range(8):
        st, bst = rr(stg, c)
        dma("sp", st[:, 0:INW], w_in[c * 128:(c + 1) * 128, :], bst, writes=[bst])
        dma("sp", st[:, INW:WC], w_rot[c * 128:(c + 1) * 128, :], bst, writes=[bst])
        op("dve", lambda e, st=st, c=c: e.tensor_scalar(out=wb[:, c, :], in0=st[:], scalar1=lnm[:, c:c + 1], scalar2=None, op0=ALU.mult),
           reads=[bst, blnm], writes=[bwb])
    wgf, bwgf = A.sb("wgf", [33, 512], F32)
    wgb, bwgb = A.sb("wgb", [33, 512], BF16)
    dma("sp", wgf[:], wg_aug[:, :], bwgf, writes=[bwgf])
    op("dve", lambda e: e.tensor_copy(out=wgb[:], in_=wgf[:]), reads=[bwgf], writes=[bwgb])

    xts = [A.sb("xt%d" % i, [128, D], F32) for i in range(3)]
    junk, bjunk = A.sb("junk", [128, D], BF16)
    ssqs = [A.sb("ssq%d" % i, [128, 1], F32) for i in range(3)]
    xbs = [A.sb("xb%d" % i, [128, D], BF16) for i in range(2)]
    xnTs = [A.sb("xnT%d" % i, [128, 8, 512], BF16) for i in range(2)]
    raug, braug = A.sb("raug", [33, 512], BF16)
    op("dve", lambda e: e.memset(raug[:], 1.0), writes=[braug])
    ptr = [A.ps("ptr%d" % i, [128, 8, 128], BF16) for i in range(1)]
    pfm = [A.ps("pfm%d" % i, [128, 512], F32) for i in range(3)]
    pss, bpss = A.ps("pss", [128, 512], F32)
    ptm1, bptm1 = A.ps("ptm1", [128, 512], F32)
    ptm2, bptm2 = A.ps("ptm2", [128, 512], F32)
    pz, bpz = A.ps("pz", [128, 512], F32)
    fmo = [A.sb("fmo%d" % i, [128, 512], F32) for i in range(3)]
    cos_sb = [A.sb("cos%d" % i, [128, 512], F32) for i in range(2)]
    sin_sb = [A.sb("sin%d" % i, [128, 512], F32) for i in range(2)]
    sqb, bsqb = A.sb("sqb", [128, 512], BF16)
    rsd, brsd = A.sb("rsd", [128, 512], F32)
    t1s, bt1 = A.sb("t1s", [128, 512], F32)
    t2s, bt2 = A.sb("t2s", [128, 512], F32)
    qfo = [A.sb("qfo%d" % i, [128, 512], BF16) for i in range(2)]
    gko = [A.sb("gko%d" % i, [128, 256], F32) for i in range(2)]
    gvo = [A.sb("gvo%d" % i, [128, 512], BF16) for i in range(2)]
    avo = [A.sb("avo%d" % i, [128, 2, 65], BF16) for i in range(2)]
    for t, b in avo:
        op("dve", lambda e, t=t: e.memset(t[:], 1.0), writes=[b])
    ge, bge = A.sb("ge", [128, 512], F32)
    go = [A.sb("go%d" % i, [128, 512], F32) for i in range(2)]

    fmcount = [0]

    def fm_proj(col0, ncols, xnT, bxnT):
        pt, bpt = rr(pfm, fmcount[0])
        fmcount[0] += 1

        def f(e):
            ins = None
            for c in range(8):
                ins = e.matmul(pt[0:ncols, :], lhsT=wb[:, c, col0:col0 + ncols], rhs=xnT[:, c, :], start=(c == 0), stop=(c == 7))
            return ins
        op("pe", f, reads=[bwb, bxnT], writes=[bpt])
        return pt, bpt

    stc = [0]
    import os
    LVL = int(os.environ.get("PALVL", "9"))
    def stage_a1(blk):
        xnT, bxnT = rr(xnTs, blk)
        for ti in range(4):
            tile = blk * 4 + ti
            xt, bxt = rr(xts, tile)
            ssq, bssq = rr(ssqs, tile)
            xb, bxb = rr(xbs, tile)
            pt, bpt = rr(ptr, tile)
            dma("sp", xt[:], x[tile * 128:(tile + 1) * 128, :], bxt, writes=[bxt])
            op("act", lambda e, xt=xt, ssq=ssq: e.activation(out=junk[:], in_=xt[:], func=AF.Square, accum_out=ssq[:]),
               reads=[bxt], writes=[bjunk, bssq])
            rstd_from_ssq(None, ssq[:], bssq, D)
            op("dve", lambda e, xb=xb, xt=xt, ssq=ssq: e.tensor_scalar(out=xb[:], in0=xt[:], scalar1=ssq[:, 0:1], scalar2=None, op0=ALU.mult),
               reads=[bxt, bssq], writes=[bxb])

            def ftr(e, xb=xb, pt=pt):
                ins = None
                for c in range(8):
                    ins = e.transpose(out=pt[:, c, :], in_=xb[:, c * 128:(c + 1) * 128], identity=ident_b)
                return ins
            op("pe", ftr, reads=[bxb, bcstb], writes=[bpt])
            op("act", lambda e, xnT=xnT, pt=pt, ti=ti: e.copy(out=xnT[:, :, ti * 128:(ti + 1) * 128], in_=pt[:]),
               reads=[bpt], writes=[bxnT])
    def stage_proj(blk):
        xnT, bxnT = rr(xnTs, blk)
        t0 = blk * 512
        qside = blk < QB
        for (col0, dst, bdst, row0, scale) in [] if not qside else (
            [(0 + 128 * j, GQT, bGQT, 128 * j, 0.125) for j in range(2)]
            + [(256 + 128 * j, GKT, bGKT, 128 * j, 1.0) for j in range(2)]
            + [(1024 + 128 * j, GGT, bGGT, 128 * j, 1.0) for j in range(4)]
        ):
            pt, bpt = fm_proj(col0, 128, xnT, bxnT)
            so, bso = rr(fmo, stc[0]); stc[0] += 1
            op("act", lambda e, so=so, pt=pt, scale=scale: e.activation(out=so[:], in_=pt[:], func=AF.Copy, scale=scale),
               reads=[bpt], writes=[bso])
            dma("pool", dst[row0:row0 + 128, t0:t0 + 512], so[:], bso, reads=[bso], writes=[bdst])
        pt, bpt = fm_proj(1536, 32, xnT, bxnT)
        op("act", lambda e, pt=pt: e.copy(out=raug[0:32, :], in_=pt[0:32, :]), reads=[bpt], writes=[braug])
        cs, bcs = rr(cos_sb, blk)
        sn, bsn = rr(sin_sb, blk)
        dma("sp", cs[:], cosT[:, t0:t0 + 512], bcs, writes=[bcs])
        dma("sp", sn[:], sinT[:, t0:t0 + 512], bsn, writes=[bsn])
        for j in (range(5) if qside else [4]):
            col0 = 1568 + 128 * j
            colr = INW + 128 * j
            wcol = 0 if j < 4 else 2
            pa, bpa = fm_proj(col0, 128, xnT, bxnT)
            pr, bpr = fm_proj(colr, 128, xnT, bxnT)
            op("act", lambda e, pa=pa: e.activation(out=sqb[:], in_=pa[:], func=AF.Square), reads=[bpa], writes=[bsqb])
            op("pe", lambda e: e.matmul(pss[:], lhsT=BDb, rhs=sqb[:], start=True, stop=True), reads=[bsqb, bcstb], writes=[bpss])
            op("act", lambda e: e.activation(out=rsd[:], in_=pss[:], func=AF.Ln, scale=1.0 / 64, bias=EPS), reads=[bpss], writes=[brsd])
            op("act", lambda e: e.activation(out=rsd[:], in_=rsd[:], func=AF.Exp, scale=-0.5), reads=[brsd], writes=[brsd])
            op("act", lambda e, pa=pa, wcol=wcol: e.activation(out=t1s[:], in_=pa[:], func=AF.Copy, scale=nrm_sb[:, wcol:wcol + 1]), reads=[bpa, bnrm], writes=[bt1])
            op("act", lambda e, pr=pr, wcol=wcol: e.activation(out=t2s[:], in_=pr[:], func=AF.Copy, scale=nrm_sb[:, wcol + 1:wcol + 2]), reads=[bpr, bnrm], writes=[bt2])
            op("dve", lambda e, cs=cs: e.tensor_tensor(out=t1s[:], in0=t1s[:], in1=cs[:], op=ALU.mult), reads=[bt1, bcs], writes=[bt1])
            op("dve", lambda e, sn=sn: e.tensor_tensor(out=t2s[:], in0=t2s[:], in1=sn[:], op=ALU.mult), reads=[bt2, bsn], writes=[bt2])
            op("dve", lambda e: e.tensor_tensor(out=t1s[:], in0=t1s[:], in1=t2s[:], op=ALU.add), reads=[bt1, bt2], writes=[bt1])
            qo, bqo = rr(qfo, j)
            op("dve", lambda e, qo=qo: e.tensor_tensor(out=qo[:], in0=t1s[:], in1=rsd[:], op=ALU.mult), reads=[bt1, brsd], writes=[bqo])
            if j < 4:
                dma("pool", AQT[128 * j:128 * j + 128, t0:t0 + 512], qo[:], bqo, reads=[bqo], writes=[bAQT])
            else:
                dma("pool", AKT[:, t0:t0 + 512], qo[:], bqo, reads=[bqo], writes=[bAKT])
        for ti in range(4):
            tile = blk * 4 + ti
            r0 = tile * 128
            lhs = lambda c, ti=ti: xnT[:, c, ti * 128:(ti + 1) * 128]

            def ftm(e, lhs=lhs):
                ins = None
                for c in range(8):
                    ins = e.matmul(ptm1[:, 0:256], lhsT=lhs(c), rhs=wb[:, c, 256:512], start=(c == 0), stop=(c == 7))
                for c in range(8):
                    ins = e.matmul(ptm1[:, 256:384], lhsT=lhs(c), rhs=wb[:, c, 2208:2336], start=(c == 0), stop=(c == 7))
                return ins
            op("pe", ftm, reads=[bwb, bxnT], writes=[bptm1])

            def ftm2(e, lhs=lhs):
                ins = None
                for c in range(8):
                    ins = e.matmul(ptm2[:], lhsT=lhs(c), rhs=wb[:, c, 512:1024], start=(c == 0), stop=(c == 7))
                return ins
            op("pe", ftm2, reads=[bwb, bxnT], writes=[bptm2])
            op("pe", lambda e, ti=ti: e.matmul(pz[:], lhsT=raug[:, ti * 128:(ti + 1) * 128], rhs=wgb[:], start=True, stop=True),
               reads=[braug, bwgb], writes=[bpz])
            gk_t, bgk_t = rr(gko, tile)
            gv_t, bgv_t = rr(gvo, tile)
            av_t, bav_t = rr(avo, tile)
            g_t, bg_t = rr(go, tile)
            op("dve", lambda e, gk_t=gk_t: e.tensor_copy(out=gk_t[:], in_=ptm1[:, 0:256]), reads=[bptm1], writes=[bgk_t])
            op("dve", lambda e, av_t=av_t: e.tensor_copy(out=av_t[:, :, 0:64], in_=ptm1[:, 256:384].rearrange("p (g d) -> p g d", g=2)),
               reads=[bptm1], writes=[bav_t])
            op("act", lambda e, gv_t=gv_t: e.copy(out=gv_t[:], in_=ptm2[:]), reads=[bptm2], writes=[bgv_t])
            op("act", lambda e: e.activation(out=ge[:], in_=pz[:], func=AF.Exp, scale=-1.0), reads=[bpz], writes=[bge])
            op("act", lambda e: e.activation(out=ge[:], in_=ge[:], func=AF.Ln, bias=1.0), reads=[bge], writes=[bge])
            op("dve", lambda e, g_t=g_t: e.tensor_scalar(out=g_t[:], in0=ge[:], scalar1=-1.0 / 16.0, scalar2=None, op0=ALU.mult),
               reads=[bge], writes=[bg_t])
            dma("pool", GK[r0:r0 + 128, :], gk_t[:], bgk_t, reads=[bgk_t], writes=[bGK])
            dma("pool", GV[r0:r0 + 128, :], gv_t[:], bgv_t, reads=[bgv_t], writes=[bGV])
            dma("pool", AV[r0:r0 + 128, :], av_t[:].rearrange("p g d -> p (g d)"), bav_t, reads=[bav_t], writes=[bAV])
            dma("pool", G[r0:r0 + 128, :], g_t[:], bg_t, reads=[bg_t], writes=[bG])
    for blk in range(NB + 1):
        if blk < NB:
            stage_a1(blk)
        if blk >= 1:
            stage_proj(blk - 1)
    A.close()
    if stop == "A":
        return fin()

    Gp = Phase(P)
    P.defer = "G" in SCHED
    S32 = [Gp.sb("S32_%d" % i, [128, 128], F32) for i in range(2)]
    Sbf = [Gp.sb("Sbf_%d" % i, [128, 128], BF16) for i in range(2)]
    onesb, bonesb = Gp.sb("onesb", [128, 128], BF16)
    op("dve", lambda e: e.memset(onesb[:], 1.0), writes=[bonesb])
    NGB = 2
    g_g = [Gp.sb("g_g%d" % i, [128, 4, 256], F32) for i in range(NGB)]
    k_g = [Gp.sb("k_g%d" % i, [128, 4, 256], F32) for i in range(NGB)]
    v_g = [Gp.sb("v_g%d" % i, [128, 4, 512], BF16) for i in range(NGB)]
    qT_g = [Gp.sb("qT_g%d" % i, [128, 2, 512], F32) for i in range(NGB)]
    kT_g = [Gp.sb("kT_g%d" % i, [128, 2, 512], F32) for i in range(NGB)]
    of_g = [Gp.sb("of_g%d" % i, [128, 4, 512], F32) for i in range(NGB)]
    gg_g = [Gp.sb("gg_g%d" % i, [128, 4, 512], F32) for i in range(NGB)]
    mx_g = [Gp.sb("mx_g%d" % i, [128, 4, 512], BF16) for i in range(NGB)]
    E1, bE1 = Gp.sb("E1", [128, 256], F32)
    kend, bkend = Gp.sb("kend", [128, 256], BF16)
    Eb, bEb = Gp.sb("Eb", [128, 2, 128], F32)
    Enb, bEnb = Gp.sb("Enb", [128, 2, 128], F32)
    qtT = [Gp.sb("qtT%d" % i, [128, 2, 2, 128], BF16) for i in range(2)]
    for t_, b_ in qtT:
        op("dve", lambda e, t_=t_: e.memset(t_[:], 0.0), writes=[b_])
    ktT, bktT = Gp.sb("ktT", [128, 2, 128], BF16)
    Am = [Gp.sb("Am%d" % i, [128, 4, 128], BF16) for i in range(2)]
    osum, bosum = Gp.sb("osum", [128, 4, 128], F32)
    atf, batf = Gp.sb("atf", [128, 4, 128], F32)
    scs, bscs = Gp.sb("scs", [128, 4, 128], F32)
    osq, bosq = Gp.sb("osq", [128, 4, 128], BF16)
    orst, borst = Gp.sb("orst", [128, 4, 128], F32)
    gsig, bgsig = Gp.sb("gsig", [128, 4, 128], F32)
    p_rb, bp_rb = Gp.ps("p_rb", [128, 512], F32)
    p_at = [Gp.ps("p_at%d" % i, [128, 4, 128], F32) for i in range(2)]
    p_o = [Gp.ps("p_o%d" % i, [128, 4, 128], F32) for i in range(2)]
    p_sc, bp_sc = Gp.ps("p_sc", [128, 4, 128], F32)
    p_ss, bp_ss = Gp.ps("p_ss", [128, 4, 128], F32)

    NG = L // 512
    kends = [Gp.sb("kend_%d" % i, [128, 256], BF16) for i in range(2)]
    Ebs = [Gp.sb("Eb_%d" % i, [128, 2, 128], F32) for i in range(2)]
    for dr in range(2):
        for i in range(2):
            op("dve", lambda e, i=i: e.memset(S32[i][0][:], 0.0), writes=[S32[i][1]])
            op("dve", lambda e, i=i: e.memset(Sbf[i][0][:], 0.0), writes=[Sbf[i][1]])
        chunks = []
        for gi in range(QB if dr == 0 else NG):
            grp = gi if dr == 0 else NG - 1 - gi
            for cj in range(4):
                chunks.append(dict(gi=gi, grp=grp, full=(grp < QB), cj=cj, ci=(cj if dr == 0 else 3 - cj), idx=len(chunks)))
        gctx = {}
        dcol = 127 if dr == 0 else 0

        def load_group(ch):
            gi, grp, full = ch["gi"], ch["grp"], ch["full"]
            t0 = grp * 512
            gsel = dr * NG + gi
            c = dict(t0=t0)
            c["gt"], c["bgt"] = rr(g_g, gsel)
            c["kt"], c["bkt"] = rr(k_g, gsel)
            c["vt"], c["bvt"] = rr(v_g, gsel)
            dma("sp", c["gt"][:], G[t0:t0 + 512, dr * 256:(dr + 1) * 256].rearrange("(n p) c -> p n c", p=128), c["bgt"], reads=[bG], writes=[c["bgt"]])
            dma("sp", c["kt"][:], GK[t0:t0 + 512, :].rearrange("(n p) c -> p n c", p=128), c["bkt"], reads=[bGK], writes=[c["bkt"]])
            dma("sp", c["vt"][:], GV[t0:t0 + 512, :].rearrange("(n p) c -> p n c", p=128), c["bvt"], reads=[bGV], writes=[c["bvt"]])
            if full:
                c["qTt"], c["bqTt"] = rr(qT_g, gsel)
                c["kTt"], c["bkTt"] = rr(kT_g, gsel)
                dma("sp", c["qTt"][:], GQT[:, t0:t0 + 512].rearrange("(h p) t -> p h t", p=128), c["bqTt"], reads=[bGQT], writes=[c["bqTt"]])
                dma("sp", c["kTt"][:], GKT[:, t0:t0 + 512].rearrange("(h p) t -> p h t", p=128), c["bkTt"], reads=[bGKT], writes=[c["bkTt"]])
                c["oft"], c["boft"] = rr(of_g, gsel)
                if dr == 1:
                    c["ggt"], c["bggt"] = rr(gg_g, gsel)
                    c["mxt"], c["bmxt"] = rr(mx_g, gsel)
                    dma("sp", c["oft"][:], OF[:, t0:t0 + 512].rearrange("(h p) t -> p h t", p=128), c["boft"], reads=[bOF], writes=[c["boft"]])
                    dma("sp", c["ggt"][:], GGT[:, t0:t0 + 512].rearrange("(h p) t -> p h t", p=128), c["bggt"], reads=[bGGT], writes=[c["bggt"]])
            gctx[gi] = c

        def stage1(ch):
            if ch["cj"] == 0:
                load_group(ch)
            c = gctx[ch["gi"]]
            full, ci, idx = ch["full"], ch["ci"], ch["idx"]
            gt, bgt, kt, bkt = c["gt"], c["bgt"], c["kt"], c["bkt"]
            kend, bkend = rr(kends, idx)
            Eb, bEb = rr(Ebs, idx)
            tc0 = ci * 128

            def fm1(e):
                e.matmul(p_rb[:, 0:256], lhsT=LSf[dr], rhs=gt[:, ci, :], start=True, stop=True)
                ins = None
                for hp in range(2):
                    if full:
                        ins = e.matmul(p_rb[:, 256 + hp * 128:256 + (hp + 1) * 128], lhsT=gt[:, ci, hp * 128:(hp + 1) * 128], rhs=UIf[dr], start=True, stop=True)
                    else:
                        ins = e.matmul(p_rb[:, 256 + hp * 128 + dcol:256 + hp * 128 + dcol + 1], lhsT=gt[:, ci, hp * 128:(hp + 1) * 128], rhs=UIf[dr][:, dcol:dcol + 1], start=True, stop=True)
                return ins
            op("pe", fm1, reads=[bgt, bcst], writes=[bp_rb])
            op("act", lambda e: e.activation(out=E1[:], in_=p_rb[:, 0:256], func=AF.Exp), reads=[bp_rb], writes=[bE1])
            pbv = p_rb[:, 256:512].rearrange("p (h t) -> p h t", h=2)
            if full:
                op("act", lambda e: e.activation(out=Eb[:], in_=pbv, func=AF.Exp), reads=[bp_rb], writes=[bEb])
                op("act", lambda e: e.activation(out=Enb[:], in_=pbv, func=AF.Exp, scale=-1.0), reads=[bp_rb], writes=[bEnb])
            else:
                op("act", lambda e: e.activation(out=Eb[:, :, dcol:dcol + 1], in_=pbv[:, :, dcol:dcol + 1], func=AF.Exp), reads=[bp_rb], writes=[bEb])
            op("dve", lambda e: e.tensor_tensor(out=kend[:], in0=kt[:, ci, :], in1=E1[:], op=ALU.mult), reads=[bkt, bE1], writes=[bkend])
            if not full:
                return
            qTt, bqTt, kTt, bkTt = c["qTt"], c["bqTt"], c["kTt"], c["bkTt"]
            qq, bqq = rr(qtT, idx)
            op("dve", lambda e: e.tensor_tensor(out=qq[0:64, 0, :, :], in0=qTt[0:64, :, tc0:tc0 + 128], in1=Eb[0:64, :, :], op=ALU.mult), reads=[bqTt, bEb], writes=[bqq])
            op("dve", lambda e: e.tensor_tensor(out=qq[64:128, 1, :, :], in0=qTt[64:128, :, tc0:tc0 + 128], in1=Eb[64:128, :, :], op=ALU.mult), reads=[bqTt, bEb], writes=[bqq])
            op("dve", lambda e: e.tensor_tensor(out=ktT[:], in0=kTt[:, :, tc0:tc0 + 128], in1=Enb[:], op=ALU.mult), reads=[bkTt, bEnb], writes=[bktT])
            pat, bpat = rr(p_at, idx)

            def fm4(e):
                ins = None
                for h in range(4):
                    hp, h2 = h // 2, h % 2
                    ins = e.matmul(pat[:, h, :], lhsT=ktT[:, hp, :], rhs=qq[:, h2, hp, :], start=True, stop=True)
                return ins
            op("pe", fm4, reads=[bktT, bqq], writes=[bpat])
            am, bam = rr(Am, idx)
            op("act", lambda e: e.copy(out=atf[:], in_=pat[:]), reads=[bpat], writes=[batf])
            for h in range(4):
                op("pool", lambda e, h=h: e.tensor_tensor(out=am[:, h, :], in0=atf[:, h, :], in1=UIf[dr], op=ALU.mult),
                   reads=[batf, bcst], writes=[bam])

        def stage2(ch):
            c = gctx[ch["gi"]]
            full, ci, idx = ch["full"], ch["ci"], ch["idx"]
            vt, bvt = c["vt"], c["bvt"]
            kend, bkend = rr(kends, idx)
            Eb, bEb = rr(Ebs, idx)
            tc0 = ci * 128
            if full:
                qq, bqq = rr(qtT, idx)
                am, bam = rr(Am, idx)
                po, bpo = rr(p_o, idx)

                def fm5(e):
                    ins = None
                    for h in range(4):
                        hp, h2 = h // 2, h % 2
                        e.matmul(po[:, h, :], lhsT=vt[:, ci, h * 128:(h + 1) * 128], rhs=am[:, h, :], start=True, stop=False)
                        ins = e.matmul(po[:, h, :], lhsT=Sbf[hp][0][:, :], rhs=qq[:, h2, hp, :], start=False, stop=True)
                    return ins
                op("pe", fm5, reads=[bam, bvt, bqq, Sbf[0][1], Sbf[1][1]], writes=[bpo])

            def fm2(e):
                ins = None
                for h in range(4):
                    hp = h // 2
                    ins = e.matmul(p_sc[:, h, :], lhsT=kend[:, hp * 128:(hp + 1) * 128], rhs=vt[:, ci, h * 128:(h + 1) * 128], start=True, stop=True)
                return ins
            op("pe", fm2, reads=[bkend, bvt], writes=[bp_sc])
            op("act", lambda e: e.copy(out=scs[:], in_=p_sc[:]), reads=[bp_sc], writes=[bscs])
            for h in range(4):
                hp, h2 = h // 2, h % 2
                sl = slice(64 * h2, 64 * h2 + 64)
                op("dve", lambda e, hp=hp, sl=sl, h=h: e.scalar_tensor_tensor(out=S32[hp][0][sl, :], in0=S32[hp][0][sl, :], scalar=Eb[sl, hp, dcol:dcol + 1], in1=scs[sl, h, :], op0=ALU.mult, op1=ALU.add),
                   reads=[bEb, bscs], writes=[S32[hp][1]])
            for hp in range(2):
                op("act", lambda e, hp=hp: e.copy(out=Sbf[hp][0][:], in_=S32[hp][0][:]), reads=[S32[hp][1]], writes=[Sbf[hp][1]], est=0.4)
            if not full:
                return
            oft, boft = c["oft"], c["boft"]
            t0 = c["t0"]
            if dr == 0:
                op("act", lambda e: e.copy(out=oft[:, :, tc0:tc0 + 128], in_=po[:]), reads=[bpo], writes=[boft])
            else:
                ggt, bggt, mxt, bmxt = c["ggt"], c["bggt"], c["mxt"], c["bmxt"]
                op("act", lambda e: e.copy(out=osum[:], in_=po[:]), reads=[bpo], writes=[bosum])
                op("dve", lambda e: e.tensor_tensor(out=osum[:], in0=osum[:], in1=oft[:, :, tc0:tc0 + 128], op=ALU.add), reads=[bosum, boft], writes=[bosum])
                op("act", lambda e: e.activation(out=osq[:], in_=osum[:], func=AF.Square), reads=[bosum], writes=[bosq])
                op("pe", lambda e: e.matmul(p_ss[:], lhsT=onesb[:], rhs=osq[:], start=True, stop=True), reads=[bosq, bonesb], writes=[bp_ss])
                op("act", lambda e: e.activation(out=orst[:], in_=p_ss[:], func=AF.Ln, scale=1.0 / 128, bias=EPS), reads=[bp_ss], writes=[borst])
                op("act", lambda e: e.activation(out=orst[:], in_=orst[:], func=AF.Exp, scale=-0.5), reads=[borst], writes=[borst])
                op("act", lambda e: e.activation(out=gsig[:], in_=ggt[:, :, tc0:tc0 + 128], func=AF.Exp, scale=-1.0), reads=[bggt], writes=[bgsig])
                op("act", lambda e: e.activation(out=gsig[:], in_=gsig[:], func=AF.Ln, bias=1.0), reads=[bgsig], writes=[bgsig])
                op("act", lambda e: e.activation(out=gsig[:], in_=gsig[:], func=AF.Exp, scale=-1.0), reads=[bgsig], writes=[bgsig])
                op("dve", lambda e: e.tensor_tensor(out=gsig[:], in0=gsig[:], in1=ggt[:, :, tc0:tc0 + 128], op=ALU.mult), reads=[bgsig, bggt], writes=[bgsig])
                op("dve", lambda e: e.scalar_tensor_tensor(out=osum[:], in0=osum[:], scalar=nrm_sb[:, 4:5], in1=orst[:], op0=ALU.mult, op1=ALU.mult), reads=[bosum, borst, bnrm], writes=[bosum])
                op("dve", lambda e: e.tensor_tensor(out=mxt[:, :, tc0:tc0 + 128], in0=osum[:], in1=gsig[:], op=ALU.mult), reads=[bosum, bgsig], writes=[bmxt])
            if ch["cj"] == 3:
                if dr == 0:
                    dma("pool", OF[:, t0:t0 + 512].rearrange("(h p) t -> p h t", p=128), oft[:], boft, reads=[boft], writes=[bOF])
                else:
                    dma("pool", MIXT[0:512, t0:t0 + 512].rearrange("(h p) t -> p h t", p=128), mxt[:], bmxt, reads=[bmxt], writes=[bMIXT])

        for i in range(len(chunks) + 1):
            if i < len(chunks):
                stage1(chunks[i])
            if i >= 1:
                stage2(chunks[i - 1])
        P.flush()
    Gp.close()
    if stop == "G":
        return fin()

    P.flush()
    P.defer = False
    W2a = Phase(P)
    wu_b, bwu_b = W2a.sb("wu_b", [128, 8, 2 * DFF], BF16)
    lnf, blnf = W2a.sb("lnf", [128, 8], F32)
    dma("sp", lnf[:], ln_ffn[:, :], blnf, writes=[blnf])
    W2b = Phase(P)
    stg2 = [W2b.sb("stg2_%d" % i, [128, 1408], F32) for i in range(2)]

    def load_wu_chunk(k):
        c, q4 = k // 4, k % 4
        st, bst = rr(stg2, k)
        dma("pool", st[:], w_up[c * 128:(c + 1) * 128, q4 * 1408:(q4 + 1) * 1408], bst, writes=[bst])
        op("dve", lambda e, st=st, c=c, q4=q4: e.tensor_scalar(out=wu_b[:, c, q4 * 1408:(q4 + 1) * 1408], in0=st[:], scalar1=lnf[:, c:c + 1], scalar2=None, op0=ALU.mult),
           reads=[bst, blnf], writes=[bwu_b])
    B = Phase(P)
    P.defer = "B" in SCHED
    KT, bKT = B.sb("KT", [128, L], BF16)
    VA, bVA = B.sb("VA", [128, NT, 130], BF16)
    s65, bs65 = B.sb("s65", [65, 64], F32)
    dma("sp", s65[:], sel65[:, :], bs65, writes=[bs65])
    NCH = 4 if NT % 32 == 0 else (2 if NT % 16 == 0 else 1)
    TPC = NT // NCH
    bKTc = [Buf("KTc%d" % i) for i in range(NCH)]
    bVAc = [Buf("VAc%d" % i) for i in range(NCH)]
    for ch in range(NCH):
        t_lo, t_hi = ch * TPC, (ch + 1) * TPC
        dma("sp", KT[:, t_lo * 128:t_hi * 128], AKT[:, t_lo * 128:t_hi * 128], bKTc[ch], reads=[bAKT], writes=[bKTc[ch]])
        for c0 in range(t_lo, t_hi, 8):
            c1 = min(t_hi, c0 + 8)
            dma("sp", VA[:, c0:c1, :], AV[c0 * 128:c1 * 128, :].rearrange("(n p) c -> p n c", p=128), bVAc[ch], reads=[bAV], writes=[bVAc[ch]])
    QTg = [[B.sb("QT%d_%d" % (g_, i), [128, 512], BF16) for i in range(4)] for g_ in range(2)]
    for g_ in range(2):
        for t_, b_ in QTg[g_]:
            op("dve", lambda e, t_=t_: e.memset(t_[:], 0.0), writes=[b_])
    pst = [B.ps("pst%d" % i, [128, 3, 512], F32) for i in range(2)]
    pTs = [B.sb("pT%d" % i, [128, 3, 512], BF16) for i in range(3)]
    oaccs = [B.ps("oacc%d" % i, [128, 512], F32) for i in range(2)]
    osb = [B.sb("osb%d" % i, [65, 512], F32) for i in range(2)]
    rcp, brcp = B.sb("rcp", [64, 512], F32)
    oat = [B.sb("oat%d" % i, [64, 512], BF16) for i in range(2)]
    groups = []
    k0 = 0
    while k0 < NT:
        n = min(3, NT - k0)
        groups.append((k0, n))
        k0 += n
    items = []
    it = 0
    for (t0, nq) in qblocks:
        for h in range(8):
            for gi_, (k0, n) in enumerate(groups):
                items.append((it, h, t0, nq, gi_, k0, n))
            it += 1
    state = {}

    first_idx = {}
    for idx_, itm in enumerate(items):
        first_idx.setdefault(itm[0], idx_)
    n_items = it

    def load_q(it_):
        if it_ >= n_items or it_ in state:
            return
        _, h, t0, nq, _, _, _ = items[first_idx[it_]]
        g = h // 4
        qt, bqt = rr(QTg[g], it_)
        dma("sp", qt[64 * g:64 * g + 64, 0:nq], AQT[64 * h:64 * h + 64, t0:t0 + nq], bqt, reads=[bAQT], writes=[bqt])
        state[it_] = (qt, bqt)

    def emit_s(idx):
        it_, h, t0, nq, gi_, k0, n = items[idx]
        g = h // 4
        if gi_ == 0:
            load_q(it_)
            load_q(it_ + 1)
            if it_ < 32:
                load_wu_chunk(it_)
        qt, bqt = state[it_]
        st, bst = rr(pst, idx)
        pT, bpT = rr(pTs, idx)

        def fs(e):
            ins = None
            for j in range(n):
                ins = e.matmul(st[:, j, 0:nq], lhsT=KT[:, (k0 + j) * 128:(k0 + j + 1) * 128], rhs=qt[:, 0:nq], start=True, stop=True)
            return ins
        op("pe", fs, reads=[bKTc[c_] for c_ in sorted({(k0 + j) // TPC for j in range(n)})] + [bqt], writes=[bst])
        op("act", lambda e: e.activation(out=pT[:, 0:n, 0:nq], in_=st[:, 0:n, 0:nq], func=AF.Exp, scale=0.125), reads=[bst], writes=[bpT])

    def emit_pv(idx):
        it_, h, t0, nq, gi_, k0, n = items[idx]
        g = h // 4
        pT, bpT = rr(pTs, idx)
        oacc, boacc = rr(oaccs, it_)
        first = (gi_ == 0)
        last = (gi_ == len(groups) - 1)

        def fpv(e):
            ins = None
            for j in range(n):
                ins = e.matmul(oacc[0:65, 0:nq], lhsT=VA[:, k0 + j, 65 * g:65 * g + 65], rhs=pT[:, j, 0:nq], start=(first and j == 0), stop=(last and j == n - 1))
            return ins
        op("pe", fpv, reads=[bVAc[c_] for c_ in sorted({(k0 + j) // TPC for j in range(n)})] + [bpT], writes=[boacc])
        if last:
            ob, bob = rr(osb, it_)
            oa, boa = rr(oat, it_)
            op("dve", lambda e: e.tensor_copy(out=ob[:, 0:nq], in_=oacc[0:65, 0:nq]), reads=[boacc], writes=[bob])
            op("pe", lambda e: e.matmul(oacc[0:64, 0:nq], lhsT=s65[:], rhs=ob[:, 0:nq], start=True, stop=True), reads=[bob, bs65], writes=[boacc])
            op("dve", lambda e: e.reciprocal(out=rcp[:, 0:nq], in_=oacc[0:64, 0:nq]), reads=[boacc], writes=[brcp])
            op("dve", lambda e: e.tensor_tensor(out=oa[:, 0:nq], in0=ob[0:64, 0:nq], in1=rcp[:, 0:nq], op=ALU.mult), reads=[bob, brcp], writes=[boa])
            dma("pool", MIXT[512 + 64 * h:512 + 64 * h + 64, t0:t0 + nq], oa[:, 0:nq], boa, reads=[boa], writes=[bMIXT])

    for idx in range(len(items) + 1):
        if idx < len(items):
            emit_s(idx)
        if idx >= 1:
            emit_pv(idx - 1)
    for k_ in range(min(32, n_items), 32):
        load_wu_chunk(k_)
    B.close()
    W2b.close()
    wd_b, bwd_b = W2a.sb("wd_b", [128, 22, D], BF16)
    if stop == "B":
        W2a.close()
        return fin()

    C1 = Phase(P)
    P.defer = "C1" in SCHED
    wo_b, bwo_b = C1.sb("wo_b", [128, 8, D], BF16)
    stg1 = [C1.sb("stg1_%d" % i, [128, D], F32) for i in range(2)]
    for c in range(8):
        st, bst = rr(stg1, c)
        dma("sp", st[:], w_out[c * 128:(c + 1) * 128, :], bst, writes=[bst])
        op("dve", lambda e, st=st, c=c: e.tensor_copy(out=wo_b[:, c, :], in_=st[:]), reads=[bst], writes=[bwo_b])
    for f in range(22):
        st, bst = rr(stg1, f)
        dma("sp", st[:], w_down[f * 128:(f + 1) * 128, :], bst, writes=[bst])
        op("dve", lambda e, st=st, f=f: e.tensor_copy(out=wd_b[:, f, :], in_=st[:]), reads=[bst], writes=[bwd_b])
    zc, bzc = C1.sb("zc", [128, 8, 1], BF16)
    op("dve", lambda e: e.memset(zc[:], 0.0), writes=[bzc])
    dma("pool", H2T[:, 0:1].rearrange("(c p) t -> p c t", p=128), zc[:], bzc, reads=[bzc], writes=[bH2T], allow_slow_non_contiguous=True)
    dma("pool", H2T[:, L + 1:L + 2].rearrange("(c p) t -> p c t", p=128), zc[:], bzc, reads=[bzc], writes=[bH2T], allow_slow_non_contiguous=True)
    mixs = [C1.sb("mix%d" % i, [128, 8, 128], BF16) for i in range(3)]
    xs = [C1.sb("xs%d" % i, [128, D], F32) for i in range(3)]
    x1s = [C1.sb("x1s%d" % i, [128, D], F32) for i in range(2)]
    junk1, bjunk1 = C1.sb("junk1", [128, D], BF16)
    ssq1 = [C1.sb("ssq1_%d" % i, [128, 1], F32) for i in range(2)]
    h2s = [C1.sb("h2s%d" % i, [128, D], BF16) for i in range(2)]
    h2Ts = [C1.sb("h2T%d" % i, [128, 8, 128], BF16) for i in range(2)]
    pc1 = [C1.ps("pc1_%d" % i, [128, 2, 512], F32) for i in range(2)]
    ptr1 = [C1.ps("ptr1_%d" % i, [128, 8, 128], BF16) for i in range(2)]
    def c1_tile(tile, m):
        r0 = tile * 128
        mx, bmx = rr(mixs, tile)
        xt, bxt = rr(xs, tile)
        x1, bx1 = rr(x1s, tile)
        sq, bsq = rr(ssq1, tile)
        h2, bh2 = rr(h2s, tile)
        h2T, bh2T = rr(h2Ts, tile)
        pc, bpc = rr(pc1, tile)
        pt, bpt = rr(ptr1, tile)
        dma("sp", mx[:, :, 0:m], MIXT[:, r0:r0 + m].rearrange("(c p) t -> p c t", p=128), bmx, reads=[bMIXT], writes=[bmx])
        dma("sp", xt[0:m, :], x[r0:r0 + m, :], bxt, writes=[bxt])

        def fo(e):
            ins = None
            for hf in range(2):
                for c in range(8):
                    ins = e.matmul(pc[0:m, hf, :], lhsT=mx[:, c, 0:m], rhs=wo_b[:, c, hf * 512:(hf + 1) * 512], start=(c == 0), stop=(c == 7))
            return ins
        op("pe", fo, reads=[bmx, bwo_b], writes=[bpc])
        op("act", lambda e: e.copy(out=x1[0:m, :], in_=pc[0:m].rearrange("p a b -> p (a b)")), reads=[bpc], writes=[bx1])
        op("dve", lambda e: e.tensor_tensor(out=x1[0:m, :], in0=x1[0:m, :], in1=xt[0:m, :], op=ALU.add), reads=[bx1, bxt], writes=[bx1])
        dma("pool", X1[r0:r0 + m, :], x1[0:m, :], bx1, reads=[bx1], writes=[bX1])
        op("act", lambda e: e.activation(out=junk1[0:m, :], in_=x1[0:m, :], func=AF.Square, accum_out=sq[0:m, :]), reads=[bx1], writes=[bjunk1, bsq])
        rstd_from_ssq(None, sq[0:m, :], bsq, D)
        op("dve", lambda e: e.tensor_scalar(out=h2[0:m, :], in0=x1[0:m, :], scalar1=sq[0:m, 0:1], scalar2=None, op0=ALU.mult), reads=[bx1, bsq], writes=[bh2])

        def ftr1(e):
            ins = None
            for c in range(8):
                ins = e.transpose(out=pt[:, c, 0:m], in_=h2[0:m, c * 128:(c + 1) * 128], identity=cstb[0:m, 0:m])
            return ins
        op("pe", ftr1, reads=[bh2, bcstb], writes=[bpt])
        op("act", lambda e: e.copy(out=h2T[:, :, 0:m], in_=pt[:, :, 0:m]), reads=[bpt], writes=[bh2T])
        dma("pool", H2T[:, 1 + r0:1 + r0 + m].rearrange("(c p) t -> p c t", p=128), h2T[:, :, 0:m], bh2T, reads=[bh2T], writes=[bH2T])

    for tile in range(C1T):
        c1_tile(tile, EXTQ if (H is not None and tile == C1T - 1) else 128)
    C1.close()
    if stop == "C1":
        W2a.close()
        return fin()

    C2 = Phase(P)
    P.defer = "C2" in SCHED
    lfb, blfb = C2.sb("lfb", [128, D], F32)
    dma("sp", lfb[:], lnf_bc[:, :], blfb, writes=[blfb])
    cvp, bcvp = C2.sb("cvp", [128, 44, 4], F32)
    dma("sp", cvp[:], convp[:, :, :], bcvp, writes=[bcvp])
    TBW = 254
    hws = [C2.sb("hw%d" % i, [128, 8, 256], BF16) for i in range(2)]
    acs = [C2.sb("acs%d" % i, [128, 256], BF16) for i in range(6)]
    pu = [C2.ps("pu%d" % i, [128, 512], F32) for i in range(4)]
    pyt, bpyt = C2.ps("py", [128, 2, 2, 512], F32)
    tas = [C2.sb("ta%d" % i, [128, 256], F32) for i in range(2)]
    tgs = [C2.sb("tg%d" % i, [128, 256], F32) for i in range(2)]
    sgs = [C2.sb("sg%d" % i, [128, 256], F32) for i in range(2)]
    usb = [C2.sb("usb%d" % i, [128, 256], F32) for i in range(4)]
    x1l = [C2.sb("x1l%d" % i, [128, D], F32) for i in range(2)]
    yo = [C2.sb("yo%d" % i, [128, D], F32) for i in range(2)]
    junk2, bjunk2 = C2.sb("junk2", [128, D], BF16)
    ssq2 = [C2.sb("ssq2_%d" % i, [128, 1], F32) for i in range(2)]
    blocks = []
    t = 0
    while t < HO:
        n = min(TBW, HO - t)
        blocks.append((t, n))
        t += n
    nonlocal_puc = [0]
    slc = 0
    slc_ = [0]

    def do_block(bi, t0, n):
            hw, bhw = rr(hws, bi)
            dma("sp", hw[:, :, 0:n + 2], H2T[:, t0:t0 + n + 2].rearrange("(c p) t -> p c t", p=128), bhw, reads=[bH2T], writes=[bhw])
            stash = {}
            slices = []
            s0_ = 0
            while s0_ < n:
                m_ = min(128, n - s0_)
                slices.append((s0_, m_))
                s0_ += m_

            def stage1(f):
                res = []
                for part in range(2):
                    nonlocal_puc[0] += 1
                    pc_ = nonlocal_puc[0]
                    col0 = part * DFF + f * 128
                    pp, bpp = rr(pu, pc_)

                    def fu(e, pp=pp, col0=col0):
                        ins = None
                        for c in range(8):
                            ins = e.matmul(pp[:, 0:n + 2], lhsT=wu_b[:, c, col0:col0 + 128], rhs=hw[:, c, 0:n + 2], start=(c == 0), stop=(c == 7))
                        return ins
                    op("pe", fu, reads=[bwu_b, bhw], writes=[bpp])
                    tt, btt = rr(tas if part == 0 else tgs, f)
                    fc = part * 22 + f
                    us, bus = rr(usb, pc_)
                    op("act", lambda e, us=us, pp=pp: e.copy(out=us[:, 0:n + 2], in_=pp[:, 0:n + 2]), reads=[bpp], writes=[bus])
                    op("dve", lambda e, tt=tt, us=us, fc=fc: e.tensor_scalar(out=tt[:, 0:n], in0=us[:, 1:n + 1], scalar1=cvp[:, fc, 1:2], scalar2=cvp[:, fc, 3:4], op0=ALU.mult, op1=ALU.add),
                       reads=[bus, bcvp], writes=[btt])
                    op("dve", lambda e, tt=tt, us=us, fc=fc: e.scalar_tensor_tensor(out=tt[:, 0:n], in0=us[:, 0:n], scalar=cvp[:, fc, 0:1], in1=tt[:, 0:n], op0=ALU.mult, op1=ALU.add),
                       reads=[bus, bcvp, btt], writes=[btt])
                    op("dve", lambda e, tt=tt, us=us, fc=fc: e.scalar_tensor_tensor(out=tt[:, 0:n], in0=us[:, 2:n + 2], scalar=cvp[:, fc, 2:3], in1=tt[:, 0:n], op0=ALU.mult, op1=ALU.add),
                       reads=[bus, bcvp, btt], writes=[btt])
                    res.append((tt, btt))
                stash[f] = res

            def stage2(f):
                (ta, bta), (tg, btg) = stash.pop(f)
                sg, bsg = rr(sgs, f)
                op("act", lambda e: e.activation(out=sg[:, 0:n], in_=tg[:, 0:n], func=AF.Exp, scale=-1.0), reads=[btg], writes=[bsg])
                op("act", lambda e: e.activation(out=sg[:, 0:n], in_=sg[:, 0:n], func=AF.Ln, bias=1.0), reads=[bsg], writes=[bsg])
                op("act", lambda e: e.activation(out=sg[:, 0:n], in_=sg[:, 0:n], func=AF.Exp, scale=-1.0), reads=[bsg], writes=[bsg])
                op("pool", lambda e: e.tensor_tensor(out=tg[:, 0:n], in0=tg[:, 0:n], in1=sg[:, 0:n], op=ALU.mult), reads=[bsg, btg], writes=[btg])
                ac, bac = rr(acs, f)
                op("pool", lambda e: e.tensor_tensor(out=ac[:, 0:n], in0=ta[:, 0:n], in1=tg[:, 0:n], op=ALU.mult), reads=[bta, btg], writes=[bac])

            def stage3(f):
                ac, bac = rr(acs, f)

                def fd(e):
                    ins = None
                    for si, (s0, m) in enumerate(slices):
                        for hf in range(2):
                            ins = e.matmul(pyt[0:m, si, hf, :], lhsT=ac[:, s0:s0 + m], rhs=wd_b[:, f, hf * 512:(hf + 1) * 512], start=(f == 0), stop=(f == 21))
                    return ins
                op("pe", fd, reads=[bac, bwd_b], writes=[bpyt])

            for f in range(22 + 3):
                if f < 22:
                    stage1(f)
                if 0 <= f - 1 < 22:
                    stage2(f - 1)
                if 0 <= f - 3 < 22:
                    stage3(f - 3)
            for si, (s0, m) in enumerate(slices):
                r0 = t0 + s0
                xl, bxl = rr(x1l, slc_[0])
                yt_, byt = rr(yo, slc_[0])
                sq, bsq = rr(ssq2, slc_[0])
                slc_[0] += 1
                dma("sp", xl[0:m, :], X1[r0:r0 + m, :], bxl, reads=[bX1], writes=[bxl])
                op("act", lambda e, yt_=yt_, m=m, si=si: e.copy(out=yt_[0:m, :], in_=pyt[0:m, si].rearrange("p a b -> p (a b)")), reads=[bpyt], writes=[byt])
                op("dve", lambda e, yt_=yt_, xl=xl, m=m: e.tensor_tensor(out=yt_[0:m, :], in0=yt_[0:m, :], in1=xl[0:m, :], op=ALU.add),
                   reads=[byt, bxl], writes=[byt])
                op("act", lambda e, yt_=yt_, sq=sq, m=m: e.activation(out=junk2[0:m, :], in_=yt_[0:m, :], func=AF.Square, accum_out=sq[0:m, :]), reads=[byt], writes=[bjunk2, bsq])
                rstd_from_ssq(None, sq[0:m, :], bsq, D)
                op("dve", lambda e, yt_=yt_, sq=sq, m=m: e.scalar_tensor_tensor(out=yt_[0:m, :], in0=yt_[0:m, :], scalar=sq[0:m, 0:1], in1=lfb[0:m, :], op0=ALU.mult, op1=ALU.mult),
                   reads=[byt, bsq, blfb], writes=[byt])
                dma("pool", y[r0:r0 + m, :], yt_[0:m, :], byt, reads=[byt], writes=[by])

    for bi, (t0, n) in enumerate(blocks):
        do_block(bi, t0, n)
    C2.close()
    W2a.close()
    return fin()


def host_consts(L):
    i = np.arange(128)
    s = i[:, None]
    c = i[None, :]
    ident = (s == c)
    UI_f = (s <= c)
    UI_b = (s >= c)
    LS_f = (s > c)
    LS_b = (s < c)
    BD = ((s // 64) == (c // 64))
    consts = np.concatenate([m.astype(np.float32) for m in (ident, UI_f, UI_b, LS_f, LS_b, BD)], axis=1)
    sel65 = np.zeros((65, 64), np.float32)
    sel65[64, :] = 1.0
    half = 32
    inv = (1.0 / (10000.0 ** (np.arange(0, half, 2, dtype=np.float32) / half))).astype(np.float32)
    t = np.arange(L)
    row = (t // 64).astype(np.float32)
    col = (t % 64).astype(np.float32)
    ang_r = row[:, None] * inv[None, :]
    ang_c = col[:, None] * inv[None, :]
    cosT = np.zeros((64, L), np.float32)
    sinT = np.zeros((64, L), np.float32)
    for d in range(64):
        hf = d // 32
        j = d % 32
        ang = (ang_r if hf == 0 else ang_c)[:, j % 16]
        cosT[d] = np.cos(ang)
        sinT[d] = np.sin(ang) * (-1.0 if j < 16 else 1.0)
    cosT = np.ascontiguousarray(np.tile(cosT, (2, 1)))
    sinT = np.ascontiguousarray(np.tile(sinT, (2, 1)))
    return consts, sel65, cosT, sinT


def perm64():
    p = np.zeros(64, np.int64)
    for d in range(64):
        j = d % 32
        p[d] = d + 16 if j < 16 else d - 16
    return p


def host_inputs(xs, ln_mix, w_in, w_gk_fwd, b_gk_fwd, w_gk_bwd, b_gk_bwd, gla_out_norm, q_norm, k_norm,
                w_out, ln_ffn, w_up, conv_w, conv_b, w_down, ln_final, L, revs=None):
    f = np.float32
    if revs is None:
        revs = [False] * len(xs)
    consts, sel65, cosT, sinT = host_consts(L)
    w_in0 = np.ascontiguousarray(np.asarray(w_in[0], f))
    pm = perm64()
    cols = []
    for h in range(8):
        cols.append(1568 + h * 64 + pm)
    for h in range(2):
        cols.append(2080 + h * 64 + pm)
    cols = np.concatenate(cols)
    w_rot = np.ascontiguousarray(w_in0[:, cols])
    nrm = np.zeros((128, 8), f)
    qn = np.asarray(q_norm[0], f)
    kn = np.asarray(k_norm[0], f)
    nrm[:, 0] = np.tile(qn, 2)
    nrm[:, 1] = np.tile(qn[pm], 2)
    nrm[:, 2] = np.tile(kn, 2)
    nrm[:, 3] = np.tile(kn[pm], 2)
    nrm[:, 4] = np.asarray(gla_out_norm[0], f)
    cw = np.asarray(conv_w[0], f)
    cb = np.asarray(conv_b[0], f)
    variants = {}
    for rv in (False, True):
        wg = np.zeros((33, 512), f)
        fs, bs = (slice(0, 256), slice(256, 512)) if not rv else (slice(256, 512), slice(0, 256))
        wg[0:16, fs] = w_gk_fwd[0]
        wg[16:32, bs] = w_gk_bwd[0]
        wg[32, fs] = b_gk_fwd[0]
        wg[32, bs] = b_gk_bwd[0]
        convp = np.zeros((128, 44, 4), f)
        for k in range(3):
            kk = k if not rv else 2 - k
            convp[:, :, kk] = cw[k].reshape(44, 128).T
        convp[:, :, 3] = cb.reshape(44, 128).T
        ct = cosT if not rv else np.ascontiguousarray(cosT[:, ::-1])
        sn = sinT if not rv else np.ascontiguousarray(sinT[:, ::-1])
        variants[rv] = {"wg_aug": wg, "convp": convp, "cosT": ct, "sinT": sn}
    common = {
        "w_in": w_in0, "w_rot": w_rot,
        "ln_mix": np.ascontiguousarray(np.asarray(ln_mix[0], f).reshape(8, 128).T),
        "ln_ffn": np.ascontiguousarray(np.asarray(ln_ffn[0], f).reshape(8, 128).T),
        "lnf_bc": np.ascontiguousarray(np.broadcast_to(np.asarray(ln_final, f)[None, :], (128, D))),
        "nrm": nrm, "consts": consts, "sel65": sel65,
        "w_out": np.ascontiguousarray(np.asarray(w_out[0], f)),
        "w_up": np.ascontiguousarray(np.asarray(w_up[0], f)),
        "w_down": np.ascontiguousarray(np.asarray(w_down[0], f)),
    }
    maps = []
    for xx, rv in zip(xs, revs):
        m = dict(common)
        m.update(variants[bool(rv)])
        xx = np.asarray(xx, f)
        m["x"] = np.ascontiguousarray(xx[::-1] if rv else xx)
        maps.append(m)
    return maps


def run_sequences(seqs, W, L, n_cores=8):
    H = L // 2
    zero = np.zeros((L, D), np.float32)
    xs = [zero] * n_cores
    revs = [False] * n_cores
    for k, sq in enumerate(seqs):
        xs[2 * k] = sq
        xs[2 * k + 1] = sq
        revs[2 * k + 1] = True
    maps = host_inputs(xs, W["ln_mix"], W["w_in"], W["w_gk_fwd"], W["b_gk_fwd"], W["w_gk_bwd"], W["b_gk_bwd"],
                       W["gla_out_norm"], W["q_norm"], W["k_norm"], W["w_out"], W["ln_ffn"], W["w_up"],
                       W["conv_w"], W["conv_b"], W["w_down"], W["ln_final"], L, revs)
    nc, _ = build_program(L, H=H)
    res = run_bass_kernel_spmd(nc, maps, core_ids=list(range(n_cores)))
    outs = []
    for k in range(len(seqs)):
        a = np.asarray(res.results[2 * k]["y"], np.float32)
        b = np.asarray(res.results[2 * k + 1]["y"], np.float32)
        outs.append(np.concatenate([a, b[::-1]], axis=0))
    return outs


def kernel(x_prompt, x_sample, ln_mix, w_in, w_gk_fwd, b_gk_fwd, w_gk_bwd, b_gk_bwd, gla_out_norm,
           q_norm, k_norm, w_out, ln_ffn, w_up, conv_w, conv_b, w_down, ln_final):
    x_prompt = np.asarray(x_prompt)
    x_sample = np.asarray(x_sample)
    L = x_prompt.shape[1]
    seqs = [x_prompt[0], x_prompt[1], x_sample[0]]
    W = dict(ln_mix=np.asarray(ln_mix), w_in=np.asarray(w_in), w_gk_fwd=np.asarray(w_gk_fwd), b_gk_fwd=np.asarray(b_gk_fwd),
             w_gk_bwd=np.asarray(w_gk_bwd), b_gk_bwd=np.asarray(b_gk_bwd), gla_out_norm=np.asarray(gla_out_norm),
             q_norm=np.asarray(q_norm), k_norm=np.asarray(k_norm), w_out=np.asarray(w_out), ln_ffn=np.asarray(ln_ffn),
             w_up=np.asarray(w_up), conv_w=np.asarray(conv_w), conv_b=np.asarray(conv_b), w_down=np.asarray(w_down),
             ln_final=np.asarray(ln_final))
    outs = run_sequences(seqs, W, L, 8)
    y_prompt = np.stack([outs[0], outs[1]], axis=0)
    y_sample = outs[2][None]
    return (y_prompt, y_sample)
```

```python
import numpy as np
from contextlib import ExitStack
import concourse.bass as bass
import concourse.mybir as mybir
from concourse.bass_utils import run_bass_kernel_spmd

F32 = mybir.dt.float32
BF16 = mybir.dt.bfloat16
AF = mybir.ActivationFunctionType
ALU = mybir.AluOpType
AX = mybir.AxisListType

ENGS = ("pe", "act", "dve", "pool", "sp")
D = 1024
INW = 2336
DFF = 2816
EPS = 1e-6


class Buf:
    def __init__(self, name):
        self.name = name
        self.w = {}
        self.r = {}
        self.dsem = None
        self.dcnt = 0


class Prog:
    def __init__(self, nc):
        self.nc = nc
        self.es = ExitStack()
        self.sems = {}
        self.cnt = {e: 0 for e in ENGS}
        self.waited = {e: {} for e in ENGS}
        self.E = {"pe": nc.tensor, "act": nc.scalar, "dve": nc.vector, "pool": nc.gpsimd, "sp": nc.sync}
        for e in ENGS:
            self.sems[e] = self.es.enter_context(nc.semaphore("s_" + e))
        self.nins = 0
        self.allbufs = []
        self.defer = False
        self.pending = []
        self.EST = {"pe": 1.0, "act": 0.7, "dve": 0.5, "pool": 1.0, "sp": 0.1}

    def dram(self, name, shape, dt, kind="Internal"):
        return self.nc.dram_tensor(name, list(shape), dt, kind=kind).ap()

    def buf_sem(self, b, eng):
        if b.dsem is None:
            b.dsem = {}
            b.dcnt = {}
            self.allbufs.append(b)
        if eng not in b.dsem:
            sm = self.es.enter_context(self.nc.semaphore("d_%s_%s" % (b.name, eng)))
            b.dsem[eng] = sm
            b.dcnt[eng] = 0
            self.sems[sm] = sm
        return b.dsem[eng]

    def _collect(self, eng, reads, writes):
        need = {}
        for b in reads:
            for k, v in b.w.items():
                if need.get(k, 0) < v:
                    need[k] = v
        for b in writes:
            for k, v in b.w.items():
                if need.get(k, 0) < v:
                    need[k] = v
            for k, v in b.r.items():
                if need.get(k, 0) < v:
                    need[k] = v
        out = []
        wd = self.waited[eng]
        for k, v in need.items():
            if k == eng and eng == "pe":
                continue
            if wd.get(k, 0) >= v:
                continue
            wd[k] = v
            out.append((k, v))
        return out

    def _emit(self, eng, waits, fn, inc):
        e = self.E[eng]
        for k, v in waits:
            e.wait_ge(self.sems[k], v)
        if fn is None:
            return
        ins = fn(e)
        self.nins += 1
        if inc is not None:
            ins.then_inc(self.sems[inc[0]], inc[1])

    def op(self, eng, fn, reads=(), writes=(), est=None):
        if self.defer:
            self.pending.append(("op", eng, fn, tuple(reads), tuple(writes), est if est is not None else self.EST[eng], None, None))
            return
        waits = self._collect(eng, reads, writes)
        self.cnt[eng] += 1
        tok = (eng, self.cnt[eng])
        self._emit(eng, waits, fn, (eng, 1))
        for b in reads:
            b.r[tok[0]] = tok[1]
        for b in writes:
            b.w[tok[0]] = tok[1]

    def dma(self, eng, out, in_, own, reads=(), writes=(), **kw):
        if self.defer:
            self.pending.append(("dma", eng, (out, in_, kw), tuple(reads), tuple(writes), 3.0, own, None))
            return
        waits = self._collect(eng, reads, writes)
        sem = self.buf_sem(own, eng)
        own.dcnt[eng] += 16
        tok = (sem, own.dcnt[eng])
        self._emit(eng, waits, lambda e: e.dma_start(out=out, in_=in_, **kw), (sem, 16))
        for b in reads:
            b.r[tok[0]] = tok[1]
        for b in writes:
            b.w[tok[0]] = tok[1]

    def flush(self):
        ops = self.pending
        self.pending = []
        if not ops:
            return
        import heapq
        n = len(ops)
        preds = [set() for _ in range(n)]
        lastw = {}
        readers = {}
        for i, o in enumerate(ops):
            for b in o[3]:
                if b in lastw:
                    preds[i].add(lastw[b])
            for b in o[4]:
                if b in lastw:
                    preds[i].add(lastw[b])
                for r in readers.get(b, ()):
                    preds[i].add(r)
            for b in o[3]:
                readers.setdefault(b, []).append(i)
            for b in o[4]:
                lastw[b] = i
                readers[b] = []
            preds[i].discard(i)
        succs = [[] for _ in range(n)]
        npred = [0] * n
        for i in range(n):
            npred[i] = len(preds[i])
            for p in preds[i]:
                succs[p].append(i)
        ready_t = [0.0] * n
        finish = [0.0] * n
        heaps = {e: [] for e in ENGS}
        for i in range(n):
            if npred[i] == 0:
                heapq.heappush(heaps[ops[i][1]], (0.0, i))
        efree = {e: 0.0 for e in ENGS}
        order = []
        done = 0
        while done < n:
            best = None
            for e in ENGS:
                h = heaps[e]
                if not h:
                    continue
                rt, i = h[0]
                st = max(rt, efree[e])
                if best is None or (st, i) < (best[0], best[2]):
                    best = (st, e, i)
            st, e, i = best
            heapq.heappop(heaps[e])
            o = ops[i]
            if o[0] == "dma":
                efree[e] = st + 0.1
                finish[i] = st + o[5]
            else:
                efree[e] = st + o[5]
                finish[i] = st + o[5] + 0.15
            order.append(i)
            done += 1
            for sidx in succs[i]:
                npred[sidx] -= 1
                if finish[i] > ready_t[sidx]:
                    ready_t[sidx] = finish[i]
                if npred[sidx] == 0:
                    heapq.heappush(heaps[ops[sidx][1]], (ready_t[sidx], sidx))
        prev = self.defer
        self.defer = False
        for i in order:
            o = ops[i]
            if o[0] == "dma":
                out, in_, kw = o[2]
                self.dma(o[1], out, in_, o[6], reads=o[3], writes=o[4], **kw)
            else:
                self.op(o[1], o[2], reads=o[3], writes=o[4])
        self.defer = prev

    def barrier(self):
        self.flush()
        b = Buf("barrier")
        for e in ENGS:
            if self.cnt[e] > 0:
                b.w[e] = self.cnt[e]
        for k, v in self.dsem_counts().items():
            b.w[k] = v
        for e in ENGS:
            self.wait_all(e, [b])

    def dsem_counts(self):
        out = {}
        for bb in self.allbufs:
            for eng, sm in bb.dsem.items():
                out[sm] = bb.dcnt[eng]
        return out

    def wait_all(self, eng, bufs):
        self.flush()
        waits = self._collect(eng, bufs, ())
        self._emit(eng, waits, None, None)


class Phase:
    def __init__(self, P):
        self.P = P
        self.es = ExitStack()

    def sb(self, name, shape, dt):
        t = self.es.enter_context(self.P.nc.sbuf_tensor(name, list(shape), dt))
        return t, Buf(name)

    def ps(self, name, shape, dt=F32):
        t = self.es.enter_context(self.P.nc.psum_tensor(name, list(shape), dt))
        return t, Buf(name)

    def close(self):
        self.P.barrier()
        self.es.close()


def rr(lst, i):
    return lst[i % len(lst)]


def build_program(L, dbg=False, stop=None, H=None):
    assert L % 512 == 0
    EXTQ = 16
    NT = L // 128
    NB = L // 512
    if H is None:
        HO = L
        QB = NB
        qblocks = [(qb * 512, 512) for qb in range(NB)]
        C1T = NT
    else:
        assert H % 512 == 0 and H < L
        HO = H
        QB = H // 512 + 1
        qblocks = [(qb * 512, 512) for qb in range(H // 512)] + [(H, EXTQ)]
        C1T = H // 128 + 1
    nc = bass.Bass("TRN2", target_bir_lowering=False)
    P = Prog(nc)
    op, dma = P.op, P.dma

    def ext(name, shape, dt=F32):
        return P.dram(name, shape, dt, kind="ExternalInput")

    x = ext("x", [L, D])
    w_in = ext("w_in", [D, INW])
    w_rot = ext("w_rot", [D, 640])
    ln_mix = ext("ln_mix", [128, 8])
    ln_ffn = ext("ln_ffn", [128, 8])
    lnf_bc = ext("lnf_bc", [128, D])
    wg_aug = ext("wg_aug", [33, 512])
    nrm = ext("nrm", [128, 8])
    cosT = ext("cosT", [128, L])
    sinT = ext("sinT", [128, L])
    consts = ext("consts", [128, 6 * 128])
    sel65 = ext("sel65", [65, 64])
    w_out = ext("w_out", [D, D])
    w_up = ext("w_up", [D, 2 * DFF])
    w_down = ext("w_down", [DFF, D])
    convp = ext("convp", [128, 44, 4])
    y = P.dram("y", [HO, D], F32, kind="ExternalOutput")
    by = Buf("y")

    kind = "ExternalOutput" if dbg else "Internal"
    GQT = P.dram("GQT", [256, L], F32, kind=kind); bGQT = Buf("GQT")
    GKT = P.dram("GKT", [256, L], F32, kind=kind); bGKT = Buf("GKT")
    GGT = P.dram("GGT", [512, L], F32, kind=kind); bGGT = Buf("GGT")
    GK = P.dram("GK", [L, 256], F32, kind=kind); bGK = Buf("GK")
    GV = P.dram("GV", [L, 512], BF16, kind=kind); bGV = Buf("GV")
    G = P.dram("G", [L, 512], F32, kind=kind); bG = Buf("G")
    AQT = P.dram("AQT", [512, L], BF16, kind=kind); bAQT = Buf("AQT")
    AKT = P.dram("AKT", [128, L], BF16, kind=kind); bAKT = Buf("AKT")
    AV = P.dram("AV", [L, 130], BF16, kind=kind); bAV = Buf("AV")
    OF = P.dram("OF", [512, L], F32, kind=kind); bOF = Buf("OF")
    MIXT = P.dram("MIXT", [1024, L], BF16, kind=kind); bMIXT = Buf("MIXT")
    X1 = P.dram("X1", [L, D], F32, kind=kind); bX1 = Buf("X1")
    H2T = P.dram("H2T", [1024, L + 2], BF16, kind=kind); bH2T = Buf("H2T")

    G0 = Phase(P)
    cst, bcst = G0.sb("cst", [128, 6 * 128], F32)
    cstb, bcstb = G0.sb("cstb", [128, 6 * 128], BF16)
    nrm_sb, bnrm = G0.sb("nrm_sb", [128, 8], F32)
    dma("sp", cst[:], consts[:, :], bcst, writes=[bcst])
    dma("sp", nrm_sb[:], nrm[:, :], bnrm, writes=[bnrm])
    op("dve", lambda e: e.tensor_copy(out=cstb[:], in_=cst[:]), reads=[bcst], writes=[bcstb])
    ident_b = cstb[:, 0:128]
    UIf = [cst[:, 128:256], cst[:, 256:384]]
    LSf = [cst[:, 384:512], cst[:, 512:640]]
    UIb = [cstb[:, 128:256], cstb[:, 256:384]]
    BDb = cstb[:, 640:768]

    def fin():
        P.wait_all("pool", [by, bGQT, bGKT, bGGT, bGK, bGV, bG, bAQT, bAKT, bAV, bOF, bMIXT, bX1, bH2T])
        G0.close()
        P.es.close()
        return nc, P

    def rstd_from_ssq(ph_bufs, ssq, bssq, n, shape_ap=None):
        op("act", lambda e: e.activation(out=ssq, in_=ssq, func=AF.Ln, scale=1.0 / n, bias=EPS), reads=[bssq], writes=[bssq])
        op("act", lambda e: e.activation(out=ssq, in_=ssq, func=AF.Exp, scale=-0.5), reads=[bssq], writes=[bssq])

    import os
    SCHED = os.environ.get("SCHED", "A,G,C1,C2").split(",")
    A = Phase(P)
    P.defer = "A" in SCHED
    WC = INW + 640
    wb, bwb = A.sb("wb", [128, 8, WC], BF16)
    lnm, blnm = A.sb("lnm", [128, 8], F32)
    dma("sp", lnm[:], ln_mix[:, :], blnm, writes=[blnm])
    stg = [A.sb("stg%d" % i, [128, WC], F32) for i in range(2)]
    for c in range(8):
        st, bst = rr(stg, c)
        dma("sp", st[:, 0:INW], w_in[c * 128:(c + 1) * 128, :], bst, writes=[bst])
        dma("sp", st[:, INW:WC], w_rot[c * 128:(c + 1) * 128, :], bst, writes=[bst])
        op("dve", lambda e, st=st, c=c: e.tensor_scalar(out=wb[:, c, :], in0=st[:], scalar1=lnm[:, c:c + 1], scalar2=None, op0=ALU.mult),
           reads=[bst, blnm], writes=[bwb])
    wgf, bwgf = A.sb("wgf", [33, 512], F32)
    wgb, bwgb = A.sb("wgb", [33, 512], BF16)
    dma("sp", wgf[:], wg_aug[:, :], bwgf, writes=[bwgf])
    op("dve", lambda e: e.tensor_copy(out=wgb[:], in_=wgf[:]), reads=[bwgf], writes=[bwgb])

    xts = [A.sb("xt%d" % i, [128, D], F32) for i in range(3)]
    junk, bjunk = A.sb("junk", [128, D], BF16)
    ssqs = [A.sb("ssq%d" % i, [128, 1], F32) for i in range(3)]
    xbs = [A.sb("xb%d" % i, [128, D], BF16) for i in range(2)]
    xnTs = [A.sb("xnT%d" % i, [128, 8, 512], BF16) for i in range(2)]
    raug, braug = A.sb("raug", [33, 512], BF16)
    op("dve", lambda e: e.memset(raug[:], 1.0), writes=[braug])
    ptr = [A.ps("ptr%d" % i, [128, 8, 128], BF16) for i in range(1)]
    pfm = [A.ps("pfm%d" % i, [128, 512], F32) for i in range(3)]
    pss, bpss = A.ps("pss", [128, 512], F32)
    ptm1, bptm1 = A.ps("ptm1", [128, 512], F32)
    ptm2, bptm2 = A.ps("ptm2", [128, 512], F32)
    pz, bpz = A.ps("pz", [128, 512], F32)
    fmo = [A.sb("fmo%d" % i, [128, 512], F32) for i in range(3)]
    cos_sb = [A.sb("cos%d" % i, [128, 512], F32) for i in range(2)]
    sin_sb = [A.sb("sin%d" % i, [128, 512], F32) for i in range(2)]
    sqb, bsqb = A.sb("sqb", [128, 512], BF16)
    rsd, brsd = A.sb("rsd", [128, 512], F32)
    t1s, bt1 = A.sb("t1s", [128, 512], F32)
    t2s, bt2 = A.sb("t2s", [128, 512], F32)
    qfo = [A.sb("qfo%d" % i, [128, 512], BF16) for i in range(2)]
    gko = [A.sb("gko%d" % i, [128, 256], F32) for i in range(2)]
    gvo = [A.sb("gvo%d" % i, [128, 512], BF16) for i in range(2)]
    avo = [A.sb("avo%d" % i, [128, 2, 65], BF16) for i in range(2)]
    for t, b in avo:
        op("dve", lambda e, t=t: e.memset(t[:], 1.0), writes=[b])
    ge, bge = A.sb("ge", [128, 512], F32)
    go = [A.sb("go%d" % i, [128, 512], F32) for i in range(2)]

    fmcount = [0]

    def fm_proj(col0, ncols, xnT, bxnT):
        pt, bpt = rr(pfm, fmcount[0])
        fmcount[0] += 1

        def f(e):
            ins = None
            for c in range(8):
                ins = e.matmul(pt[0:ncols, :], lhsT=wb[:, c, col0:col0 + ncols], rhs=xnT[:, c, :], start=(c == 0), stop=(c == 7))
            return ins
        op("pe", f, reads=[bwb, bxnT], writes=[bpt])
        return pt, bpt

    stc = [0]
    import os
    LVL = int(os.environ.get("PALVL", "9"))
    def stage_a1(blk):
        xnT, bxnT = rr(xnTs, blk)
        for ti in range(4):
            tile = blk * 4 + ti
            xt, bxt = rr(xts, tile)
            ssq, bssq = rr(ssqs, tile)
            xb, bxb = rr(xbs, tile)
            pt, bpt = rr(ptr, tile)
            dma("sp", xt[:], x[tile * 128:(tile + 1) * 128, :], bxt, writes=[bxt])
            op("act", lambda e, xt=xt, ssq=ssq: e.activation(out=junk[:], in_=xt[:], func=AF.Square, accum_out=ssq[:]),
               reads=[bxt], writes=[bjunk, bssq])
            rstd_from_ssq(None, ssq[:], bssq, D)
            op("dve", lambda e, xb=xb, xt=xt, ssq=ssq: e.tensor_scalar(out=xb[:], in0=xt[:], scalar1=ssq[:, 0:1], scalar2=None, op0=ALU.mult),
               reads=[bxt, bssq], writes=[bxb])

            def ftr(e, xb=xb, pt=pt):
                ins = None
                for c in range(8):
                    ins = e.transpose(out=pt[:, c, :], in_=xb[:, c * 128:(c + 1) * 128], identity=ident_b)
                return ins
            op("pe", ftr, reads=[bxb, bcstb], writes=[bpt])
            op("act", lambda e, xnT=xnT, pt=pt, ti=ti: e.copy(out=xnT[:, :, ti * 128:(ti + 1) * 128], in_=pt[:]),
               reads=[bpt], writes=[bxnT])
    def stage_proj(blk):
        xnT, bxnT = rr(xnTs, blk)
        t0 = blk * 512
        qside = blk < QB
        for (col0, dst, bdst, row0, scale) in [] if not qside else (
            [(0 + 128 * j, GQT, bGQT, 128 * j, 0.125) for j in range(2)]
            + [(256 + 128 * j, GKT, bGKT, 128 * j, 1.0) for j in range(2)]
            + [(1024 + 128 * j, GGT, bGGT, 128 * j, 1.0) for j in range(4)]
        ):
            pt, bpt = fm_proj(col0, 128, xnT, bxnT)
            so, bso = rr(fmo, stc[0]); stc[0] += 1
            op("act", lambda e, so=so, pt=pt, scale=scale: e.activation(out=so[:], in_=pt[:], func=AF.Copy, scale=scale),
               reads=[bpt], writes=[bso])
            dma("pool", dst[row0:row0 + 128, t0:t0 + 512], so[:], bso, reads=[bso], writes=[bdst])
        pt, bpt = fm_proj(1536, 32, xnT, bxnT)
        op("act", lambda e, pt=pt: e.copy(out=raug[0:32, :], in_=pt[0:32, :]), reads=[bpt], writes=[braug])
        cs, bcs = rr(cos_sb, blk)
        sn, bsn = rr(sin_sb, blk)
        dma("sp", cs[:], cosT[:, t0:t0 + 512], bcs, writes=[bcs])
        dma("sp", sn[:], sinT[:, t0:t0 + 512], bsn, writes=[bsn])
        for j in (range(5) if qside else [4]):
            col0 = 1568 + 128 * j
            colr = INW + 128 * j
            wcol = 0 if j < 4 else 2
            pa, bpa = fm_proj(col0, 128, xnT, bxnT)
            pr, bpr = fm_proj(colr, 128, xnT, bxnT)
            op("act", lambda e, pa=pa: e.activation(out=sqb[:], in_=pa[:], func=AF.Square), reads=[bpa], writes=[bsqb])
            op("pe", lambda e: e.matmul(pss[:], lhsT=BDb, rhs=sqb[:], start=True, stop=True), reads=[bsqb, bcstb], writes=[bpss])
            op("act", lambda e: e.activation(out=rsd[:], in_=pss[:], func=AF.Ln, scale=1.0 / 64, bias=EPS), reads=[bpss], writes=[brsd])
            op("act", lambda e: e.activation(out=rsd[:], in_=rsd[:], func=AF.Exp, scale=-0.5), reads=[brsd], writes=[brsd])
            op("act", lambda e, pa=pa, wcol=wcol: e.activation(out=t1s[:], in_=pa[:], func=AF.Copy, scale=nrm_sb[:, wcol:wcol + 1]), reads=[bpa, bnrm], writes=[bt1])
            op("act", lambda e, pr=pr, wcol=wcol: e.activation(out=t2s[:], in_=pr[:], func=AF.Copy, scale=nrm_sb[:, wcol + 1:wcol + 2]), reads=[bpr, bnrm], writes=[bt2])
            op("dve", lambda e, cs=cs: e.tensor_tensor(out=t1s[:], in0=t1s[:], in1=cs[:], op=ALU.mult), reads=[bt1, bcs], writes=[bt1])
            op("dve", lambda e, sn=sn: e.tensor_tensor(out=t2s[:], in0=t2s[:], in1=sn[:], op=ALU.mult), reads=[bt2, bsn], writes=[bt2])
            op("dve", lambda e: e.tensor_tensor(out=t1s[:], in0=t1s[:], in1=t2s[:], op=ALU.add), reads=[bt1, bt2], writes=[bt1])
            qo, bqo = rr(qfo, j)
            op("dve", lambda e, qo=qo: e.tensor_tensor(out=qo[:], in0=t1s[:], in1=rsd[:], op=ALU.mult), reads=[bt1, brsd], writes=[bqo])
            if j < 4:
                dma("pool", AQT[128 * j:128 * j + 128, t0:t0 + 512], qo[:], bqo, reads=[bqo], writes=[bAQT])
            else:
                dma("pool", AKT[:, t0:t0 + 512], qo[:], bqo, reads=[bqo], writes=[bAKT])
        for ti in range(4):
            tile = blk * 4 + ti
            r0 = tile * 128
            lhs = lambda c, ti=ti: xnT[:, c, ti * 128:(ti + 1) * 128]

            def ftm(e, lhs=lhs):
                ins = None
                for c in range(8):
                    ins = e.matmul(ptm1[:, 0:256], lhsT=lhs(c), rhs=wb[:, c, 256:512], start=(c == 0), stop=(c == 7))
                for c in range(8):
                    ins = e.matmul(ptm1[:, 256:384], lhsT=lhs(c), rhs=wb[:, c, 2208:2336], start=(c == 0), stop=(c == 7))
                return ins
            op("pe", ftm, reads=[bwb, bxnT], writes=[bptm1])

            def ftm2(e, lhs=lhs):
                ins = None
                for c in range(8):
                    ins = e.matmul(ptm2[:], lhsT=lhs(c), rhs=wb[:, c, 512:1024], start=(c == 0), stop=(c == 7))
                return ins
            op("pe", ftm2, reads=[bwb, bxnT], writes=[bptm2])
            op("pe", lambda e, ti=ti: e.matmul(pz[:], lhsT=raug[:, ti * 128:(ti + 1) * 128], rhs=wgb[:], start=True, stop=True),
               reads=[braug, bwgb], writes=[bpz])
            gk_t, bgk_t = rr(gko, tile)
            gv_t, bgv_t = rr(gvo, tile)
            av_t, bav_t = rr(avo, tile)
            g_t, bg_t = rr(go, tile)
            op("dve", lambda e, gk_t=gk_t: e.tensor_copy(out=gk_t[:], in_=ptm1[:, 0:256]), reads=[bptm1], writes=[bgk_t])
            op("dve", lambda e, av_t=av_t: e.tensor_copy(out=av_t[:, :, 0:64], in_=ptm1[:, 256:384].rearrange("p (g d) -> p g d", g=2)),
               reads=[bptm1], writes=[bav_t])
            op("act", lambda e, gv_t=gv_t: e.copy(out=gv_t[:], in_=ptm2[:]), reads=[bptm2], writes=[bgv_t])
            op("act", lambda e: e.activation(out=ge[:], in_=pz[:], func=AF.Exp, scale=-1.0), reads=[bpz], writes=[bge])
            op("act", lambda e: e.activation(out=ge[:], in_=ge[:], func=AF.Ln, bias=1.0), reads=[bge], writes=[bge])
            op("dve", lambda e, g_t=g_t: e.tensor_scalar(out=g_t[:], in0=ge[:], scalar1=-1.0 / 16.0, scalar2=None, op0=ALU.mult),
               reads=[bge], writes=[bg_t])
            dma("pool", GK[r0:r0 + 128, :], gk_t[:], bgk_t, reads=[bgk_t], writes=[bGK])
            dma("pool", GV[r0:r0 + 128, :], gv_t[:], bgv_t, reads=[bgv_t], writes=[bGV])
            dma("pool", AV[r0:r0 + 128, :], av_t[:].rearrange("p g d -> p (g d)"), bav_t, reads=[bav_t], writes=[bAV])
            dma("pool", G[r0:r0 + 128, :], g_t[:], bg_t, reads=[bg_t], writes=[bG])
    for blk in range(NB + 1):
        if blk < NB:
            stage_a1(blk)
        if blk >= 1:
            stage_proj(blk - 1)
    A.close()
    if stop == "A":
        return fin()

    Gp = Phase(P)
    P.defer = "G" in SCHED
    S32 = [Gp.sb("S32_%d" % i, [128, 128], F32) for i in range(2)]
    Sbf = [Gp.sb("Sbf_%d" % i, [128, 128], BF16) for i in range(2)]
    onesb, bonesb = Gp.sb("onesb", [128, 128], BF16)
    op("dve", lambda e: e.memset(onesb[:], 1.0), writes=[bonesb])
    NGB = 2
    g_g = [Gp.sb("g_g%d" % i, [128, 4, 256], F32) for i in range(NGB)]
    k_g = [Gp.sb("k_g%d" % i, [128, 4, 256], F32) for i in range(NGB)]
    v_g = [Gp.sb("v_g%d" % i, [128, 4, 512], BF16) for i in range(NGB)]
    qT_g = [Gp.sb("qT_g%d" % i, [128, 2, 512], F32) for i in range(NGB)]
    kT_g = [Gp.sb("kT_g%d" % i, [128, 2, 512], F32) for i in range(NGB)]
    of_g = [Gp.sb("of_g%d" % i, [128, 4, 512], F32) for i in range(NGB)]
    gg_g = [Gp.sb("gg_g%d" % i, [128, 4, 512], F32) for i in range(NGB)]
    mx_g = [Gp.sb("mx_g%d" % i, [128, 4, 512], BF16) for i in range(NGB)]
    E1, bE1 = Gp.sb("E1", [128, 256], F32)
    kend, bkend = Gp.sb("kend", [128, 256], BF16)
    Eb, bEb = Gp.sb("Eb", [128, 2, 128], F32)
    Enb, bEnb = Gp.sb("Enb", [128, 2, 128], F32)
    qtT = [Gp.sb("qtT%d" % i, [128, 2, 2, 128], BF16) for i in range(2)]
    for t_, b_ in qtT:
        op("dve", lambda e, t_=t_: e.memset(t_[:], 0.0), writes=[b_])
    ktT, bktT = Gp.sb("ktT", [128, 2, 128], BF16)
    Am = [Gp.sb("Am%d" % i, [128, 4, 128], BF16) for i in range(2)]
    osum, bosum = Gp.sb("osum", [128, 4, 128], F32)
    atf, batf = Gp.sb("atf", [128, 4, 128], F32)
    scs, bscs = Gp.sb("scs", [128, 4, 128], F32)
    osq, bosq = Gp.sb("osq", [128, 4, 128], BF16)
    orst, borst = Gp.sb("orst", [128, 4, 128], F32)
    gsig, bgsig = Gp.sb("gsig", [128, 4, 128], F32)
    p_rb, bp_rb = Gp.ps("p_rb", [128, 512], F32)
    p_at = [Gp.ps("p_at%d" % i, [128, 4, 128], F32) for i in range(2)]
    p_o = [Gp.ps("p_o%d" % i, [128, 4, 128], F32) for i in range(2)]
    p_sc, bp_sc = Gp.ps("p_sc", [128, 4, 128], F32)
    p_ss, bp_ss = Gp.ps("p_ss", [128, 4, 128], F32)

    NG = L // 512
    kends = [Gp.sb("kend_%d" % i, [128, 256], BF16) for i in range(2)]
    Ebs = [Gp.sb("Eb_%d" % i, [128, 2, 128], F32) for i in range(2)]
    for dr in range(2):
        for i in range(2):
            op("dve", lambda e, i=i: e.memset(S32[i][0][:], 0.0), writes=[S32[i][1]])
            op("dve", lambda e, i=i: e.memset(Sbf[i][0][:], 0.0), writes=[Sbf[i][1]])
        chunks = []
        for gi in range(QB if dr == 0 else NG):
            grp = gi if dr == 0 else NG - 1 - gi
            for cj in range(4):
                chunks.append(dict(gi=gi, grp=grp, full=(grp < QB), cj=cj, ci=(cj if dr == 0 else 3 - cj), idx=len(chunks)))
        gctx = {}
        dcol = 127 if dr == 0 else 0

        def load_group(ch):
            gi, grp, full = ch["gi"], ch["grp"], ch["full"]
            t0 = grp * 512
            gsel = dr * NG + gi
            c = dict(t0=t0)
            c["gt"], c["bgt"] = rr(g_g, gsel)
            c["kt"], c["bkt"] = rr(k_g, gsel)
            c["vt"], c["bvt"] = rr(v_g, gsel)
            dma("sp", c["gt"][:], G[t0:t0 + 512, dr * 256:(dr + 1) * 256].rearrange("(n p) c -> p n c", p=128), c["bgt"], reads=[bG], writes=[c["bgt"]])
            dma("sp", c["kt"][:], GK[t0:t0 + 512, :].rearrange("(n p) c -> p n c", p=128), c["bkt"], reads=[bGK], writes=[c["bkt"]])
            dma("sp", c["vt"][:], GV[t0:t0 + 512, :].rearrange("(n p) c -> p n c", p=128), c["bvt"], reads=[bGV], writes=[c["bvt"]])
            if full:
                c["qTt"], c["bqTt"] = rr(qT_g, gsel)
                c["kTt"], c["bkTt"] = rr(kT_g, gsel)
                dma("sp", c["qTt"][:], GQT[:, t0:t0 + 512].rearrange("(h p) t -> p h t", p=128), c["bqTt"], reads=[bGQT], writes=[c["bqTt"]])
                dma("sp", c["kTt"][:], GKT[:, t0:t0 + 512].rearrange("(h p) t -> p h t", p=128), c["bkTt"], reads=[bGKT], writes=[c["bkTt"]])
                c["oft"], c["boft"] = rr(of_g, gsel)
                if dr == 1:
                    c["ggt"], c["bggt"] = rr(gg_g, gsel)
                    c["mxt"], c["bmxt"] = rr(mx_g, gsel)
                    dma("sp", c["oft"][:], OF[:, t0:t0 + 512].rearrange("(h p) t -> p h t", p=128), c["boft"], reads=[bOF], writes=[c["boft"]])
                    dma("sp", c["ggt"][:], GGT[:, t0:t0 + 512].rearrange("(h p) t -> p h t", p=128), c["bggt"], reads=[bGGT], writes=[c["bggt"]])
            gctx[gi] = c

        def stage1(ch):
            if ch["cj"] == 0:
                load_group(ch)
            c = gctx[ch["gi"]]
            full, ci, idx = ch["full"], ch["ci"], ch["idx"]
            gt, bgt, kt, bkt = c["gt"], c["bgt"], c["kt"], c["bkt"]
            kend, bkend = rr(kends, idx)
            Eb, bEb = rr(Ebs, idx)
            tc0 = ci * 128

            def fm1(e):
                e.matmul(p_rb[:, 0:256], lhsT=LSf[dr], rhs=gt[:, ci, :], start=True, stop=True)
                ins = None
                for hp in range(2):
                    if full:
                        ins = e.matmul(p_rb[:, 256 + hp * 128:256 + (hp + 1) * 128], lhsT=gt[:, ci, hp * 128:(hp + 1) * 128], rhs=UIf[dr], start=True, stop=True)
                    else:
                        ins = e.matmul(p_rb[:, 256 + hp * 128 + dcol:256 + hp * 128 + dcol + 1], lhsT=gt[:, ci, hp * 128:(hp + 1) * 128], rhs=UIf[dr][:, dcol:dcol + 1], start=True, stop=True)
                return ins
            op("pe", fm1, reads=[bgt, bcst], writes=[bp_rb])
            op("act", lambda e: e.activation(out=E1[:], in_=p_rb[:, 0:256], func=AF.Exp), reads=[bp_rb], writes=[bE1])
            pbv = p_rb[:, 256:512].rearrange("p (h t) -> p h t", h=2)
            if full:
                op("act", lambda e: e.activation(out=Eb[:], in_=pbv, func=AF.Exp), reads=[bp_rb], writes=[bEb])
                op("act", lambda e: e.activation(out=Enb[:], in_=pbv, func=AF.Exp, scale=-1.0), reads=[bp_rb], writes=[bEnb])
            else:
                op("act", lambda e: e.activation(out=Eb[:, :, dcol:dcol + 1], in_=pbv[:, :, dcol:dcol + 1], func=AF.Exp), reads=[bp_rb], writes=[bEb])
            op("dve", lambda e: e.tensor_tensor(out=kend[:], in0=kt[:, ci, :], in1=E1[:], op=ALU.mult), reads=[bkt, bE1], writes=[bkend])
            if not full:
                return
            qTt, bqTt, kTt, bkTt = c["qTt"], c["bqTt"], c["kTt"], c["bkTt"]
            qq, bqq = rr(qtT, idx)
            op("dve", lambda e: e.tensor_tensor(out=qq[0:64, 0, :, :], in0=qTt[0:64, :, tc0:tc0 + 128], in1=Eb[0:64, :, :], op=ALU.mult), reads=[bqTt, bEb], writes=[bqq])
            op("dve", lambda e: e.tensor_tensor(out=qq[64:128, 1, :, :], in0=qTt[64:128, :, tc0:tc0 + 128], in1=Eb[64:128, :, :], op=ALU.mult), reads=[bqTt, bEb], writes=[bqq])
            op("dve", lambda e: e.tensor_tensor(out=ktT[:], in0=kTt[:, :, tc0:tc0 + 128], in1=Enb[:], op=ALU.mult), reads=[bkTt, bEnb], writes=[bktT])
            pat, bpat = rr(p_at, idx)

            def fm4(e):
                ins = None
                for h in range(4):
                    hp, h2 = h // 2, h % 2
                    ins = e.matmul(pat[:, h, :], lhsT=ktT[:, hp, :], rhs=qq[:, h2, hp, :], start=True, stop=True)
                return ins
            op("pe", fm4, reads=[bktT, bqq], writes=[bpat])
            am, bam = rr(Am, idx)
            op("act", lambda e: e.copy(out=atf[:], in_=pat[:]), reads=[bpat], writes=[batf])
            for h in range(4):
                op("pool", lambda e, h=h: e.tensor_tensor(out=am[:, h, :], in0=atf[:, h, :], in1=UIf[dr], op=ALU.mult),
                   reads=[batf, bcst], writes=[bam])

        def stage2(ch):
            c = gctx[ch["gi"]]
            full, ci, idx = ch["full"], ch["ci"], ch["idx"]
            vt, bvt = c["vt"], c["bvt"]
            kend, bkend = rr(kends, idx)
            Eb, bEb = rr(Ebs, idx)
            tc0 = ci * 128
            if full:
                qq, bqq = rr(qtT, idx)
                am, bam = rr(Am, idx)
                po, bpo = rr(p_o, idx)

                def fm5(e):
                    ins = None
                    for h in range(4):
                        hp, h2 = h // 2, h % 2
                        e.matmul(po[:, h, :], lhsT=vt[:, ci, h * 128:(h + 1) * 128], rhs=am[:, h, :], start=True, stop=False)
                        ins = e.matmul(po[:, h, :], lhsT=Sbf[hp][0][:, :], rhs=qq[:, h2, hp, :], start=False, stop=True)
                    return ins
                op("pe", fm5, reads=[bam, bvt, bqq, Sbf[0][1], Sbf[1][1]], writes=[bpo])

            def fm2(e):
                ins = None
                for h in range(4):
                    hp = h // 2
                    ins = e.matmul(p_sc[:, h, :], lhsT=kend[:, hp * 128:(hp + 1) * 128], rhs=vt[:, ci, h * 128:(h + 1) * 128], start=True, stop=True)
                return ins
            op("pe", fm2, reads=[bkend, bvt], writes=[bp_sc])
            op("act", lambda e: e.copy(out=scs[:], in_=p_sc[:]), reads=[bp_sc], writes=[bscs])
            for h in range(4):
                hp, h2 = h // 2, h % 2
                sl = slice(64 * h2, 64 * h2 + 64)
                op("dve", lambda e, hp=hp, sl=sl, h=h: e.scalar_tensor_tensor(out=S32[hp][0][sl, :], in0=S32[hp][0][sl, :], scalar=Eb[sl, hp, dcol:dcol + 1], in1=scs[sl, h, :], op0=ALU.mult, op1=ALU.add),
                   reads=[bEb, bscs], writes=[S32[hp][1]])
            for hp in range(2):
                op("act", lambda e, hp=hp: e.copy(out=Sbf[hp][0][:], in_=S32[hp][0][:]), reads=[S32[hp][1]], writes=[Sbf[hp][1]], est=0.4)
            if not full:
                return
            oft, boft = c["oft"], c["boft"]
            t0 = c["t0"]
            if dr == 0:
                op("act", lambda e: e.copy(out=oft[:, :, tc0:tc0 + 128], in_=po[:]), reads=[bpo], writes=[boft])
            else:
                ggt, bggt, mxt, bmxt = c["ggt"], c["bggt"], c["mxt"], c["bmxt"]
                op("act", lambda e: e.copy(out=osum[:], in_=po[:]), reads=[bpo], writes=[bosum])
                op("dve", lambda e: e.tensor_tensor(out=osum[:], in0=osum[:], in1=oft[:, :, tc0:tc0 + 128], op=ALU.add), reads=[bosum, boft], writes=[bosum])
                op("act", lambda e: e.activation(out=osq[:], in_=osum[:], func=AF.Square), reads=[bosum], writes=[bosq])
                op("pe", lambda e: e.matmul(p_ss[:], lhsT=onesb[:], rhs=osq[:], start=True, stop=True), reads=[bosq, bonesb], writes=[bp_ss])
                op("act", lambda e: e.activation(out=orst[:], in_=p_ss[:], func=AF.Ln, scale=1.0 / 128, bias=EPS), reads=[bp_ss], writes=[borst])
                op("act", lambda e: e.activation(out=orst[:], in_=orst[:], func=AF.Exp, scale=-0.5), reads=[borst], writes=[borst])
                op("act", lambda e: e.activation(out=gsig[:], in_=ggt[:, :, tc0:tc0 + 128], func=AF.Exp, scale=-1.0), reads=[bggt], writes=[bgsig])
                op("act", lambda e: e.activation(out=gsig[:], in_=gsig[:], func=AF.Ln, bias=1.0), reads=[bgsig], writes=[bgsig])
                op("act", lambda e: e.activation(out=gsig[:], in_=gsig[:], func=AF.Exp, scale=-1.0), reads=[bgsig], writes=[bgsig])
                op("dve", lambda e: e.tensor_tensor(out=gsig[:], in0=gsig[:], in1=ggt[:, :, tc0:tc0 + 128], op=ALU.mult), reads=[bgsig, bggt], writes=[bgsig])
                op("dve", lambda e: e.scalar_tensor_tensor(out=osum[:], in0=osum[:], scalar=nrm_sb[:, 4:5], in1=orst[:], op0=ALU.mult, op1=ALU.mult), reads=[bosum, borst, bnrm], writes=[bosum])
                op("dve", lambda e: e.tensor_tensor(out=mxt[:, :, tc0:tc0 + 128], in0=osum[:], in1=gsig[:], op=ALU.mult), reads=[bosum, bgsig], writes=[bmxt])
            if ch["cj"] == 3:
                if dr == 0:
                    dma("pool", OF[:, t0:t0 + 512].rearrange("(h p) t -> p h t", p=128), oft[:], boft, reads=[boft], writes=[bOF])
                else:
                    dma("pool", MIXT[0:512, t0:t0 + 512].rearrange("(h p) t -> p h t", p=128), mxt[:], bmxt, reads=[bmxt], writes=[bMIXT])

        for i in range(len(chunks) + 1):
            if i < len(chunks):
                stage1(chunks[i])
            if i >= 1:
                stage2(chunks[i - 1])
        P.flush()
    Gp.close()
    if stop == "G":
        return fin()

    P.flush()
    P.defer = False
    W2a = Phase(P)
    wu_b, bwu_b = W2a.sb("wu_b", [128, 8, 2 * DFF], BF16)
    lnf, blnf = W2a.sb("lnf", [128, 8], F32)
    dma("sp", lnf[:], ln_ffn[:, :], blnf, writes=[blnf])
    W2b = Phase(P)
    stg2 = [W2b.sb("stg2_%d" % i, [128, 1408], F32) for i in range(2)]

    def load_wu_chunk(k):
        c, q4 = k // 4, k % 4
        st, bst = rr(stg2, k)
        dma("pool", st[:], w_up[c * 128:(c + 1) * 128, q4 * 1408:(q4 + 1) * 1408], bst, writes=[bst])
        op("dve", lambda e, st=st, c=c, q4=q4: e.tensor_scalar(out=wu_b[:, c, q4 * 1408:(q4 + 1) * 1408], in0=st[:], scalar1=lnf[:, c:c + 1], scalar2=None, op0=ALU.mult),
           reads=[bst, blnf], writes=[bwu_b])
    B = Phase(P)
    P.defer = "B" in SCHED
    KT, bKT = B.sb("KT", [128, L], BF16)
    VA, bVA = B.sb("VA", [128, NT, 130], BF16)
    s65, bs65 = B.sb("s65", [65, 64], F32)
    dma("sp", s65[:], sel65[:, :], bs65, writes=[bs65])
    NCH = 4 if NT % 32 == 0 else (2 if NT % 16 == 0 else 1)
    TPC = NT // NCH
    bKTc = [Buf("KTc%d" % i) for i in range(NCH)]
    bVAc = [Buf("VAc%d" % i) for i in range(NCH)]
    for ch in range(NCH):
        t_lo, t_hi = ch * TPC, (ch + 1) * TPC
        dma("sp", KT[:, t_lo * 128:t_hi * 128], AKT[:, t_lo * 128:t_hi * 128], bKTc[ch], reads=[bAKT], writes=[bKTc[ch]])
        for c0 in range(t_lo, t_hi, 8):
            c1 = min(t_hi, c0 + 8)
            dma("sp", VA[:, c0:c1, :], AV[c0 * 128:c1 * 128, :].rearrange("(n p) c -> p n c", p=128), bVAc[ch], reads=[bAV], writes=[bVAc[ch]])
    QTg = [[B.sb("QT%d_%d" % (g_, i), [128, 512], BF16) for i in range(4)] for g_ in range(2)]
    for g_ in range(2):
        for t_, b_ in QTg[g_]:
            op("dve", lambda e, t_=t_: e.memset(t_[:], 0.0), writes=[b_])
    pst = [B.ps("pst%d" % i, [128, 3, 512], F32) for i in range(2)]
    pTs = [B.sb("pT%d" % i, [128, 3, 512], BF16) for i in range(3)]
    oaccs = [B.ps("oacc%d" % i, [128, 512], F32) for i in range(2)]
    osb = [B.sb("osb%d" % i, [65, 512], F32) for i in range(2)]
    rcp, brcp = B.sb("rcp", [64, 512], F32)
    oat = [B.sb("oat%d" % i, [64, 512], BF16) for i in range(2)]
    groups = []
    k0 = 0
    while k0 < NT:
        n = min(3, NT - k0)
        groups.append((k0, n))
        k0 += n
    items = []
    it = 0
    for (t0, nq) in qblocks:
        for h in range(8):
            for gi_, (k0, n) in enumerate(groups):
                items.append((it, h, t0, nq, gi_, k0, n))
            it += 1
    state = {}

    first_idx = {}
    for idx_, itm in enumerate(items):
        first_idx.setdefault(itm[0], idx_)
    n_items = it

    def load_q(it_):
        if it_ >= n_items or it_ in state:
            return
        _, h, t0, nq, _, _, _ = items[first_idx[it_]]
        g = h // 4
        qt, bqt = rr(QTg[g], it_)
        dma("sp", qt[64 * g:64 * g + 64, 0:nq], AQT[64 * h:64 * h + 64, t0:t0 + nq], bqt, reads=[bAQT], writes=[bqt])
        state[it_] = (qt, bqt)

    def emit_s(idx):
        it_, h, t0, nq, gi_, k0, n = items[idx]
        g = h // 4
        if gi_ == 0:
            load_q(it_)
            load_q(it_ + 1)
            if it_ < 32:
                load_wu_chunk(it_)
        qt, bqt = state[it_]
        st, bst = rr(pst, idx)
        pT, bpT = rr(pTs, idx)

        def fs(e):
            ins = None
            for j in range(n):
                ins = e.matmul(st[:, j, 0:nq], lhsT=KT[:, (k0 + j) * 128:(k0 + j + 1) * 128], rhs=qt[:, 0:nq], start=True, stop=True)
            return ins
        op("pe", fs, reads=[bKTc[c_] for c_ in sorted({(k0 + j) // TPC for j in range(n)})] + [bqt], writes=[bst])
        op("act", lambda e: e.activation(out=pT[:, 0:n, 0:nq], in_=st[:, 0:n, 0:nq], func=AF.Exp, scale=0.125), reads=[bst], writes=[bpT])

    def emit_pv(idx):
        it_, h, t0, nq, gi_, k0, n = items[idx]
        g = h // 4
        pT, bpT = rr(pTs, idx)
        oacc, boacc = rr(oaccs, it_)
        first = (gi_ == 0)
        last = (gi_ == len(groups) - 1)

        def fpv(e):
            ins = None
            for j in range(n):
                ins = e.matmul(oacc[0:65, 0:nq], lhsT=VA[:, k0 + j, 65 * g:65 * g + 65], rhs=pT[:, j, 0:nq], start=(first and j == 0), stop=(last and j == n - 1))
            return ins
        op("pe", fpv, reads=[bVAc[c_] for c_ in sorted({(k0 + j) // TPC for j in range(n)})] + [bpT], writes=[boacc])
        if last:
            ob, bob = rr(osb, it_)
            oa, boa = rr(oat, it_)
            op("dve", lambda e: e.tensor_copy(out=ob[:, 0:nq], in_=oacc[0:65, 0:nq]), reads=[boacc], writes=[bob])
            op("pe", lambda e: e.matmul(oacc[0:64, 0:nq], lhsT=s65[:], rhs=ob[:, 0:nq], start=True, stop=True), reads=[bob, bs65], writes=[boacc])
            op("dve", lambda e: e.reciprocal(out=rcp[:, 0:nq], in_=oacc[0:64, 0:nq]), reads=[boacc], writes=[brcp])
            op("dve", lambda e: e.tensor_tensor(out=oa[:, 0:nq], in0=ob[0:64, 0:nq], in1=rcp[:, 0:nq], op=ALU.mult), reads=[bob, brcp], writes=[boa])
            dma("pool", MIXT[512 + 64 * h:512 + 64 * h + 64, t0:t0 + nq], oa[:, 0:nq], boa, reads=[boa], writes=[bMIXT])

    for idx in range(len(items) + 1):
        if idx < len(items):
            emit_s(idx)
        if idx >= 1:
            emit_pv(idx - 1)
    for k_ in range(min(32, n_items), 32):
        load_wu_chunk(k_)
    B.close()
    W2b.close()
    wd_b, bwd_b = W2a.sb("wd_b", [128, 22, D], BF16)
    if stop == "B":
        W2a.close()
        return fin()

    C1 = Phase(P)
    P.defer = "C1" in SCHED
    wo_b, bwo_b = C1.sb("wo_b", [128, 8, D], BF16)
    stg1 = [C1.sb("stg1_%d" % i, [128, D], F32) for i in range(2)]
    for c in range(8):
        st, bst = rr(stg1, c)
        dma("sp", st[:], w_out[c * 128:(c + 1) * 128, :], bst, writes=[bst])
        op("dve", lambda e, st=st, c=c: e.tensor_copy(out=wo_b[:, c, :], in_=st[:]), reads=[bst], writes=[bwo_b])
    for f in range(22):
        st, bst = rr(stg1, f)
        dma("sp", st[:], w_down[f * 128:(f + 1) * 128, :], bst, writes=[bst])
        op("dve", lambda e, st=st, f=f: e.tensor_copy(out=wd_b[:, f, :], in_=st[:]), reads=[bst], writes=[bwd_b])
    zc, bzc = C1.sb("zc", [128, 8, 1], BF16)
    op("dve", lambda e: e.memset(zc[:], 0.0), writes=[bzc])
    dma("pool", H2T[:, 0:1].rearrange("(c p) t -> p c t", p=128), zc[:], bzc, reads=[bzc], writes=[bH2T], allow_slow_non_contiguous=True)
    dma("pool", H2T[:, L + 1:L + 2].rearrange("(c p) t -> p c t", p=128), zc[:], bzc, reads=[bzc], writes=[bH2T], allow_slow_non_contiguous=True)
    mixs = [C1.sb("mix%d" % i, [128, 8, 256], BF16) for i in range(2)]
    xs = [C1.sb("xs%d" % i, [128, D], F32) for i in range(2)]
    x1s = [C1.sb("x1s%d" % i, [128, D], F32) for i in range(2)]
    junk1, bjunk1 = C1.sb("junk1", [128, D], BF16)
    ssq1 = [C1.sb("ssq1_%d" % i, [128, 1], F32) for i in range(2)]
    h2s = [C1.sb("h2s%d" % i, [128, D], BF16) for i in range(2)]
    h2Ts = [C1.sb("h2T%d" % i, [128, 8, 256], BF16) for i in range(2)]
    pc1 = [C1.ps("pc1_%d" % i, [128, 2, 512], F32) for i in range(2)]
    ptr1 = [C1.ps("ptr1_%d" % i, [128, 8, 128], BF16) for i in range(2)]
    def c1_tile(tile, m, pair, mo):
        r0 = tile * 128
        mx_, bmx = rr(mixs, pair)
        mx = mx_[:, :, mo:mo + 128]
        xt, bxt = rr(xs, tile)
        x1, bx1 = rr(x1s, tile)
        sq, bsq = rr(ssq1, tile)
        h2, bh2 = rr(h2s, tile)
        h2T_, bh2T = rr(h2Ts, pair)
        h2T = h2T_[:, :, mo:mo + 128]
        pc, bpc = rr(pc1, tile)
        pt, bpt = rr(ptr1, tile)
        dma("sp", xt[0:m, :], x[r0:r0 + m, :], bxt, writes=[bxt])

        def fo(e):
            ins = None
            for hf in range(2):
                for c in range(8):
                    ins = e.matmul(pc[0:m, hf, :], lhsT=mx[:, c, 0:m], rhs=wo_b[:, c, hf * 512:(hf + 1) * 512], start=(c == 0), stop=(c == 7))
            return ins
        op("pe", fo, reads=[bmx, bwo_b], writes=[bpc])
        op("act", lambda e: e.copy(out=x1[0:m, :], in_=pc[0:m].rearrange("p a b -> p (a b)")), reads=[bpc], writes=[bx1])
        op("dve", lambda e: e.tensor_tensor(out=x1[0:m, :], in0=x1[0:m, :], in1=xt[0:m, :], op=ALU.add), reads=[bx1, bxt], writes=[bx1])
        dma("pool", X1[r0:r0 + m, :], x1[0:m, :], bx1, reads=[bx1], writes=[bX1])
        op("act", lambda e: e.activation(out=junk1[0:m, :], in_=x1[0:m, :], func=AF.Square, accum_out=sq[0:m, :]), reads=[bx1], writes=[bjunk1, bsq])
        rstd_from_ssq(None, sq[0:m, :], bsq, D)
        op("dve", lambda e: e.tensor_scalar(out=h2[0:m, :], in0=x1[0:m, :], scalar1=sq[0:m, 0:1], scalar2=None, op0=ALU.mult), reads=[bx1, bsq], writes=[bh2])

        def ftr1(e):
            ins = None
            for c in range(8):
                ins = e.transpose(out=pt[:, c, 0:m], in_=h2[0:m, c * 128:(c + 1) * 128], identity=cstb[0:m, 0:m])
            return ins
        op("pe", ftr1, reads=[bh2, bcstb], writes=[bpt])
        op("act", lambda e: e.copy(out=h2T[:, :, 0:m], in_=pt[:, :, 0:m]), reads=[bpt], writes=[bh2T])

    def c1_pair(pair, tiles):
        r0 = tiles[0][0] * 128
        ncol = sum(m_ for _, m_ in tiles) if len(tiles) == 1 else 128 + tiles[1][1]
        mx_, bmx = rr(mixs, pair)
        h2T_, bh2T = rr(h2Ts, pair)
        dma("sp", mx_[:, :, 0:ncol], MIXT[:, r0:r0 + ncol].rearrange("(c p) t -> p c t", p=128), bmx, reads=[bMIXT], writes=[bmx])
        for i_, (tile, m_) in enumerate(tiles):
            c1_tile(tile, m_, pair, i_ * 128)
        dma("pool", H2T[:, 1 + r0:1 + r0 + ncol].rearrange("(c p) t -> p c t", p=128), h2T_[:, :, 0:ncol], bh2T, reads=[bh2T], writes=[bH2T])

    tl = [(tile, EXTQ if (H is not None and tile == C1T - 1) else 128) for tile in range(C1T)]
    for pair in range((C1T + 1) // 2):
        c1_pair(pair, tl[2 * pair:2 * pair + 2])
    C1.close()
    if stop == "C1":
        W2a.close()
        return fin()

    C2 = Phase(P)
    P.defer = "C2" in SCHED
    lfb, blfb = C2.sb("lfb", [128, D], F32)
    dma("sp", lfb[:], lnf_bc[:, :], blfb, writes=[blfb])
    cvp, bcvp = C2.sb("cvp", [128, 44, 4], F32)
    dma("sp", cvp[:], convp[:, :, :], bcvp, writes=[bcvp])
    TBW = 254
    hws = [C2.sb("hw%d" % i, [128, 8, 256], BF16) for i in range(2)]
    acs = [C2.sb("acs%d" % i, [128, 256], BF16) for i in range(6)]
    pu = [C2.ps("pu%d" % i, [128, 512], F32) for i in range(4)]
    pyt, bpyt = C2.ps("py", [128, 2, 2, 512], F32)
    tas = [C2.sb("ta%d" % i, [128, 256], F32) for i in range(2)]
    tgs = [C2.sb("tg%d" % i, [128, 256], F32) for i in range(2)]
    sgs = [C2.sb("sg%d" % i, [128, 256], F32) for i in range(2)]
    usb = [C2.sb("usb%d" % i, [128, 256], F32) for i in range(4)]
    x1l = [C2.sb("x1l%d" % i, [128, D], F32) for i in range(2)]
    yo = [C2.sb("yo%d" % i, [128, D], F32) for i in range(2)]
    junk2, bjunk2 = C2.sb("junk2", [128, D], BF16)
    ssq2 = [C2.sb("ssq2_%d" % i, [128, 1], F32) for i in range(2)]
    blocks = []
    t = 0
    while t < HO:
        n = min(TBW, HO - t)
        blocks.append((t, n))
        t += n
    nonlocal_puc = [0]
    slc = 0
    slc_ = [0]

    def do_block(bi, t0, n):
            hw, bhw = rr(hws, bi)
            dma("sp", hw[:, :, 0:n + 2], H2T[:, t0:t0 + n + 2].rearrange("(c p) t -> p c t", p=128), bhw, reads=[bH2T], writes=[bhw])
            stash = {}
            slices = []
            s0_ = 0
            while s0_ < n:
                m_ = min(128, n - s0_)
                slices.append((s0_, m_))
                s0_ += m_

            def stage1(f):
                res = []
                for part in range(2):
                    nonlocal_puc[0] += 1
                    pc_ = nonlocal_puc[0]
                    col0 = part * DFF + f * 128
                    pp, bpp = rr(pu, pc_)

                    def fu(e, pp=pp, col0=col0):
                        ins = None
                        for c in range(8):
                            ins = e.matmul(pp[:, 0:n + 2], lhsT=wu_b[:, c, col0:col0 + 128], rhs=hw[:, c, 0:n + 2], start=(c == 0), stop=(c == 7))
                        return ins
                    op("pe", fu, reads=[bwu_b, bhw], writes=[bpp])
                    tt, btt = rr(tas if part == 0 else tgs, f)
                    fc = part * 22 + f
                    us, bus = rr(usb, pc_)
                    op("act", lambda e, us=us, pp=pp: e.copy(out=us[:, 0:n + 2], in_=pp[:, 0:n + 2]), reads=[bpp], writes=[bus])
                    op("dve", lambda e, tt=tt, us=us, fc=fc: e.tensor_scalar(out=tt[:, 0:n], in0=us[:, 1:n + 1], scalar1=cvp[:, fc, 1:2], scalar2=cvp[:, fc, 3:4], op0=ALU.mult, op1=ALU.add),
                       reads=[bus, bcvp], writes=[btt])
                    op("dve", lambda e, tt=tt, us=us, fc=fc: e.scalar_tensor_tensor(out=tt[:, 0:n], in0=us[:, 0:n], scalar=cvp[:, fc, 0:1], in1=tt[:, 0:n], op0=ALU.mult, op1=ALU.add),
                       reads=[bus, bcvp, btt], writes=[btt])
                    op("dve", lambda e, tt=tt, us=us, fc=fc: e.scalar_tensor_tensor(out=tt[:, 0:n], in0=us[:, 2:n + 2], scalar=cvp[:, fc, 2:3], in1=tt[:, 0:n], op0=ALU.mult, op1=ALU.add),
                       reads=[bus, bcvp, btt], writes=[btt])
                    res.append((tt, btt))
                stash[f] = res

            def stage2(f):
                (ta, bta), (tg, btg) = stash.pop(f)
                sg, bsg = rr(sgs, f)
                op("act", lambda e: e.activation(out=sg[:, 0:n], in_=tg[:, 0:n], func=AF.Exp, scale=-1.0), reads=[btg], writes=[bsg])
                op("act", lambda e: e.activation(out=sg[:, 0:n], in_=sg[:, 0:n], func=AF.Ln, bias=1.0), reads=[bsg], writes=[bsg])
                op("act", lambda e: e.activation(out=sg[:, 0:n], in_=sg[:, 0:n], func=AF.Exp, scale=-1.0), reads=[bsg], writes=[bsg])
                op("pool", lambda e: e.tensor_tensor(out=tg[:, 0:n], in0=tg[:, 0:n], in1=sg[:, 0:n], op=ALU.mult), reads=[bsg, btg], writes=[btg])
                ac, bac = rr(acs, f)
                op("pool", lambda e: e.tensor_tensor(out=ac[:, 0:n], in0=ta[:, 0:n], in1=tg[:, 0:n], op=ALU.mult), reads=[bta, btg], writes=[bac])

            def stage3(f):
                ac, bac = rr(acs, f)

                def fd(e):
                    ins = None
                    for si, (s0, m) in enumerate(slices):
                        for hf in range(2):
                            ins = e.matmul(pyt[0:m, si, hf, :], lhsT=ac[:, s0:s0 + m], rhs=wd_b[:, f, hf * 512:(hf + 1) * 512], start=(f == 0), stop=(f == 21))
                    return ins
                op("pe", fd, reads=[bac, bwd_b], writes=[bpyt])

            for f in range(22 + 3):
                if f < 22:
                    stage1(f)
                if 0 <= f - 1 < 22:
                    stage2(f - 1)
                if 0 <= f - 3 < 22:
                    stage3(f - 3)
            for si, (s0, m) in enumerate(slices):
                r0 = t0 + s0
                xl, bxl = rr(x1l, slc_[0])
                yt_, byt = rr(yo, slc_[0])
                sq, bsq = rr(ssq2, slc_[0])
                slc_[0] += 1
                dma("sp", xl[0:m, :], X1[r0:r0 + m, :], bxl, reads=[bX1], writes=[bxl])
                op("act", lambda e, yt_=yt_, m=m, si=si: e.copy(out=yt_[0:m, :], in_=pyt[0:m, si].rearrange("p a b -> p (a b)")), reads=[bpyt], writes=[byt])
                op("dve", lambda e, yt_=yt_, xl=xl, m=m: e.tensor_tensor(out=yt_[0:m, :], in0=yt_[0:m, :], in1=xl[0:m, :], op=ALU.add),
                   reads=[byt, bxl], writes=[byt])
                op("act", lambda e, yt_=yt_, sq=sq, m=m: e.activation(out=junk2[0:m, :], in_=yt_[0:m, :], func=AF.Square, accum_out=sq[0:m, :]), reads=[byt], writes=[bjunk2, bsq])
                rstd_from_ssq(None, sq[0:m, :], bsq, D)
                op("dve", lambda e, yt_=yt_, sq=sq, m=m: e.scalar_tensor_tensor(out=yt_[0:m, :], in0=yt_[0:m, :], scalar=sq[0:m, 0:1], in1=lfb[0:m, :], op0=ALU.mult, op1=ALU.mult),
                   reads=[byt, bsq, blfb], writes=[byt])
                dma("pool", y[r0:r0 + m, :], yt_[0:m, :], byt, reads=[byt], writes=[by])

    for bi, (t0, n) in enumerate(blocks):
        do_block(bi, t0, n)
    C2.close()
    W2a.close()
    return fin()


def host_consts(L):
    i = np.arange(128)
    s = i[:, None]
    c = i[None, :]
    ident = (s == c)
    UI_f = (s <= c)
    UI_b = (s >= c)
    LS_f = (s > c)
    LS_b = (s < c)
    BD = ((s // 64) == (c // 64))
    consts = np.concatenate([m.astype(np.float32) for m in (ident, UI_f, UI_b, LS_f, LS_b, BD)], axis=1)
    sel65 = np.zeros((65, 64), np.float32)
    sel65[64, :] = 1.0
    half = 32
    inv = (1.0 / (10000.0 ** (np.arange(0, half, 2, dtype=np.float32) / half))).astype(np.float32)
    t = np.arange(L)
    row = (t // 64).astype(np.float32)
    col = (t % 64).astype(np.float32)
    ang_r = row[:, None] * inv[None, :]
    ang_c = col[:, None] * inv[None, :]
    cosT = np.zeros((64, L), np.float32)
    sinT = np.zeros((64, L), np.float32)
    for d in range(64):
        hf = d // 32
        j = d % 32
        ang = (ang_r if hf == 0 else ang_c)[:, j % 16]
        cosT[d] = np.cos(ang)
        sinT[d] = np.sin(ang) * (-1.0 if j < 16 else 1.0)
    cosT = np.ascontiguousarray(np.tile(cosT, (2, 1)))
    sinT = np.ascontiguousarray(np.tile(sinT, (2, 1)))
    return consts, sel65, cosT, sinT


def perm64():
    p = np.zeros(64, np.int64)
    for d in range(64):
        j = d % 32
        p[d] = d + 16 if j < 16 else d - 16
    return p


def host_inputs(xs, ln_mix, w_in, w_gk_fwd, b_gk_fwd, w_gk_bwd, b_gk_bwd, gla_out_norm, q_norm, k_norm,
                w_out, ln_ffn, w_up, conv_w, conv_b, w_down, ln_final, L, revs=None):
    f = np.float32
    if revs is None:
        revs = [False] * len(xs)
    consts, sel65, cosT, sinT = host_consts(L)
    w_in0 = np.ascontiguousarray(np.asarray(w_in[0], f))
    pm = perm64()
    cols = []
    for h in range(8):
        cols.append(1568 + h * 64 + pm)
    for h in range(2):
        cols.append(2080 + h * 64 + pm)
    cols = np.concatenate(cols)
    w_rot = np.ascontiguousarray(w_in0[:, cols])
    nrm = np.zeros((128, 8), f)
    qn = np.asarray(q_norm[0], f)
    kn = np.asarray(k_norm[0], f)
    nrm[:, 0] = np.tile(qn, 2)
    nrm[:, 1] = np.tile(qn[pm], 2)
    nrm[:, 2] = np.tile(kn, 2)
    nrm[:, 3] = np.tile(kn[pm], 2)
    nrm[:, 4] = np.asarray(gla_out_norm[0], f)
    cw = np.asarray(conv_w[0], f)
    cb = np.asarray(conv_b[0], f)
    variants = {}
    for rv in (False, True):
        wg = np.zeros((33, 512), f)
        fs, bs = (slice(0, 256), slice(256, 512)) if not rv else (slice(256, 512), slice(0, 256))
        wg[0:16, fs] = w_gk_fwd[0]
        wg[16:32, bs] = w_gk_bwd[0]
        wg[32, fs] = b_gk_fwd[0]
        wg[32, bs] = b_gk_bwd[0]
        convp = np.zeros((128, 44, 4), f)
        for k in range(3):
            kk = k if not rv else 2 - k
            convp[:, :, kk] = cw[k].reshape(44, 128).T
        convp[:, :, 3] = cb.reshape(44, 128).T
        ct = cosT if not rv else np.ascontiguousarray(cosT[:, ::-1])
        sn = sinT if not rv else np.ascontiguousarray(sinT[:, ::-1])
        variants[rv] = {"wg_aug": wg, "convp": convp, "cosT": ct, "sinT": sn}
    common = {
        "w_in": w_in0, "w_rot": w_rot,
        "ln_mix": np.ascontiguousarray(np.asarray(ln_mix[0], f).reshape(8, 128).T),
        "ln_ffn": np.ascontiguousarray(np.asarray(ln_ffn[0], f).reshape(8, 128).T),
        "lnf_bc": np.ascontiguousarray(np.broadcast_to(np.asarray(ln_final, f)[None, :], (128, D))),
        "nrm": nrm, "consts": consts, "sel65": sel65,
        "w_out": np.ascontiguousarray(np.asarray(w_out[0], f)),
        "w_up": np.ascontiguousarray(np.asarray(w_up[0], f)),
        "w_down": np.ascontiguousarray(np.asarray(w_down[0], f)),
    }
    maps = []
    for xx, rv in zip(xs, revs):
        m = dict(common)
        m.update(variants[bool(rv)])
        xx = np.asarray(xx, f)
        m["x"] = np.ascontiguousarray(xx[::-1] if rv else xx)
        maps.append(m)
    return maps


def run_sequences(seqs, W, L, n_cores=8):
    H = L // 2
    zero = np.zeros((L, D), np.float32)
    xs = [zero] * n_cores
    revs = [False] * n_cores
    for k, sq in enumerate(seqs):
        xs[2 * k] = sq
        xs[2 * k + 1] = sq
        revs[2 * k + 1] = True
    maps = host_inputs(xs, W["ln_mix"], W["w_in"], W["w_gk_fwd"], W["b_gk_fwd"], W["w_gk_bwd"], W["b_gk_bwd"],
                       W["gla_out_norm"], W["q_norm"], W["k_norm"], W["w_out"], W["ln_ffn"], W["w_up"],
                       W["conv_w"], W["conv_b"], W["w_down"], W["ln_final"], L, revs)
    nc, _ = build_program(L, H=H)
    res = run_bass_kernel_spmd(nc, maps, core_ids=list(range(n_cores)))
    outs = []
    for k in range(len(seqs)):
        a = np.asarray(res.results[2 * k]["y"], np.float32)
        b = np.asarray(res.results[2 * k + 1]["y"], np.float32)
        outs.append(np.concatenate([a, b[::-1]], axis=0))
    return outs


def kernel(x_prompt, x_sample, ln_mix, w_in, w_gk_fwd, b_gk_fwd, w_gk_bwd, b_gk_bwd, gla_out_norm,
           q_norm, k_norm, w_out, ln_ffn, w_up, conv_w, conv_b, w_down, ln_final):
    x_prompt = np.asarray(x_prompt)
    x_sample = np.asarray(x_sample)
    L = x_prompt.shape[1]
    seqs = [x_prompt[0], x_prompt[1], x_sample[0]]
    W = dict(ln_mix=np.asarray(ln_mix), w_in=np.asarray(w_in), w_gk_fwd=np.asarray(w_gk_fwd), b_gk_fwd=np.asarray(b_gk_fwd),
             w_gk_bwd=np.asarray(w_gk_bwd), b_gk_bwd=np.asarray(b_gk_bwd), gla_out_norm=np.asarray(gla_out_norm),
             q_norm=np.asarray(q_norm), k_norm=np.asarray(k_norm), w_out=np.asarray(w_out), ln_ffn=np.asarray(ln_ffn),
             w_up=np.asarray(w_up), conv_w=np.asarray(conv_w), conv_b=np.asarray(conv_b), w_down=np.asarray(w_down),
             ln_final=np.asarray(ln_final))
    outs = run_sequences(seqs, W, L, 8)
    y_prompt = np.stack([outs[0], outs[1]], axis=0)
    y_sample = outs[2][None]
    return (y_prompt, y_sample)
```
